# Optimizing a Trainium2 kernel written in Bass

```python
import math
import jax, jax.numpy as jnp
from jax import lax
import numpy as np


D_MODEL = 1024
BATCH = 16
SEQ = 256
DEPTH = 2
DEC_BATCH = 2
DEC_SEQ = 4096
PAST_LEN = 256

GRID_W = 64
Q_BLOCK = 128
CHUNK = 64
EPS = 1e-6
MLA_HEADS = 8
MLA_NOPE = 64
ROPE_DIM = 32
AXIS_DIM = ROPE_DIM // 2
MLA_V = 64
Q_LORA = 256
KV_LORA = 128
ROPE_THETA = 10000.0
GLA_HEADS = 4
GLA_DK = 64
GLA_DV = 128
GLA_RANK = 16
GLA_TAU = 16.0
GDN_HEADS = 8
GDN_DK = 64
GDN_DV = 64
CONV_K = 3
GDN_CONV_W = 2 * GDN_HEADS * GDN_DK + GDN_HEADS * GDN_DV
N_BRANCH = 3
BRANCH_W = 512
D_FF = ((8 * D_MODEL + 3 * 256 - 1) // (3 * 256)) * 256
MOD_W = 6 * D_MODEL
IN_SIZES = (
    Q_LORA, KV_LORA + ROPE_DIM,
    GLA_HEADS * GLA_DK, GLA_HEADS * GLA_DK, GLA_HEADS * GLA_DV,
    GLA_HEADS * GLA_DV, 2 * GLA_RANK,
    GDN_CONV_W, GDN_HEADS * GDN_DV, 2 * GDN_HEADS, 2 * GDN_HEADS,
    N_BRANCH * D_MODEL,
)
IN_W = sum(IN_SIZES)

kernel_name = "hybrid_mla_gla_gdn_dit_step"


def rmsnorm(x, g):
    xf = x.astype(jnp.float32)
    y = xf * lax.rsqrt(jnp.mean(xf * xf, axis=-1, keepdims=True) + EPS)
    return (y * g.astype(jnp.float32)).astype(x.dtype)


def l2norm(x):
    xf = x.astype(jnp.float32)
    return (xf * lax.rsqrt(jnp.sum(xf * xf, axis=-1, keepdims=True) + EPS)).astype(x.dtype)


def rope_half(x, ang):
    n = ang.shape[-1]
    x1, x2 = x[..., :n], x[..., n:]
    cos, sin = jnp.cos(ang).astype(x.dtype), jnp.sin(ang).astype(x.dtype)
    return jnp.concatenate([x1 * cos - x2 * sin, x1 * sin + x2 * cos], axis=-1)


def axial_rope(x, ang_r, ang_c):
    return jnp.concatenate([rope_half(x[..., :AXIS_DIM], ang_r), rope_half(x[..., AXIS_DIM:], ang_c)], axis=-1)


def axial_angles(n_tokens):
    rows = n_tokens // GRID_W
    row = jnp.repeat(jnp.arange(rows, dtype=jnp.float32), GRID_W)
    col = jnp.tile(jnp.arange(GRID_W, dtype=jnp.float32), rows)
    inv = ROPE_THETA ** (-jnp.arange(0, AXIS_DIM, 2, dtype=jnp.float32) / AXIS_DIM)
    return row[:, None] * inv, col[:, None] * inv


def centred_conv(x, w):
    t = x.shape[1]
    pad = CONV_K // 2
    xp = jnp.pad(x, ((0, 0), (pad, pad), (0, 0)))
    return sum(xp[:, j:j + t] * w[j] for j in range(CONV_K))


def block_attention(q_nope, q_rope, k_nope, k_rope, v):
    b, t, h, _ = q_nope.shape
    nb = t // Q_BLOCK
    scale = (MLA_NOPE + ROPE_DIM) ** -0.5

    def blocks(a):
        return jnp.moveaxis(a.reshape(b, nb, Q_BLOCK, h, a.shape[-1]), 1, 0)

    def one(qs):
        qn, qr = qs
        s = jnp.einsum('bqhd,bkhd->bhqk', qn, k_nope) + jnp.einsum('bqhr,bkr->bhqk', qr, k_rope)
        p = jax.nn.softmax(s.astype(jnp.float32) * scale, axis=-1).astype(v.dtype)
        return jnp.einsum('bhqk,bkhd->bqhd', p, v)

    o = lax.map(one, (blocks(q_nope), blocks(q_rope)))
    return jnp.moveaxis(o, 0, 1).reshape(b, t, h * MLA_V)


def mla_branch(q_lat, kv_lat, ctx_kv, ang, q_norm, w_uq, kv_norm, w_ukv):
    b, t, _ = q_lat.shape
    q = (rmsnorm(q_lat, q_norm) @ w_uq).reshape(b, t, MLA_HEADS, MLA_NOPE + ROPE_DIM)
    q_nope, q_rope = q[..., :MLA_NOPE], q[..., MLA_NOPE:]
    ckv = rmsnorm(kv_lat[..., :KV_LORA], kv_norm)
    k_rope = kv_lat[..., KV_LORA:]
    if ang is not None:
        ang_r, ang_c = ang
        q_rope = axial_rope(q_rope, ang_r[:, None], ang_c[:, None])
        k_rope = axial_rope(k_rope, ang_r, ang_c)
    own_kv = jnp.concatenate([ckv, k_rope], axis=-1)
    all_kv = own_kv if ctx_kv is None else jnp.concatenate([ctx_kv.astype(own_kv.dtype), own_kv], axis=1)
    s = all_kv.shape[1]
    kv = (all_kv[..., :KV_LORA] @ w_ukv).reshape(b, s, MLA_HEADS, MLA_NOPE + MLA_V)
    out = block_attention(q_nope, q_rope, kv[..., :MLA_NOPE], all_kv[..., KV_LORA:], kv[..., MLA_NOPE:])
    return out, own_kv


def to_chunks(x):
    b, t, h, d = x.shape
    return x.reshape(b, t // CHUNK, CHUNK, h, d).transpose(0, 1, 3, 2, 4)


def from_chunks(x):
    b, n, h, c, d = x.shape
    return x.transpose(0, 1, 3, 2, 4).reshape(b, n * c, h, d)


def gla_scan(q, k, v, log_a, s0):
    dt = q.dtype
    q, k, v = to_chunks(q), to_chunks(k), to_chunks(v)
    bcum = jnp.cumsum(to_chunks(log_a).astype(jnp.float32), axis=3)
    b_last = bcum[:, :, :, -1:, :]
    q_dec = q * jnp.exp(bcum).astype(dt)
    k_inv = k * jnp.exp(-bcum).astype(dt)
    k_end = k * jnp.exp(b_last - bcum).astype(dt)
    tri = jnp.tril(jnp.ones((CHUNK, CHUNK), dtype=bool))
    a_intra = jnp.where(tri, jnp.einsum('bnhck,bnhsk->bnhcs', q_dec, k_inv), 0)
    o_intra = jnp.einsum('bnhcs,bnhsv->bnhcv', a_intra, v)
    upd = jnp.einsum('bnhck,bnhcv->bnhkv', k_end, v)
    decay = jnp.exp(b_last[:, :, :, 0, :]).astype(dt)

    def step(s, inp):
        d_n, u_n = inp
        return (d_n[..., None] * s + u_n).astype(s.dtype), s

    s_fin, s_prev = lax.scan(step, s0.astype(dt), (jnp.moveaxis(decay, 1, 0), jnp.moveaxis(upd, 1, 0)))
    s_prev = jnp.moveaxis(s_prev, 0, 1)
    o = o_intra + jnp.einsum('bnhck,bnhkv->bnhcv', q_dec, s_prev)
    return from_chunks(o), s_fin


def gdn_scan(q, k, v, g, beta, s0):
    dt = q.dtype
    f32 = jnp.float32
    q, k, v = to_chunks(q), to_chunks(k), to_chunks(v)
    gam = jnp.cumsum(to_chunks(g[..., None])[..., 0], axis=-1)
    bet = to_chunks(beta[..., None])[..., 0]
    tri = jnp.tril(jnp.ones((CHUNK, CHUNK), dtype=bool))
    strict = jnp.tril(jnp.ones((CHUNK, CHUNK), dtype=bool), -1)
    decay = jnp.exp(jnp.where(tri, gam[..., :, None] - gam[..., None, :], -jnp.inf))
    kk = jnp.einsum('bnhck,bnhsk->bnhcs', k, k).astype(f32)
    m = jnp.where(strict, bet[..., :, None] * kk * decay, 0.0)
    rhs = jnp.concatenate([v.astype(f32) * bet[..., None],
                           k.astype(f32) * (bet * jnp.exp(gam))[..., None]], axis=-1)
    sol = lax.linalg.triangular_solve(m, rhs, left_side=True, lower=True, unit_diagonal=True)
    u = sol[..., :v.shape[-1]].astype(dt)
    w = sol[..., v.shape[-1]:].astype(dt)
    a_qk = (jnp.einsum('bnhck,bnhsk->bnhcs', q, k).astype(f32) * decay).astype(dt)
    q_dec = (q.astype(f32) * jnp.exp(gam)[..., None]).astype(dt)
    k_end = (k.astype(f32) * jnp.exp(gam[..., -1:] - gam)[..., None]).astype(dt)
    g_end = jnp.exp(gam[..., -1]).astype(dt)

    def step(s, inp):
        u_n, w_n, qd_n, ke_n, a_n, ge_n = inp
        v_new = u_n - jnp.einsum('bhck,bhkv->bhcv', w_n, s)
        o_n = jnp.einsum('bhck,bhkv->bhcv', qd_n, s) + jnp.einsum('bhcs,bhsv->bhcv', a_n, v_new)
        s = (ge_n[..., None, None] * s + jnp.einsum('bhck,bhcv->bhkv', ke_n, v_new)).astype(s.dtype)
        return s, o_n

    xs = tuple(jnp.moveaxis(a, 1, 0) for a in (u, w, q_dec, k_end, a_qk, g_end))
    s_fin, o = lax.scan(step, s0.astype(dt), xs)
    return from_chunks(jnp.moveaxis(o, 0, 1)), s_fin


def gla_branch(q, k, v, r, glr, s0, w_gate_up, b_gate, norm):
    b, t, _ = q.shape
    q = q.reshape(b, t, GLA_HEADS, GLA_DK) * GLA_DK ** -0.5
    k = k.reshape(b, t, GLA_HEADS, GLA_DK)
    v = v.reshape(b, t, GLA_HEADS, GLA_DV)
    z = jnp.einsum('btzr,zrk->btzk', glr.reshape(b, t, 2, GLA_RANK), w_gate_up) + b_gate
    log_a = (jax.nn.log_sigmoid(z.astype(jnp.float32)) / GLA_TAU).reshape(b, t, 2, GLA_HEADS, GLA_DK)
    o_f, s_f = gla_scan(q, k, v, log_a[:, :, 0], s0[:, 0])
    o_b, s_b = gla_scan(jnp.flip(q, 1), jnp.flip(k, 1), jnp.flip(v, 1), jnp.flip(log_a[:, :, 1], 1), s0[:, 1])
    o = rmsnorm(o_f + jnp.flip(o_b, 1), norm) * jax.nn.silu(r.reshape(b, t, GLA_HEADS, GLA_DV))
    return o.reshape(b, t, GLA_HEADS * GLA_DV), jnp.stack([s_f, s_b], axis=1)


def gdn_branch(qkv, z, a, bt, s0, conv_w, a_log, dt_bias, norm):
    b, t, _ = qkv.shape
    hk = GDN_HEADS * GDN_DK
    qkv = jax.nn.silu(centred_conv(qkv, conv_w))
    q = l2norm(qkv[..., :hk].reshape(b, t, GDN_HEADS, GDN_DK)) * GDN_DK ** -0.5
    k = l2norm(qkv[..., hk:2 * hk].reshape(b, t, GDN_HEADS, GDN_DK))
    v = qkv[..., 2 * hk:].reshape(b, t, GDN_HEADS, GDN_DV)
    a = a.reshape(b, t, 2, GDN_HEADS).astype(jnp.float32)
    bt = bt.reshape(b, t, 2, GDN_HEADS).astype(jnp.float32)
    g = -jnp.exp(a_log.astype(jnp.float32)) * jax.nn.softplus(a + dt_bias.astype(jnp.float32))
    beta = jax.nn.sigmoid(bt)
    o_f, s_f = gdn_scan(q, k, v, g[:, :, 0], beta[:, :, 0], s0[:, 0])
    o_b, s_b = gdn_scan(jnp.flip(q, 1), jnp.flip(k, 1), jnp.flip(v, 1),
                        jnp.flip(g[:, :, 1], 1), jnp.flip(beta[:, :, 1], 1), s0[:, 1])
    o = rmsnorm(o_f + jnp.flip(o_b, 1), norm) * jax.nn.silu(z.reshape(b, t, GDN_HEADS, GDN_DV))
    return o.reshape(b, t, GDN_HEADS * GDN_DV), jnp.stack([s_f, s_b], axis=1)


def trunk_layer(h, cond, ang, ctx_kv, s0_gla, s0_gdn, p, l):
    b, t, _ = h.shape
    mod = jax.nn.silu(cond) @ p['w_mod'][l] + p['b_mod'][l]
    sh_a, sc_a, g_a, sh_f, sc_f, g_f = jnp.split(mod[:, None, :], 6, axis=-1)
    xn = rmsnorm(h, p['norm_mix'][l]) * (1 + sc_a) + sh_a
    proj = xn @ p['w_in'][l]
    split_points = np.cumsum(IN_SIZES)[:-1].tolist()
    mq, mkv, gq, gk, gv, gr, glr, dqkv, dz, da, db, gates = jnp.split(proj, split_points, axis=-1)
    y_mla, own_kv = mla_branch(mq, mkv, ctx_kv, ang, p['mla_q_norm'][l], p['mla_w_uq'][l],
                               p['mla_kv_norm'][l], p['mla_w_ukv'][l])
    y_gla, st_gla = gla_branch(gq, gk, gv, gr, glr, s0_gla, p['gla_w_gate'][l], p['gla_b_gate'][l],
                               p['gla_norm'][l])
    y_gdn, st_gdn = gdn_branch(dqkv, dz, da, db, s0_gdn, p['gdn_conv'][l], p['gdn_a_log'][l],
                               p['gdn_dt_bias'][l], p['gdn_norm'][l])
    branches = jnp.stack([y_mla, y_gla, y_gdn], axis=2)
    proj_b = jnp.einsum('btnk,nkd->btnd', branches, p['w_branch'][l])
    gates = jax.nn.sigmoid(gates.reshape(b, t, N_BRANCH, D_MODEL))
    y = jnp.sum(gates * proj_b, axis=2) @ p['w_out'][l]
    h = h + g_a * y
    xf = rmsnorm(h, p['norm_ffn'][l]) * (1 + sc_f) + sh_f
    gu = xf @ p['ffn_w_in'][l]
    h = h + g_f * ((jax.nn.silu(gu[..., :D_FF]) * gu[..., D_FF:]) @ p['ffn_w_out'][l])
    return h, own_kv, st_gla, st_gdn


def setup_inputs(seed: int = 0) -> dict:
    key = jax.random.key(seed)
    ks = jax.random.split(key, 32)
    f32 = jnp.float32

    def nrm(k, shape, scale):
        return jax.random.normal(k, shape, f32) * scale

    dt = jnp.exp(jax.random.uniform(ks[20], (DEPTH, 2, GDN_HEADS), f32, math.log(1e-3), math.log(1e-1)))
    return {
        'x_prompt': nrm(ks[0], (BATCH, SEQ, D_MODEL), 1.0),
        'x_sample': nrm(ks[1], (DEC_BATCH, DEC_SEQ, D_MODEL), 1.0),
        'cache_mla': nrm(ks[2], (DEC_BATCH, DEPTH, PAST_LEN, KV_LORA + ROPE_DIM), 1.0),
        'state_gla': nrm(ks[3], (DEC_BATCH, DEPTH, 2, GLA_HEADS, GLA_DK, GLA_DV), 0.1),
        'state_gdn': nrm(ks[4], (DEC_BATCH, DEPTH, 2, GDN_HEADS, GDN_DK, GDN_DV), 0.1),
        'c': nrm(ks[5], (DEC_BATCH, D_MODEL), 1.0),
        'c_ctx': nrm(ks[6], (D_MODEL,), 1.0),
        'w_mod': nrm(ks[7], (DEPTH, D_MODEL, MOD_W), 0.5 * D_MODEL ** -0.5),
        'b_mod': nrm(ks[8], (DEPTH, MOD_W), 0.01),
        'norm_mix': 1.0 + nrm(ks[9], (DEPTH, D_MODEL), 0.02),
        'w_in': nrm(ks[10], (DEPTH, D_MODEL, IN_W), D_MODEL ** -0.5),
        'mla_q_norm': 1.0 + nrm(ks[11], (DEPTH, Q_LORA), 0.02),
        'mla_w_uq': nrm(ks[12], (DEPTH, Q_LORA, MLA_HEADS * (MLA_NOPE + ROPE_DIM)), Q_LORA ** -0.5),
        'mla_kv_norm': 1.0 + nrm(ks[13], (DEPTH, KV_LORA), 0.02),
        'mla_w_ukv': nrm(ks[14], (DEPTH, KV_LORA, MLA_HEADS * (MLA_NOPE + MLA_V)), KV_LORA ** -0.5),
        'gla_w_gate': nrm(ks[15], (DEPTH, 2, GLA_RANK, GLA_HEADS * GLA_DK), GLA_RANK ** -0.5),
        'gla_b_gate': nrm(ks[16], (DEPTH, 2, GLA_HEADS * GLA_DK), 0.01),
        'gla_norm': 1.0 + nrm(ks[17], (DEPTH, GLA_DV), 0.02),
        'gdn_conv': nrm(ks[18], (DEPTH, CONV_K, GDN_CONV_W), CONV_K ** -0.5),
        'gdn_a_log': jnp.log(jax.random.uniform(ks[19], (DEPTH, 2, GDN_HEADS), f32, 1.0, 16.0)),
        'gdn_dt_bias': dt + jnp.log(-jnp.expm1(-dt)),
        'gdn_norm': 1.0 + nrm(ks[21], (DEPTH, GDN_DV), 0.02),
        'w_branch': nrm(ks[22], (DEPTH, N_BRANCH, BRANCH_W, D_MODEL), BRANCH_W ** -0.5),
        'w_out': nrm(ks[23], (DEPTH, D_MODEL, D_MODEL), D_MODEL ** -0.5),
        'norm_ffn': 1.0 + nrm(ks[24], (DEPTH, D_MODEL), 0.02),
        'ffn_w_in': nrm(ks[25], (DEPTH, D_MODEL, 2 * D_FF), D_MODEL ** -0.5),
        'ffn_w_out': nrm(ks[26], (DEPTH, D_FF, D_MODEL), D_FF ** -0.5),
        'final_norm': 1.0 + nrm(ks[27], (D_MODEL,), 0.02),
    }


def reference(x_prompt, x_sample, cache_mla, state_gla, state_gdn, c, c_ctx,
              w_mod, b_mod, norm_mix, w_in, mla_q_norm, mla_w_uq, mla_kv_norm, mla_w_ukv,
              gla_w_gate, gla_b_gate, gla_norm, gdn_conv, gdn_a_log, gdn_dt_bias, gdn_norm,
              w_branch, w_out, norm_ffn, ffn_w_in, ffn_w_out, final_norm):
    p = dict(w_mod=w_mod, b_mod=b_mod, norm_mix=norm_mix, w_in=w_in, mla_q_norm=mla_q_norm,
             mla_w_uq=mla_w_uq, mla_kv_norm=mla_kv_norm, mla_w_ukv=mla_w_ukv, gla_w_gate=gla_w_gate,
             gla_b_gate=gla_b_gate, gla_norm=gla_norm, gdn_conv=gdn_conv, gdn_a_log=gdn_a_log,
             gdn_dt_bias=gdn_dt_bias, gdn_norm=gdn_norm, w_branch=w_branch, w_out=w_out,
             norm_ffn=norm_ffn, ffn_w_in=ffn_w_in, ffn_w_out=ffn_w_out)

    b = x_prompt.shape[0]
    h = x_prompt
    cond_ctx = c_ctx[None, :]
    zeros_gla = jnp.zeros((b, 2, GLA_HEADS, GLA_DK, GLA_DV), x_prompt.dtype)
    zeros_gdn = jnp.zeros((b, 2, GDN_HEADS, GDN_DK, GDN_DV), x_prompt.dtype)
    kv_list, gla_list, gdn_list = [], [], []
    for l in range(DEPTH):
        h, kv_l, sg_l, sd_l = trunk_layer(h, cond_ctx, None, None, zeros_gla, zeros_gdn, p, l)
        kv_list.append(kv_l)
        gla_list.append(sg_l)
        gdn_list.append(sd_l)
    y_prompt = rmsnorm(h, final_norm)
    new_cache_mla = jnp.stack(kv_list, axis=1)
    new_state_gla = jnp.stack(gla_list, axis=1)
    new_state_gdn = jnp.stack(gdn_list, axis=1)

    ang = axial_angles(x_sample.shape[1])
    h = x_sample
    for l in range(DEPTH):
        h, _, _, _ = trunk_layer(h, c, ang, cache_mla[:, l], state_gla[:, l], state_gdn[:, l], p, l)
    y_sample = rmsnorm(h, final_norm)
    return (y_prompt, y_sample, new_cache_mla, new_state_gla, new_state_gdn)
```

```python
import os
import numpy as np
import concourse.bass as bass
import concourse.mybir as mybir
from concourse.bass_utils import run_bass_kernel_spmd
from contextlib import ExitStack

F32 = mybir.dt.float32
BF16 = mybir.dt.bfloat16
AF = mybir.ActivationFunctionType
ALU = mybir.AluOpType
AX = mybir.AxisListType

D = 1024
DEPTH = 2
EPS = 1e-6
NDMA_SEMS = 40
COMPUTE = ("pe", "act", "dve", "pool")

O_MQ, O_MKV, O_GQ, O_GK, O_GV, O_GR, O_GLR, O_DQKV, O_DZ, O_DA, O_DB, O_GATES = (
    0, 256, 416, 672, 928, 1440, 1952, 1984, 3520, 4032, 4048, 4064)
C_MQ = 0
C_GQ = 256
C_GKF = 512
C_GR = 768
C_GLR = 1280
C_DQKV = 1344
C_DZ = 2880
C_MKV = 3392
C_DAB = 3552
C_GKT = 3584
C_GV = 3840
C_GATES = 4352
NWIN = 7424


class _Op:
    __slots__ = ("eng", "fn", "waits", "dma", "idx", "milestone", "slot", "target", "guard", "cc")


class KB:
    def __init__(self):
        self.nc = bass.Bass("TRN2", target_bir_lowering=False)
        self.es = ExitStack()
        self.ops = []
        self.cnt = {e: 0 for e in ("pe", "act", "dve", "pool", "sp")}
        self.res = {}
        self.known = {e: {} for e in self.cnt}
        self.dma_n = 0
        self.slot_last = {}
        self.slot_cnt = {}

    def sb(self, name, shape, dt, stack=None):
        self.uid = getattr(self, "uid", 0) + 1
        return (stack or self.es).enter_context(self.nc.sbuf_tensor("%s_%d" % (name, self.uid), list(shape), dt))

    def ps(self, name, shape, dt, stack=None):
        return (stack or self.es).enter_context(self.nc.psum_tensor(name, list(shape), dt))

    def op(self, eng, fn, reads=(), writes=(), pwrites=(), dma=False, cc=False, force=()):
        o = _Op()
        o.cc = cc
        if cc:
            dma = True
        o.eng = eng; o.fn = fn; o.dma = dma; o.milestone = False
        o.idx = self.cnt[eng]; self.cnt[eng] += 1
        deps = []
        for k in reads:
            st = self.res.get(k)
            if st:
                deps += st["w"]
        for k in writes:
            st = self.res.get(k)
            if st:
                deps += st["r"]; deps += st["w"]
        for k in pwrites:
            st = self.res.get(k)
            if st:
                deps += st["r"]; deps += [d for d in st["w"] if not d[1]]
        kn = self.known[eng]
        need = {}
        for d in deps:
            dop = d[0]
            if dop.dma:
                key = ("dma", dop.slot)
                if key not in need or need[key].target < dop.target:
                    need[key] = dop
            else:
                if dop.eng == eng and eng in ("pe", "sp"):
                    continue
                key = dop.eng
                if key not in need or need[key].idx < dop.idx:
                    need[key] = dop
        waits = []
        for fo in force:
            if kn.get(fo.eng, -1) < fo.idx:
                kn[fo.eng] = fo.idx
                fo.milestone = True
                waits.append(fo)
        for key, dop in need.items():
            if dop.dma:
                if kn.get(key, -1) >= dop.target:
                    continue
                kn[key] = dop.target
            else:
                if kn.get(key, -1) >= dop.idx:
                    continue
                kn[key] = dop.idx
                dop.milestone = True
            waits.append(dop)
        o.guard = None
        if cc:
            self.cc_n = getattr(self, "cc_n", 0) + 1
            o.slot = 2000 + self.cc_n
            o.target = 1
            self.slot_last[o.slot] = o
        elif dma and eng == "pool" and os.environ.get("KSIM") == "1":
            self.pool_n = getattr(self, "pool_n", 0) + 1
            o.slot = 1000 + self.pool_n
            o.target = 16
            self.slot_last[o.slot] = o
        elif dma:
            o.slot = self.dma_n % NDMA_SEMS
            self.dma_n += 1
            self.slot_cnt[o.slot] = self.slot_cnt.get(o.slot, 0) + 16
            o.target = self.slot_cnt[o.slot]
            prev = self.slot_last.get(o.slot)
            if prev is not None:
                key = ("dma", o.slot)
                if kn.get(key, -1) < prev.target:
                    kn[key] = prev.target
                    o.guard = prev
            self.slot_last[o.slot] = o
        o.waits = waits
        self.ops.append(o)
        for k in reads:
            st = self.res.setdefault(k, {"w": [], "r": []})
            if dma:
                st["r"].append((o, False))
            else:
                st["r"] = [d for d in st["r"] if d[0].dma or d[0].eng != eng] + [(o, False)]
        for k in writes:
            self.res[k] = {"w": [(o, False)], "r": []}
        for k in pwrites:
            st = self.res.setdefault(k, {"w": [], "r": []})
            if dma:
                st["w"].append((o, True))
            else:
                st["w"] = [d for d in st["w"] if d[0].dma or d[0].eng != eng or not d[1]] + [(o, True)]
        return o

    def barrier(self):
        last = {}
        for o in self.ops:
            if o.fn is not None and not o.dma:
                last[o.eng] = o
        dmas = list(self.slot_last.values())
        for e in ("pe", "act", "dve", "pool", "sp"):
            b = _Op(); b.eng = e; b.fn = None; b.dma = False; b.milestone = False; b.idx = None; b.guard = None; b.cc = False
            waits = []
            kn = self.known[e]
            for x, lo in last.items():
                if x == e:
                    continue
                if kn.get(x, -1) >= lo.idx:
                    continue
                kn[x] = lo.idx; lo.milestone = True; waits.append(lo)
            for d in dmas:
                key = ("dma", d.slot)
                if kn.get(key, -1) >= d.target:
                    continue
                kn[key] = d.target; waits.append(d)
            b.waits = waits
            self.ops.append(b)
        self.res = {}

    def emit(self):
        nc = self.nc
        engs = {"pe": nc.tensor, "act": nc.scalar, "dve": nc.vector, "pool": nc.gpsimd, "sp": nc.sync}
        sems = {e: self.es.enter_context(nc.semaphore("sem_" + e)) for e in COMPUTE}
        dsem = {i: self.es.enter_context(nc.semaphore("dsem%d" % i)) for i in range(NDMA_SEMS)}
        for i in range(getattr(self, "pool_n", 0)):
            dsem[1001 + i] = self.es.enter_context(nc.semaphore("psem%d" % i))
        for i in range(getattr(self, "cc_n", 0)):
            dsem[2001 + i] = self.es.enter_context(nc.semaphore("ccsem%d" % i))
        mc = {e: 0 for e in COMPUTE}
        for o in self.ops:
            if o.fn is not None and not o.dma and o.milestone:
                mc[o.eng] += 1
                o.target = mc[o.eng]
        nw = 0
        for o in self.ops:
            e = engs[o.eng]
            for d in o.waits:
                if d.dma:
                    e.wait_ge(dsem[d.slot], d.target)
                else:
                    e.wait_ge(sems[d.eng], d.target)
                nw += 1
            if o.fn is None:
                continue
            if o.dma:
                if o.guard is not None:
                    e.wait_ge(dsem[o.slot], o.guard.target); nw += 1
                ins = o.fn(e)
                if o.cc:
                    ins.then_inc(dsem[o.slot])
                elif ins is not None:
                    ins.then_inc(dsem[o.slot], 16)
            else:
                ins = o.fn(e)
                if o.milestone:
                    ins.then_inc(sems[o.eng], 1)
        self.stats = dict(n_ops=len(self.ops), n_waits=nw, milestones=mc)
        return nc

    @staticmethod
    def _n(ap):
        return ap.name

    def mm(self, out, lhsT, rhs, start=True, stop=True, tr=False, part=False):
        rk = [lhsT.name, rhs.name]
        kw = dict(writes=[out.name]) if (start and not part) else dict(pwrites=[out.name])
        r0 = lhsT.base_partition(); r1 = r0 + lhsT.partition_size()
        c0 = out.base_partition(); c1 = c0 + out.partition_size()
        force = []
        prev = getattr(self, "_pmm", None)
        if prev is not None:
            (pr0, pr1, pc0, pc1, pop) = prev
            if r1 <= pr0 or pr1 <= r0:
                force = [pop]
        if tr:
            o = self.op("pe", lambda e: e.transpose(out=out, in_=lhsT, identity=rhs), reads=rk, force=force, **kw)
        else:
            o = self.op("pe", lambda e: e.matmul(out, lhsT=lhsT, rhs=rhs, start=start, stop=stop), reads=rk, force=force, **kw)
        self._pmm = (r0, r1, c0, c1, o)
        return o

    def act(self, out, in_, func, scale=None, bias=None, accum=None, pw=False, eng="act"):
        rk = [in_.name]
        kw = {}
        if scale is not None:
            kw["scale"] = scale
            if not isinstance(scale, (int, float)):
                rk.append(scale.name)
        if bias is not None:
            kw["bias"] = bias
            if not isinstance(bias, (int, float)):
                rk.append(bias.name)
        wk = [out.name]
        if accum is not None:
            kw["accum_out"] = accum
            wk.append(accum.name)
        wkw = dict(pwrites=wk) if pw else dict(writes=wk)
        return self.op(eng, lambda e: e.activation(out=out, in_=in_, func=func, **kw), reads=rk, **wkw)

    def tt(self, out, in0, in1, op, pw=False, eng="dve"):
        wkw = dict(pwrites=[out.name]) if pw else dict(writes=[out.name])
        return self.op(eng, lambda e: e.tensor_tensor(out=out, in0=in0, in1=in1, op=op), reads=[in0.name, in1.name], **wkw)

    def ts(self, out, in0, s1, s2, op0, op1=None, pw=False, eng="dve"):
        rk = [in0.name]
        for s_ in (s1, s2):
            if s_ is not None and not isinstance(s_, (int, float)):
                rk.append(s_.name)
        wkw = dict(pwrites=[out.name]) if pw else dict(writes=[out.name])
        if op1 is None:
            return self.op(eng, lambda e: e.tensor_scalar(out=out, in0=in0, scalar1=s1, scalar2=None, op0=op0), reads=rk, **wkw)
        return self.op(eng, lambda e: e.tensor_scalar(out=out, in0=in0, scalar1=s1, scalar2=s2, op0=op0, op1=op1), reads=rk, **wkw)

    def stt(self, out, in0, scalar, in1, op0, op1, pw=False):
        rk = [in0.name, in1.name]
        if not isinstance(scalar, (int, float)):
            rk.append(scalar.name)
        wkw = dict(pwrites=[out.name]) if pw else dict(writes=[out.name])
        return self.op("dve", lambda e: e.scalar_tensor_tensor(out=out, in0=in0, scalar=scalar, in1=in1, op0=op0, op1=op1), reads=rk, **wkw)

    def cp(self, out, in_, pw=False, eng="dve"):
        wkw = dict(pwrites=[out.name]) if pw else dict(writes=[out.name])
        if eng == "act":
            return self.op("act", lambda e: e.copy(out=out, in_=in_), reads=[in_.name], **wkw)
        return self.op(eng, lambda e: e.tensor_copy(out=out, in_=in_), reads=[in_.name], **wkw)

    def recip(self, out, in_, pw=False):
        wkw = dict(pwrites=[out.name]) if pw else dict(writes=[out.name])
        return self.op("dve", lambda e: e.reciprocal(out=out, in_=in_), reads=[in_.name], **wkw)

    def memset(self, ap, val, eng="dve", pw=False):
        wkw = dict(pwrites=[ap.name]) if pw else dict(writes=[ap.name])
        return self.op(eng, lambda e: e.memset(ap, val), **wkw)

    def dma(self, out, in_, q="sp", pw=False):
        wkw = dict(pwrites=[out.name]) if pw else dict(writes=[out.name])
        return self.op(q, lambda e: e.dma_start(out=out, in_=in_), reads=[in_.name], dma=True, **wkw)


class Unit:
    def __init__(self, name, tile0, ntile, segs, cond, sample):
        self.name = name; self.tile0 = tile0; self.ntile = ntile; self.T = ntile * 128
        self.segs = segs; self.cond = cond; self.sample = sample
        self.nblk = self.T // 512
        if sample:
            self.qblocks = [(b * 512, 512, list(range(34))) for b in range(self.nblk)]
            self.nkt = 34
        else:
            self.qblocks = [(s0, n, list(range(s0 // 128, (s0 + n) // 128))) for (s0, n) in segs]
            self.nkt = ntile


UNIT_P = Unit("P", 0, 4, [(0, 256), (256, 256)], 0, False)
UNIT_S = Unit("S", 4, 8, [(0, 1024)], 1, True)
ATT_SCALE = float(96 ** -0.5)


class Prog:
    def __init__(self, stage=9, units="PS", dbg=()):
        self.k = KB()
        self.stage = stage
        self.units = units
        self.dbg = dbg
        self.din = {}
        self.dout = {}
        self.wi = 0
        self.build()

    def inp(self, name, shape, dt=F32):
        t = self.k.nc.dram_tensor(name, list(shape), dt, kind="ExternalInput")
        self.din[name] = t
        return t

    def outp(self, name, shape, dt=F32):
        t = self.k.nc.dram_tensor(name, list(shape), dt, kind="ExternalOutput")
        self.dout[name] = t
        return t

    def wload(self, src, nk, ncol, parts=128):
        slot = self.wslots[self.wi % len(self.wslots)]
        self.wi += 1
        view = slot[0:parts, 0:nk * ncol].rearrange("p (k c) -> p k c", k=nk)
        self.k.dma(view, src, q="pool")
        return view

    def evac(self, out, in_, pw=True):
        self.ev = getattr(self, "ev", 0) + 1
        self.k.cp(out, in_, pw=pw, eng=("act" if self.ev % 2 else "dve"))

    def rstd_from_ss(self, st, n):
        k = self.k
        k.act(st[:, 1:2], st[:, 0:1], AF.Ln, scale=1.0 / n, bias=self.epsc[:, 0:1])
        k.act(st[:, 2:3], st[:, 1:2], AF.Exp, scale=-0.5)

    def build(self):
        k = self.k
        nc = k.nc
        inp, outp = self.inp, self.outp
        d_x = inp("xin", [1536, D])
        d_cache = inp("cache", [DEPTH, 256, 160])
        d_sgla = inp("sgla", [DEPTH, 2, 4, 64, 128])
        d_sgdn = inp("sgdn", [DEPTH, 2, 8, 64, 64])
        d_cond = inp("condT", [128, 16])
        d_cst = inp("cst", [128, 8, 128])
        d_rope = inp("ropeT", [32, 2, 1536])
        d_ropetok = inp("ropetok", [128, 2, 12, 32])
        d_rank = inp("rankm", [128, 16])
        d_wmod = inp("wmod", [DEPTH, 128, 8, 6144])
        d_bmod = inp("bmodc", [128, DEPTH, 48])
        d_win = inp("win", [DEPTH, 128, 8, NWIN])
        d_ncol = inp("ncol", [128, DEPTH, 2, 8])
        d_fnorm = inp("fnorm", [1, D])
        d_wuq = inp("wuq", [DEPTH, 128, 2, 1024])
        d_qnorm = inp("qnorm", [128, DEPTH, 2])
        d_kvnorm = inp("kvnorm", [DEPTH, 1, 128])
        d_wukT = inp("wukT", [DEPTH, 64, 8, 128])
        d_wuv = inp("wuv", [DEPTH, 128, 512])
        d_wgate = inp("wgate", [DEPTH, 17, 512])
        d_gcol = inp("gcol", [128, DEPTH, 2])
        d_conv = inp("convc", [128, DEPTH, 12, 3])
        d_adt = inp("adt", [DEPTH, 1, 32])
        d_wbr = inp("wbr", [DEPTH, 128, 8, 3, 4, 128])
        d_wout = inp("wout", [DEPTH, 128, 8, D])
        d_fwin = inp("fwin", [DEPTH, 128, 8, 5632])
        d_fwout = inp("fwout", [DEPTH, 128, 8, 22, 128])
        d_y = outp("y", [1536, D])
        d_ncache = outp("ncache", [2, DEPTH, 256, 160])
        d_nsgla = outp("nsgla", [2, DEPTH, 2, 4, 64, 128])
        d_nsgdn = outp("nsgdn", [2, DEPTH, 2, 8, 64, 64])
        self.d = dict(x=d_x, cache=d_cache, sgla=d_sgla, sgdn=d_sgdn, win=d_win, wuq=d_wuq, kvnorm=d_kvnorm,
                      wukT=d_wukT, wuv=d_wuv, wgate=d_wgate, adt=d_adt, wbr=d_wbr, wout=d_wout, fwin=d_fwin,
                      fwout=d_fwout, y=d_y, ncache=d_ncache, nsgla=d_nsgla, nsgdn=d_nsgdn, fnorm=d_fnorm)

        sb = k.sb
        self.wslots = [sb("ws%d" % i, [128, 4096], BF16) for i in range(3)]
        cst = sb("cst_sb", [128, 8, 128], F32)
        k.dma(cst[:], d_cst[:, :, :])
        self.ident = cst[:, 0, :]
        self.triF = cst[:, 1, :]; self.triB = cst[:, 2, :]
        self.negFs = cst[:, 3, :]; self.negFi = cst[:, 4, :]; self.negBs = cst[:, 5, :]; self.negBi = cst[:, 6, :]
        self.bones = cst[:, 7, :]
        self.cst = cst
        self.identb = sb("identb", [128, 128], BF16)
        k.cp(self.identb[:], self.ident)
        self.onesf = sb("onesf", [128, 128], F32); k.memset(self.onesf[:], 1.0)
        self.onesb = sb("onesb", [128, 128], BF16); k.memset(self.onesb[:], 1.0)
        self.epsc = sb("epsc", [128, 1], F32); k.memset(self.epsc[:], EPS)
        self.d_rope = d_rope; self.d_ropetok = d_ropetok
        self.rank = sb("rank_sb", [128, 16], F32); k.dma(self.rank[:], d_rank[:, :])
        self.ncol = sb("ncol_sb", [128, DEPTH, 2, 8], F32); k.dma(self.ncol[:], d_ncol[:, :, :, :])
        self.qnorm = sb("qnorm_sb", [128, DEPTH, 2], F32); k.dma(self.qnorm[:], d_qnorm[:, :, :])
        self.gcol = sb("gcol_sb", [128, DEPTH, 2], F32); k.dma(self.gcol[:], d_gcol[:, :, :])
        self.convc = sb("convc_sb", [128, DEPTH, 12, 3], F32); k.dma(self.convc[:], d_conv[:, :, :, :])
        self.frow = sb("frow", [128, D], F32); k.dma(self.frow[:], d_fnorm[0:1, :].partition_broadcast(128))
        self.modc = sb("modc", [128, DEPTH, 48, 2], F32)
        self.d_hsp = nc.dram_tensor("hspill", [1024, D], F32)
        self.Ga = sb("Ga", [128, 8], F32); self.Gf = sb("Gf", [128, 8], F32)
        self.grow = [sb("grow%d" % i, [128, D], BF16) for i in range(2)]
        self.st = [sb("st%d" % i, [128, 4], F32) for i in range(4)]
        self.sti = 0
        self.junk = sb("junk", [128, D], BF16)
        self.pb = [k.ps("pb%d" % i, [128, 512], F32) for i in range(7)]
        self.pbb = k.ps("pbb", [128, 1024], BF16)

        condT = sb("condT_sb", [128, 16], F32)
        k.dma(condT[:], d_cond[:, :])
        scT = sb("scT", [128, 8, 2], BF16)
        k.act(scT[:].rearrange("p k c -> p (k c)"), condT[:], AF.Silu)
        bmc = sb("bmc", [128, DEPTH, 48], F32)
        k.dma(bmc[:], d_bmod[:, :, :])
        for l in range(DEPTH):
            pbk = self.pb[l]
            for g in range(12):
                w = self.wload(d_wmod[l, :, :, g * 512:(g + 1) * 512], 8, 512)
                for c4 in range(4):
                    j = g * 4 + c4
                    for kk in range(8):
                        k.mm(pbk[:, j * 2:(j + 1) * 2], w[:, kk, c4 * 128:(c4 + 1) * 128], scT[:, kk, :],
                             start=(kk == 0), stop=(kk == 7), part=True)
            k.tt(self.modc[:, l, :, :], pbk[:, 0:96].rearrange("p (j c) -> p j c", c=2),
                 bmc[:, l, :].unsqueeze(2).to_broadcast([128, 48, 2]), ALU.add)
        k.barrier()

        try:
            self.ckpt(0)
            for u in (UNIT_P, UNIT_S):
                if u.name not in self.units:
                    continue
                self.run_unit(u)
        except StopIteration:
            pass
        k.barrier()
        k.emit()

    def ckpt(self, n):
        import os
        self.k.barrier()
        if float(os.environ.get("KSTOP", "999")) <= n:
            raise StopIteration

    def dump(self, name, ap, shape, dt):
        if os.environ.get("KDBG") != "1" or name in self.dout:
            return
        t = self.outp(name, shape, dt)
        self.k.dma(t.ap() if hasattr(t, "ap") else t, ap)

    def nst(self):
        self.sti += 1
        return self.st[self.sti % len(self.st)]

    def run_unit(self, u):
        k = self.k
        d = self.d
        g0 = u.tile0 * 128
        with ExitStack() as ust:
            self.xT = k.sb("xT_" + u.name, [128, 8, u.T], BF16, ust)
            self.ymla = k.sb("ymla_" + u.name, [128, 4, u.T], BF16, ust)
            self.ygla = k.sb("ygla_" + u.name, [128, 4, u.T], BF16, ust)
            self.ygdn = k.sb("ygdn_" + u.name, [128, 4, u.T], BF16, ust)
            self.rope = k.sb("rope_" + u.name, [32, 2, u.T], F32, ust)
            k.dma(self.rope[:], self.d_rope[:, :, g0:g0 + u.T])
            self.ropetok = k.sb("ropetok_" + u.name, [128, 2, u.ntile, 32], F32, ust)
            k.dma(self.ropetok[:], self.d_ropetok[:, :, u.tile0:u.tile0 + u.ntile, :])
            hst = ExitStack()
            self.h = [k.sb("h%d_%s" % (t, u.name), [128, D], F32, hst) for t in range(u.ntile)]
            for ti in range(u.ntile):
                k.dma(self.h[ti][:], d["x"][g0 + ti * 128:g0 + (ti + 1) * 128, :])
            for l in range(DEPTH):
                self.layer_mod(u, l)
                self.ckpt(1)
                self.norm_T(u, l, self.Ga, 0)
                for ti in range(u.ntile):
                    k.dma(self.d_hsp[ti * 128:(ti + 1) * 128, :], self.h[ti][:])
                self.ckpt(2)
                hst.close()
                with ExitStack() as ph:
                    self.mla(u, l, ph)
                    self.ckpt(3)
                if self.stage >= 2:
                    with ExitStack() as ph:
                        self.gla(u, l, ph)
                        self.ckpt(4)
                if self.stage >= 3:
                    with ExitStack() as ph:
                        self.gdn(u, l, ph)
                        self.ckpt(5)
                hst = ExitStack()
                self.h = [k.sb("h%d_%s" % (t, u.name), [128, D], F32, hst) for t in range(u.ntile)]
                for ti in range(u.ntile):
                    k.dma(self.h[ti][:], self.d_hsp[ti * 128:(ti + 1) * 128, :])
                with ExitStack() as ph:
                    self.merge(u, l, ph)
                    self.ckpt(6)
                self.norm_T(u, l, self.Gf, 24)
                with ExitStack() as ph:
                    self.ffn(u, l, ph)
                    self.ckpt(7)
            with ExitStack() as ph:
                yo = [k.sb("yo%d_%s" % (i, u.name), [128, D], F32, ph) for i in range(2)]
                for ti in range(u.ntile):
                    st = self.nst()
                    k.act(self.junk[:], self.h[ti][:], AF.Square, accum=st[:, 0:1])
                    self.rstd_from_ss(st, D)
                    k.stt(yo[ti % 2][:], self.h[ti][:], st[:, 2:3], self.frow[:], ALU.mult, ALU.mult)
                    k.dma(d["y"][g0 + ti * 128:g0 + (ti + 1) * 128, :], yo[ti % 2][:])
                k.barrier()
            hst.close()

    def layer_mod(self, u, l):
        k = self.k
        c = u.cond
        k.stt(self.Ga[:], self.modc[:, l, 8:16, c], 1.0, self.ncol[:, l, 0, :], ALU.add, ALU.mult)
        k.stt(self.Gf[:], self.modc[:, l, 32:40, c], 1.0, self.ncol[:, l, 1, :], ALU.add, ALU.mult)
        with ExitStack() as ph:
            bcf = [k.sb("bcf%d" % i, [128, 128], F32, ph) for i in range(2)]
            n = 0
            for gi, v in enumerate((16, 40)):
                for j in range(8):
                    b = bcf[n % 2]
                    k.cp(b[:], self.modc[:, l, v + j, c:c + 1].to_broadcast([128, 128]))
                    pb = self.pb[n % 4]
                    k.mm(pb[:, 0:128], b[:], self.ident, tr=True)
                    self.evac(self.grow[gi][:, j * 128:(j + 1) * 128], pb[:, 0:128])
                    n += 1
            k.barrier()

    def norm_T(self, u, l, G, shbase):
        k = self.k
        c = u.cond
        with ExitStack() as ph:
            hs = k.sb("hs_" + u.name, [128, 4, D], BF16, ph)
            for b in range(u.nblk):
                for ti in range(4):
                    t = b * 4 + ti
                    st = self.nst()
                    k.act(self.junk[:], self.h[t][:], AF.Square, accum=st[:, 0:1])
                    self.rstd_from_ss(st, D)
                    k.act(hs[:, ti, :], self.h[t][:], AF.Copy, scale=st[:, 2:3], pw=True)
                for j in range(8):
                    half = (j % 2) * 512
                    for ti in range(4):
                        k.mm(self.pbb[:, half + ti * 128:half + (ti + 1) * 128], hs[:, ti, j * 128:(j + 1) * 128],
                             self.identb[:], tr=True, part=True)
                    k.act(self.xT[:, j, b * 512:(b + 1) * 512], self.pbb[:, half:half + 512], AF.Identity,
                          scale=G[:, j:j + 1], bias=self.modc[:, l, shbase + j, c:c + 1], pw=True)
            k.barrier()

    def mla(self, u, l, ph):
        k = self.k
        d = self.d
        sb = lambda n, shp, dt: k.sb("%s_%s%d" % (n, u.name, l), shp, dt, ph)
        T = u.T
        pb = self.pb
        wuq = self.wload(d["wuq"][l, :, :, :], 2, 1024)
        wuqg = sb("wuqg", [128, 2, 1024], BF16)
        for kk in range(2):
            k.ts(wuqg[:, kk, :], wuq[:, kk, :], self.qnorm[:, l, kk:kk + 1], None, ALU.mult, pw=True)
        wukT = sb("wukT", [64, 8, 128], BF16)
        k.dma(wukT[:], d["wukT"][l, :, :, :], q="pool")
        wuv = sb("wuv", [128, 512], BF16)
        k.dma(wuv[:], d["wuv"][l, :, :], q="pool")
        gkv = sb("gkv", [128, 128], F32)
        k.dma(gkv[:], d["kvnorm"][l, 0:1, :].partition_broadcast(128))
        wkv = self.wload(d["win"][l, :, :, C_MKV:C_MKV + 160], 8, 160)
        own = sb("own", [128, u.ntile, 160], F32)
        nkc = u.nkt * 128
        Kl = sb("Kl", [128, nkc], BF16)
        Kr = sb("Kr", [32, nkc], BF16)
        r1 = sb("r1", [128, 32], F32); r2 = sb("r2", [128, 32], F32)
        if u.sample:
            KlO = sb("KlO", [128, T], BF16); KrO = sb("KrO", [32, T], BF16)
            kdst, rdst, kofs = KlO, KrO, 0
        else:
            kdst, rdst, kofs = Kl, Kr, 0
        self.ckpt(2.1)
        for ti in range(u.ntile):
            t = ti
            p = pb[ti % 2]
            for kk in range(8):
                k.mm(p[:, 0:160], self.xT[:, kk, ti * 128:(ti + 1) * 128], wkv[:, kk, :], start=(kk == 0), stop=(kk == 7))
            st = self.nst()
            k.act(self.junk[:, 0:128], p[:, 0:128], AF.Square, accum=st[:, 0:1])
            self.rstd_from_ss(st, 128)
            k.stt(own[:, ti, 0:128], p[:, 0:128], st[:, 2:3], gkv[:], ALU.mult, ALU.mult, pw=True)
            k.tt(r1[:], p[:, 128:160], self.ropetok[:, 0, t, :], ALU.mult)
            for (a, b) in ((0, 8), (8, 0), (16, 24), (24, 16)):
                k.tt(r2[:, a:a + 8], p[:, 128 + b:128 + b + 8], self.ropetok[:, 1, t, a:a + 8], ALU.mult, pw=True)
            k.tt(own[:, ti, 128:160], r1[:], r2[:], ALU.add, pw=True)
            if not u.sample:
                k.dma(d["ncache"][ti // 2, l, (ti % 2) * 128:(ti % 2 + 1) * 128, :], own[:, ti, :])
            import os
            if os.environ.get("KSKIP") == "tr":
                continue
            pt = pb[2 + ti % 2]
            k.mm(pt[:, 0:128], own[:, ti, 0:128], self.ident)
            k.cp(kdst[:, kofs + ti * 128:kofs + (ti + 1) * 128], pt[:, 0:128], pw=True, eng="act")
            if os.environ.get("KSKIP") == "tr2":
                continue
            k.mm(pt[:, 128:256], own[:, ti, 32:160], self.ident, part=True)
            k.cp(rdst[:, kofs + ti * 128:kofs + (ti + 1) * 128], pt[96:128, 128:256], pw=True)
        if u.sample:
            self.kv_exchange(u, l, ph, KlO, KrO, Kl, Kr)
        self.ckpt(2.2)
        V = sb("V", [128, u.nkt, 512], BF16)
        for kt in range(u.nkt):
            p = pb[kt % 4]
            k.mm(p[:], Kl[:, kt * 128:(kt + 1) * 128], wuv[:])
            self.evac(V[:, kt, :], p[:])
        self.ckpt(2.3)
        wq = self.wload(d["win"][l, :, :, C_MQ:C_MQ + 256], 8, 256)
        mqT = sb("mqT", [128, 2, T], BF16)
        qnT = sb("qnT", [128, 2, T], BF16)
        sq = sb("sq", [128, 2, 512], F32)
        rq = sb("rq", [128, 512], F32)
        for b in range(u.nblk):
            bs = slice(b * 512, (b + 1) * 512)
            for c in range(2):
                p = pb[c]
                for kk in range(8):
                    k.mm(p[:], wq[:, kk, c * 128:(c + 1) * 128], self.xT[:, kk, bs], start=(kk == 0), stop=(kk == 7))
                k.cp(mqT[:, c, bs], p[:], pw=True)
                if os.environ.get("KSKIP") == "c1":
                    continue
                k.act(sq[:, c, :], mqT[:, c, bs], AF.Square, pw=True)
            if os.environ.get("KSKIP") in ("c1", "c2"):
                continue
            pss = pb[2]
            for c in range(2):
                k.mm(pss[:], self.onesf[:], sq[:, c, :], start=(c == 0), stop=(c == 1))
            k.act(rq[:], pss[:], AF.Ln, scale=1.0 / 256, bias=self.epsc[:, 0:1])
            k.act(rq[:], rq[:], AF.Exp, scale=-0.5)
            if os.environ.get("KSKIP") == "c3":
                continue
            for c in range(2):
                k.tt(qnT[:, c, bs], mqT[:, c, bs], rq[:], ALU.mult, pw=True)
        self.ckpt(2.4)
        qn_ = [sb("qn%d" % i, [64, 512], BF16) for i in range(2)]
        qa = [sb("qa%d" % i, [128, 512], BF16) for i in range(2)]
        qr = [sb("qr%d" % i, [32, 512], BF16) for i in range(2)]
        t1 = sb("t1", [32, 512], F32); t2 = sb("t2", [32, 512], F32)
        PT = [sb("PT%d" % i, [128, 512], BF16) for i in range(3)]
        rl = sb("rl", [64, 512], F32)
        g0 = 0
        it = 0
        for h in range(8):
            for (q0, nq, kts) in u.qblocks:
                i2 = it % 2; it += 1
                qs = slice(q0, q0 + nq)
                p1 = pb[5]
                for c in range(2):
                    k.mm(p1[0:64, 0:nq], wuqg[:, c, h * 128:h * 128 + 64], qnT[:, c, qs], start=(c == 0), stop=(c == 1))
                k.cp(qn_[i2][:, 0:nq], p1[0:64, 0:nq], eng="act")
                p2 = pb[6]
                k.mm(p2[:, 0:nq], wukT[:, h, :], qn_[i2][:, 0:nq])
                k.cp(qa[i2][:, 0:nq], p2[:, 0:nq])
                p3 = pb[5]; p4 = pb[6]
                for c in range(2):
                    k.mm(p3[0:32, 0:nq], wuqg[:, c, h * 128 + 64:h * 128 + 96], qnT[:, c, qs], start=(c == 0), stop=(c == 1))
                for c in range(2):
                    k.mm(p4[0:32, 0:nq], wuqg[:, c, h * 128 + 96:h * 128 + 128], qnT[:, c, qs], start=(c == 0), stop=(c == 1))
                k.tt(t1[:, 0:nq], p3[0:32, 0:nq], self.rope[:, 0, g0 + q0:g0 + q0 + nq], ALU.mult)
                k.tt(t2[:, 0:nq], p4[0:32, 0:nq], self.rope[:, 1, g0 + q0:g0 + q0 + nq], ALU.mult)
                k.tt(qr[i2][:, 0:nq], t1[:, 0:nq], t2[:, 0:nq], ALU.add)
                po = pb[0]; pl = pb[1]
                nk = len(kts)
                def score(i):
                    kt_ = kts[i]
                    psc_ = pb[2 + i % 3]
                    k.mm(psc_[:, 0:nq], Kl[:, kt_ * 128:(kt_ + 1) * 128], qa[i2][:, 0:nq], start=True, stop=False)
                    k.mm(psc_[:, 0:nq], Kr[:, kt_ * 128:(kt_ + 1) * 128], qr[i2][:, 0:nq], start=False, stop=True)

                score(0)
                if nk > 1:
                    score(1)
                for i, kt in enumerate(kts):
                    if i + 2 < nk:
                        score(i + 2)
                    psc = pb[2 + i % 3]
                    P = PT[i % 3]
                    k.act(P[:, 0:nq], psc[:, 0:nq], AF.Exp, scale=ATT_SCALE)
                    k.mm(po[0:64, 0:nq], V[:, kt, h * 64:(h + 1) * 64], P[:, 0:nq], start=(i == 0), stop=(i == nk - 1))
                    k.mm(pl[0:64, 0:nq], self.onesb[:, 0:64], P[:, 0:nq], start=(i == 0), stop=(i == nk - 1))
                k.recip(rl[:, 0:nq], pl[0:64, 0:nq])
                k.tt(self.ymla[(h % 2) * 64:(h % 2) * 64 + 64, h // 2, qs], po[0:64, 0:nq], rl[:, 0:nq], ALU.mult, pw=True)

    def allgather(self, gin, gout):
        k = self.k
        groups = [[0, 1, 2, 3], [4, 5, 6, 7]]
        return k.op("pool", lambda e: e.collective_compute("AllGather", ALU.bypass, replica_groups=groups,
                                                            ins=[gin.ap().opt()], outs=[gout.ap().opt()]),
                    reads=[gin.name], writes=[gout.name], cc=True)

    def kv_exchange(self, u, l, ph, KlO, KrO, Kl, Kr):
        k = self.k
        nc = k.nc
        pb = self.pb
        gin = nc.dram_tensor("kvgin%d" % l, [160, 1024], BF16)
        gout = nc.dram_tensor("kvgout%d" % l, [640, 1024], BF16)
        k.dma(gin[0:128, :], KlO[:])
        k.dma(gin[128:160, :], KrO[:], pw=True)
        self.allgather(gin, gout)
        for r in range(4):
            k.dma(Kl[:, 256 + r * 1024:256 + (r + 1) * 1024], gout[r * 160:r * 160 + 128, :], pw=True)
            k.dma(Kr[:, 256 + r * 1024:256 + (r + 1) * 1024], gout[r * 160 + 128:r * 160 + 160, :], pw=True)
        cch = k.sb("cch%d" % l, [128, 2, 160], F32, ph)
        k.dma(cch[:], self.d["cache"][l].rearrange("(a p) f -> p a f", p=128))
        for a_ in range(2):
            pt = pb[2 + a_]
            k.mm(pt[:, 0:128], cch[:, a_, 0:128], self.ident)
            k.cp(Kl[:, a_ * 128:(a_ + 1) * 128], pt[:, 0:128], pw=True, eng="act")
            k.mm(pt[:, 128:256], cch[:, a_, 32:160], self.ident, part=True)
            k.cp(Kr[:, a_ * 128:(a_ + 1) * 128], pt[96:128, 128:256], pw=True)

    def gla(self, u, l, ph):
        k = self.k
        d = self.d
        pb = self.pb
        T = u.T
        sb = lambda n, shp, dt: k.sb("%s_%s%d" % (n, u.name, l), shp, dt, ph)
        xT = self.xT
        wgate = sb("wgate", [17, 512], BF16)
        k.dma(wgate[:], d["wgate"][l, :, :], q="pool")
        triFn = sb("triFn", [128, 128], F32); triBn = sb("triBn", [128, 128], F32)
        k.ts(triFn[:], self.triF, -1.0 / 16, None, ALU.mult)
        k.ts(triBn[:], self.triB, -1.0 / 16, None, ALU.mult)
        gqT = sb("gqT", [128, 2, T], BF16); gkT = sb("gkT", [128, 2, T], BF16); sgrT = sb("sgrT", [128, 4, T], BF16)
        glr = [sb("glrf", [17, T], BF16), sb("glrb", [17, T], BF16)]
        k.memset(glr[0][:], 1.0); k.memset(glr[1][:], 1.0)
        gk_tok = sb("gk_tok", [128, u.ntile, 256], BF16); gv_tok = sb("gv_tok", [128, u.ntile, 512], BF16)
        wA = self.wload(d["win"][l, :, :, C_GQ:C_GQ + 512], 8, 512)
        for b_ in range(u.nblk):
            bs = slice(b_ * 512, (b_ + 1) * 512)
            for c in range(4):
                p = pb[c % 4]
                for kk in range(8):
                    k.mm(p[:], wA[:, kk, c * 128:(c + 1) * 128], xT[:, kk, bs], start=(kk == 0), stop=(kk == 7))
                if c < 2:
                    k.act(gqT[:, c, bs], p[:], AF.Copy, scale=0.125, pw=True)
                else:
                    k.cp(gkT[:, c - 2, bs], p[:], pw=True)
        wB = self.wload(d["win"][l, :, :, C_GR:C_GR + 512], 8, 512)
        for b_ in range(u.nblk):
            bs = slice(b_ * 512, (b_ + 1) * 512)
            for c in range(4):
                p = pb[c % 4]
                for kk in range(8):
                    k.mm(p[:], wB[:, kk, c * 128:(c + 1) * 128], xT[:, kk, bs], start=(kk == 0), stop=(kk == 7))
                k.act(sgrT[:, c, bs], p[:], AF.Silu, pw=True)
        wC = self.wload(d["win"][l, :, :, C_GLR:C_GLR + 64], 8, 64)
        for b_ in range(u.nblk):
            bs = slice(b_ * 512, (b_ + 1) * 512)
            p = pb[4 + b_ % 2]
            for kk in range(8):
                k.mm(p[0:64, :], wC[:, kk, :], xT[:, kk, bs], start=(kk == 0), stop=(kk == 7))
            k.cp(glr[0][0:16, bs], p[0:16, :], pw=True)
            k.cp(glr[1][0:16, bs], p[32:48, :], pw=True)
        wD = self.wload(d["win"][l, :, :, C_GKT:C_GKT + 512], 8, 512)
        wE = self.wload(d["win"][l, :, :, C_GKT + 512:C_GKT + 768], 8, 256)
        for ti in range(u.ntile):
            ts_ = slice(ti * 128, (ti + 1) * 128)
            p = pb[(ti % 2) * 2]; p2 = pb[(ti % 2) * 2 + 1]
            for kk in range(8):
                k.mm(p[:], xT[:, kk, ts_], wD[:, kk, :], start=(kk == 0), stop=(kk == 7))
            for kk in range(8):
                k.mm(p2[:, 0:256], xT[:, kk, ts_], wE[:, kk, :], start=(kk == 0), stop=(kk == 7))
            k.cp(gk_tok[:, ti, :], p[:, 0:256], pw=True)
            k.cp(gv_tok[:, ti, 0:256], p[:, 256:512], pw=True)
            k.cp(gv_tok[:, ti, 256:512], p2[:, 0:256], pw=True, eng="act")
        self.ckpt(3.1)
        W = []
        for i in range(2):
            W.append(dict(
                e1=sb("ge1_%d" % i, [128, 512], F32), sp=sb("gsp_%d" % i, [128, 512], F32),
                en=sb("gen_%d" % i, [128, 512], F32), kinv=sb("gkinv_%d" % i, [128, 512], BF16),
                EpT=sb("gEpT_%d" % i, [128, 512], F32), EnT=sb("gEnT_%d" % i, [128, 512], F32),
                qdec=sb("gqdec_%d" % i, [128, 4, 128], BF16), kinvT=sb("gkinvT_%d" % i, [128, 4, 128], BF16),
                aTm=[sb("gaTm0_%d" % i, [128, 512], BF16), sb("gaTm1_%d" % i, [128, 512], BF16)]))
        pz, pc, pcT, pu, pa, po = pb[0], pb[1], pb[2], pb[3], [pb[4], pb[5]], pb[6]
        tri = [triFn, triBn]
        msk = [self.triF, self.triB]

        def prep(ti, w):
            ts_ = slice(ti * 128, (ti + 1) * 128)
            for dd in range(2):
                k.mm(pz[:, dd * 256:(dd + 1) * 256], glr[dd][:, ts_], wgate[:, dd * 256:(dd + 1) * 256], part=True)
            k.act(w["e1"][:], pz[:], AF.Exp, scale=-1.0)
            k.act(w["sp"][:], w["e1"][:], AF.Ln, bias=1.0)
            for dd in range(2):
                k.mm(pc[:, dd * 256:(dd + 1) * 256], tri[dd][:], w["sp"][:, dd * 256:(dd + 1) * 256], part=True)
            k.act(w["en"][:], pc[:], AF.Exp, scale=-1.0)
            k.tt(w["kinv"][:].rearrange("p (a b) -> p a b", a=2), w["en"][:].rearrange("p (a b) -> p a b", a=2),
                 gk_tok[:, ti, :].unsqueeze(1).to_broadcast([128, 2, 256]), ALU.mult)
            for dd in range(2):
                for pr_ in range(2):
                    i4 = dd * 2 + pr_
                    k.mm(pcT[:, i4 * 128:(i4 + 1) * 128], w["sp"][:, dd * 256 + pr_ * 128:dd * 256 + (pr_ + 1) * 128], tri[dd][:], part=True)
            k.act(w["EpT"][:], pcT[:], AF.Exp)
            k.act(w["EnT"][:], pcT[:], AF.Exp, scale=-1.0)
            k.tt(w["qdec"][:].rearrange("p (a b) c -> p a b c", a=2), w["EpT"][:].rearrange("p (a b c) -> p a b c", a=2, b=2),
                 gqT[:, :, ts_].unsqueeze(1).to_broadcast([128, 2, 2, 128]), ALU.mult)
            k.tt(w["kinvT"][:].rearrange("p (a b) c -> p a b c", a=2), w["EnT"][:].rearrange("p (a b c) -> p a b c", a=2, b=2),
                 gkT[:, :, ts_].unsqueeze(1).to_broadcast([128, 2, 2, 128]), ALU.mult)

        def dec_col(w, dd, pr_, ch):
            c = (dd * 2 + pr_) * 128 + ch * 64 + (63 if dd == 0 else 0)
            return w["EpT"][:, c:c + 1]

        nch = 2 * u.ntile
        S = [sb("gS%d" % i, [128, 2, 128], F32) for i in range(2)]
        Stmp = sb("gStmp", [128, 2, 128], F32)
        SpB = sb("gSpB", [128, nch, 2, 128], BF16)
        Sfb = [sb("gSfb%d" % i, [128, 2, 2, 128], BF16) for i in range(2)]
        oT = sb("goT", [128, 4, T], F32)
        exch = u.sample
        if exch:
            qhT = sb("gqhT", [128, 2, 2, T], BF16)
            cum = [sb("gcum%d" % i, [128, 2], F32) for i in range(2)]

        def state_update(w, ti, dd, ch):
            cr = slice(ch * 64, (ch + 1) * 64)
            for h in range(4):
                k.mm(pu[(h % 2) * 64:(h % 2) * 64 + 64, (h // 2) * 128:(h // 2 + 1) * 128],
                     w["kinv"][cr, dd * 256 + h * 64:dd * 256 + (h + 1) * 64], gv_tok[cr, ti, h * 128:(h + 1) * 128], part=True)
            k.tt(Stmp[:].rearrange("p a b -> p (a b)"), S[dd][:].rearrange("p a b -> p (a b)"), pu[:, 0:256], ALU.add)
            for pr_ in range(2):
                k.ts(S[dd][:, pr_, :], Stmp[:, pr_, :], dec_col(w, dd, pr_, ch), None, ALU.mult, pw=True)

        def qhat(w, ti, dd, ch):
            if not exch:
                return
            col = slice(ti * 128 + ch * 64, ti * 128 + (ch + 1) * 64)
            for pr_ in range(2):
                k.ts(qhT[:, dd, pr_, col], w["qdec"][:, dd * 2 + pr_, ch * 64:(ch + 1) * 64], cum[dd][:, pr_:pr_ + 1], None, ALU.mult, pw=True)
                k.tt(cum[dd][:, pr_:pr_ + 1], cum[dd][:, pr_:pr_ + 1], dec_col(w, dd, pr_, ch), ALU.mult, pw=True)

        for (s0, n) in u.segs:
            tiles = list(range(s0 // 128, (s0 + n) // 128))
            seg = s0 // 256
            k.memset(S[1][:], 0.0)
            if exch:
                k.memset(cum[1][:], 1.0)
            for it, ti in enumerate(reversed(tiles)):
                w = W[it % 2]
                prep(ti, w)
                for ch in (1, 0):
                    k.cp(SpB[:, ti * 2 + ch, :, :], S[1][:], pw=True, eng="act")
                    qhat(w, ti, 1, ch)
                    state_update(w, ti, 1, ch)
            if not exch:
                k.dma(d["nsgla"][seg, l, 1].rearrange("(j i) k v -> (i k) j v", i=2), S[1][:])
            self.ckpt(3.2)
            k.memset(S[0][:], 0.0)
            if exch:
                k.memset(cum[0][:], 1.0)
            for it, ti in enumerate(tiles):
                w = W[it % 2]
                sf = Sfb[it % 2]
                ts_ = slice(ti * 128, (ti + 1) * 128)
                prep(ti, w)
                for ch in (0, 1):
                    k.cp(sf[:, ch, :, :], S[0][:], pw=True, eng="act")
                    qhat(w, ti, 0, ch)
                    state_update(w, ti, 0, ch)
                sk = os.environ.get("KSKIP", "")
                if sk == "s2a":
                    continue
                for dd in range(2):
                    for h in (0, 2, 1, 3):
                        hp = slice((h % 2) * 64, (h % 2) * 64 + 64)
                        for sh in range(2):
                            k.mm(pa[dd][sh * 64:(sh + 1) * 64, h * 128:(h + 1) * 128], w["kinvT"][hp, dd * 2 + h // 2, sh * 64:(sh + 1) * 64],
                                 w["qdec"][hp, dd * 2 + h // 2, :], part=True)
                    k.tt(w["aTm"][dd][:].rearrange("p (a b) -> p a b", a=4), pa[dd][:].rearrange("p (a b) -> p a b", a=4),
                         msk[dd].unsqueeze(1).to_broadcast([128, 4, 128]), ALU.mult)
                if sk == "s2b":
                    continue
                pi_ = pb[3]
                for h in range(4):
                    hp = slice((h % 2) * 64, (h % 2) * 64 + 64)
                    reg = po[:, h * 128:(h + 1) * 128]
                    k.mm(reg, gv_tok[:, ti, h * 128:(h + 1) * 128], w["aTm"][0][:, h * 128:(h + 1) * 128], start=True, stop=False, part=True)
                    k.mm(reg, gv_tok[:, ti, h * 128:(h + 1) * 128], w["aTm"][1][:, h * 128:(h + 1) * 128], start=False, stop=True, part=True)
                if sk == "s2c":
                    k.cp(oT[:, :, ts_], po[:].rearrange("p (a b) -> p a b", a=4), pw=True, eng="act")
                    continue
                for h in (0, 2, 1, 3):
                    hp = slice((h % 2) * 64, (h % 2) * 64 + 64)
                    for ch in range(2):
                        for vh in range(2):
                            sub = pi_[vh * 64:(vh + 1) * 64, h * 128 + ch * 64:h * 128 + (ch + 1) * 64]
                            k.mm(sub, sf[hp, ch, h // 2, vh * 64:(vh + 1) * 64], w["qdec"][hp, 0 * 2 + h // 2, ch * 64:(ch + 1) * 64], start=True, stop=False, part=True)
                            k.mm(sub, SpB[hp, ti * 2 + ch, h // 2, vh * 64:(vh + 1) * 64], w["qdec"][hp, 1 * 2 + h // 2, ch * 64:(ch + 1) * 64], start=False, stop=True, part=True)
                k.cp(oT[:, :, ts_], po[:].rearrange("p (a b) -> p a b", a=4), pw=True, eng="act")
                k.tt(oT[:, :, ts_], oT[:, :, ts_], pi_[:].rearrange("p (a b) -> p a b", a=4), ALU.add, pw=True)
            if not exch:
                k.dma(d["nsgla"][seg, l, 0].rearrange("(j i) k v -> (i k) j v", i=2), S[0][:])
            self.ckpt(3.3)
        if exch:
            self.gla_exchange(u, l, ph, S, cum, qhT, oT)
        k.barrier()
        sq = [W[0]["e1"], W[0]["sp"]]
        rs = [W[0]["en"], W[0]["EpT"]]
        n = 0
        for h in range(4):
            for b_ in range(u.nblk):
                bs = slice(b_ * 512, (b_ + 1) * 512)
                i2 = n % 2; n += 1
                k.tt(sq[i2][:], oT[:, h, bs], oT[:, h, bs], ALU.mult)
                pss = pb[i2]
                k.mm(pss[:], self.onesf[:], sq[i2][:])
                k.act(rs[i2][:], pss[:], AF.Ln, scale=1.0 / 128, bias=self.epsc[:, 0:1])
                k.act(rs[i2][:], rs[i2][:], AF.Exp, scale=-0.5)
                k.tt(sq[i2][:], oT[:, h, bs], rs[i2][:], ALU.mult)
                k.stt(self.ygla[:, h, bs], sq[i2][:], self.gcol[:, l, 0:1], sgrT[:, h, bs], ALU.mult, ALU.mult, pw=True)

    def gla_exchange(self, u, l, ph, S, cum, qhT, oT):
        k = self.k
        nc = k.nc
        pb = self.pb
        sb = lambda n, shp, dt: k.sb("%s_%s%d" % (n, u.name, l), shp, dt, ph)
        cg = sb("gcg", [128, 2, 2, 129], F32)
        for dd in range(2):
            k.cp(cg[:, dd, :, 0:128], S[dd][:], pw=True)
            k.cp(cg[:, dd, :, 128:129], cum[dd][:].unsqueeze(2), pw=True)
        gin = nc.dram_tensor("glagin%d" % l, [128, 516], F32)
        gout = nc.dram_tensor("glagout%d" % l, [512, 516], F32)
        k.dma(gin[:, :], cg[:].rearrange("p a b c -> p (a b c)"))
        self.allgather(gin, gout)
        cgs = sb("gcgs", [128, 4, 516], F32)
        k.dma(cgs[:], gout[:, :].rearrange("(r p) f -> p r f", p=128))
        S0 = sb("gS0", [128, 2, 2, 128], F32)
        for dd in range(2):
            k.dma(S0[:, dd, :, :], self.d["sgla"][l, dd].rearrange("(j i) k v -> (i k) j v", i=2), pw=True)
        new = sb("gnew", [128, 2, 128], F32); diff = sb("gdiff", [128, 2, 128], F32)
        Sent = sb("gSent", [128, 2, 2, 128], BF16)
        for dd in range(2):
            cur = S0[:, dd, :, :]
            for j in ([0, 1, 2] if dd == 0 else [3, 2, 1]):
                for pr_ in range(2):
                    off = (dd * 2 + pr_) * 129
                    k.stt(new[:, pr_, :], cur[:, pr_, :], cgs[:, j, off + 128:off + 129], cgs[:, j, off:off + 128], ALU.mult, ALU.add, pw=True)
                k.tt(diff[:], new[:], cur, ALU.subtract)
                mc = (j if dd == 0 else 4 + j)
                k.stt(cur, diff[:], self.rank[:, mc:mc + 1], cur, ALU.mult, ALU.add, pw=True)
            k.cp(Sent[:, dd, :, :], cur, pw=True)
        n = 0
        for h in range(4):
            hp = slice((h % 2) * 64, (h % 2) * 64 + 64)
            for b_ in range(u.nblk):
                bs = slice(b_ * 512, (b_ + 1) * 512)
                pc_ = pb[n % 4]; n += 1
                for vh in range(2):
                    for dd in range(2):
                        k.mm(pc_[vh * 64:(vh + 1) * 64, :], Sent[hp, dd, h // 2, vh * 64:(vh + 1) * 64], qhT[hp, dd, h // 2, bs],
                             start=(dd == 0), stop=(dd == 1), part=True)
                k.tt(oT[:, h, bs], oT[:, h, bs], pc_[:], ALU.add, pw=True)

    def gdn(self, u, l, ph):
        k = self.k
        d = self.d
        pb = self.pb
        T = u.T
        sb = lambda n, shp, dt: k.sb("%s_%s%d" % (n, u.name, l), shp, dt, ph)
        xT = self.xT
        exch = u.sample
        Wd = 128 if exch else 64
        nseg = len(u.segs)
        Tp = T + 2 * nseg
        adtb = sb("adtb", [128, 32], F32)
        k.dma(adtb[:], d["adt"][l, 0:1, :].partition_broadcast(128))
        negA = sb("negA", [128, 16], F32)
        k.act(negA[:], adtb[:, 0:16], AF.Exp)
        k.ts(negA[:], negA[:], -1.0, None, ALU.mult)
        qT = sb("dqT", [128, 4, T], BF16); kT = sb("dkT", [128, 4, T], BF16)
        k_tok = sb("dk_tok", [128, u.ntile, 512], BF16); v_tok = sb("dv_tok", [128, u.ntile, 512], BF16)
        O1 = sb("dO1", [128, 8, T], BF16)
        O2 = sb("dO2", [128, 8, T], BF16) if exch else None
        nt = u.ntile
        dab = sb("dab", [128, u.ntile, 32], F32)
        g_tok = sb("dg", [128, nt, 16], F32); l2 = sb("dl2", [128, nt, 16], F32); bet = sb("dbet", [128, nt, 16], F32)
        tmpg = sb("dtmpg", [128, nt, 16], F32)
        halo = None
        if exch:
            halo = self.gdn_halo(u, l, ph, None)
            self._gcomp = sb("dgcomp", [128, 2, 2, 2, 128], F32)
        cs_stack = ExitStack()
        sbc = lambda n, shp, dt: k.sb("%s_%s%d" % (n, u.name, l), shp, dt, cs_stack)
        vT = sbc("dvT", [128, 4, T], BF16)
        raw = [sbc("draw%d" % i, [128, Tp], F32) for i in range(2)]
        cacc = [sbc("dcacc%d" % i, [128, T], F32) for i in range(2)]
        csl = [sbc("dcsl%d" % i, [128, T], F32) for i in range(2)]
        sq = [sbc("dsq%d" % i, [128, 512], F32) for i in range(2)]
        rn = [sbc("drn%d" % i, [128, 512], F32) for i in range(2)]
        for r_ in raw:
            k.memset(r_[:], 0.0)
        for c3 in range(3):
            wq_ = self.wload(d["win"][l, :, :, C_DQKV + c3 * 512:C_DQKV + (c3 + 1) * 512], 8, 512)
            for c4 in range(4):
                c = c3 * 4 + c4
                rw = raw[c % 2]; ca = cacc[c % 2]; cs_ = csl[c % 2]
                for b_ in range(u.nblk):
                    p = pb[(c * u.nblk + b_) % 4]
                    for kk in range(8):
                        k.mm(p[:], wq_[:, kk, c4 * 128:(c4 + 1) * 128], xT[:, kk, b_ * 512:(b_ + 1) * 512], start=(kk == 0), stop=(kk == 7))
                    for si, (s0, n) in enumerate(u.segs):
                        lo = max(s0, b_ * 512); hi = min(s0 + n, (b_ + 1) * 512)
                        if lo < hi:
                            o = 2 * si + 1
                            k.cp(rw[:, o + lo:o + hi], p[:, lo - b_ * 512:hi - b_ * 512], pw=True)
                if halo is not None:
                    k.cp(rw[:, 0:1], halo[0][:, c:c + 1], pw=True)
                    k.cp(rw[:, Tp - 1:Tp], halo[1][:, c:c + 1], pw=True)
                for si, (s0, n) in enumerate(u.segs):
                    o = s0 + 2 * si
                    k.ts(ca[:, s0:s0 + n], rw[:, o:o + n], self.convc[:, l, c, 0:1], None, ALU.mult, pw=True)
                    k.stt(ca[:, s0:s0 + n], rw[:, o + 1:o + 1 + n], self.convc[:, l, c, 1:2], ca[:, s0:s0 + n], ALU.mult, ALU.add, pw=True)
                    k.stt(ca[:, s0:s0 + n], rw[:, o + 2:o + 2 + n], self.convc[:, l, c, 2:3], ca[:, s0:s0 + n], ALU.mult, ALU.add, pw=True)
                if c3 == 2:
                    k.act(vT[:, c4, :], ca[:], AF.Silu, pw=True)
                    continue
                k.act(cs_[:], ca[:], AF.Silu)
                for b_ in range(u.nblk):
                    bs = slice(b_ * 512, (b_ + 1) * 512)
                    i2 = (c * u.nblk + b_) % 2
                    k.tt(sq[i2][:], cs_[:, bs], cs_[:, bs], ALU.mult)
                    pss = pb[4 + i2]
                    k.mm(pss[:], self.bones, sq[i2][:])
                    k.act(rn[i2][:], pss[:], AF.Ln, bias=self.epsc[:, 0:1])
                    k.act(rn[i2][:], rn[i2][:], AF.Exp, scale=-0.5)
                    if c3 == 0:
                        k.stt(qT[:, c4, bs], cs_[:, bs], 0.125, rn[i2][:], ALU.mult, ALU.mult, pw=True)
                    else:
                        k.tt(kT[:, c4, bs], cs_[:, bs], rn[i2][:], ALU.mult, pw=True)
        wab = self.wload(d["win"][l, :, :, C_DAB:C_DAB + 32], 8, 32)
        for ti in range(u.ntile):
            p = pb[ti % 4]
            for kk in range(8):
                k.mm(p[:, 0:32], xT[:, kk, ti * 128:(ti + 1) * 128], wab[:, kk, :], start=(kk == 0), stop=(kk == 7))
            self.evac(dab[:, ti, :], p[:, 0:32])
        k.tt(tmpg[:], dab[:, :, 0:16], adtb[:, 16:32].unsqueeze(1).to_broadcast([128, nt, 16]), ALU.add)
        k.act(tmpg[:], tmpg[:], AF.Exp)
        k.act(tmpg[:], tmpg[:], AF.Ln, bias=1.0)
        k.tt(g_tok[:], tmpg[:], negA[:].unsqueeze(1).to_broadcast([128, nt, 16]), ALU.mult)
        k.act(l2[:], dab[:, :, 16:32], AF.Exp, scale=-1.0)
        k.act(l2[:], l2[:], AF.Ln, bias=1.0)
        k.act(bet[:], l2[:], AF.Exp, scale=-1.0)
        for ti in range(nt):
            ts_ = slice(ti * 128, (ti + 1) * 128)
            for c in range(4):
                k.mm(self.pbb[:, c * 128:(c + 1) * 128], kT[:, c, ts_], self.identb[:], tr=True, part=True)
                k.mm(self.pbb[:, 512 + c * 128:512 + (c + 1) * 128], vT[:, c, ts_], self.identb[:], tr=True, part=True)
            k.cp(k_tok[:, ti, :], self.pbb[:, 0:512], pw=True)
            k.cp(v_tok[:, ti, :], self.pbb[:, 512:1024], pw=True)
        k.barrier()
        cs_stack.close()
        cm_stack = ExitStack()
        sb_ph = sb
        sb = lambda n, shp, dt: k.sb("%s_%s%d" % (n, u.name, l), shp, dt, cm_stack)
        tri01 = [self.triF, self.triB]
        negi = [self.negFi, self.negBi]; negs = [self.negFs, self.negBs]
        def mkset(i):
            f4 = lambda n: sb("%s_%d" % (n, i), [128, 4, 128], F32)
            B = dict(rhsG=f4("drhsG"), rhsG2=f4("drhsG2"), tmp4=f4("dtmp4"), arg=f4("darg"), Einc=f4("dEinc"), Estr=f4("dEstr"),
                     EG=f4("dEG"), A=f4("dA"), Lm=f4("dL"), U2=f4("dU2"), L2m=f4("dL2"), Wm=f4("dWm"))
            B["aqk"] = sb("daqk_%d" % i, [128, 4, 128], BF16)
            B["BR"] = sb("dBR_%d" % i, [128, 4, 64], F32); B["BRk"] = sb("dBRk_%d" % i, [128, 4, 128], F32)
            B["gam"] = sb("dgam_%d" % i, [128, 4], F32); B["egam"] = sb("degam_%d" % i, [128, 4], F32); B["kf"] = sb("dkf_%d" % i, [128, 4], F32)
            B["u_tok"] = sb("du_tok_%d" % i, [128, 4, Wd], F32)
            B["wT"] = sb("dwT_%d" % i, [128, 4, 128], BF16)
            B["kend"] = sb("dkend_%d" % i, [128, 4, 64], BF16)
            B["qdT"] = sb("dqdT_%d" % i, [128, 4, 128], BF16)
            B["vnew"] = sb("dvnew_%d" % i, [128, 4, Wd], BF16)
            k.memset(B["BRk"][:], 0.0)
            k.memset(B["u_tok"][:], 0.0)
            return B
        bsets = [mkset(0)]
        if os.environ.get("KDB2", "1") == "1":
            try:
                bsets.append(mkset(1))
            except AssertionError:
                bsets.append(bsets[0])
        else:
            bsets.append(bsets[0])
        git = 0
        S = sb("dS", [128, 2, Wd], F32)
        Sb = sb("dSb", [128, 2, Wd], BF16)
        self.gdn_comp = []
        first_dir = True
        for dd in range(2):
            cl = [63, 127] if dd == 0 else [0, 64]
            for hh in range(2):
                gc = slice(dd * 8 + hh * 4, dd * 8 + hh * 4 + 4)
                for si, (s0, n) in enumerate(u.segs):
                    tiles = list(range(s0 // 128, (s0 + n) // 128))
                    if dd == 1:
                        tiles = tiles[::-1]
                    seg = s0 // 256
                    k.memset(S[:], 0.0)
                    if exch:
                        for pr_ in range(2):
                            for q in range(2):
                                k.cp(S[q * 64:(q + 1) * 64, pr_, 64:128], self.ident[q * 64:(q + 1) * 64, q * 64:(q + 1) * 64], pw=True)
                    k.cp(Sb[:], S[:])
                    for ti in tiles:
                        ts_ = slice(ti * 128, (ti + 1) * 128)
                        B = bsets[git % 2]; git += 1
                        rhsG, rhsG2, tmp4, arg, Einc, Estr, EG = B["rhsG"], B["rhsG2"], B["tmp4"], B["arg"], B["Einc"], B["Estr"], B["EG"]
                        A, Lm, U2, L2m, Wm = B["A"], B["Lm"], B["U2"], B["L2m"], B["Wm"]
                        aqk, BR, BRk, gam, egam, kf = B["aqk"], B["BR"], B["BRk"], B["gam"], B["egam"], B["kf"]
                        u_tok, wT, kend, qdT, vnew = B["u_tok"], B["wT"], B["kend"], B["qdT"], B["vnew"]
                        k.tt(rhsG[:], tri01[dd].unsqueeze(1).to_broadcast([128, 4, 128]), g_tok[:, ti, gc].unsqueeze(2).to_broadcast([128, 4, 128]), ALU.mult)
                        k.tt(tmp4[:], self.ident.unsqueeze(1).to_broadcast([128, 4, 128]), l2[:, ti, gc].unsqueeze(2).to_broadcast([128, 4, 128]), ALU.mult)
                        k.tt(rhsG2[:], rhsG[:], tmp4[:], ALU.subtract)
                        pG, pG2, pg = pb[0], pb[1], pb[2]
                        k.mm(pG[:], self.onesf[:], rhsG[:].rearrange("p a b -> p (a b)"))
                        k.mm(pG2[:], self.onesf[:], rhsG2[:].rearrange("p a b -> p (a b)"))
                        k.mm(pg[:, 0:4], tri01[dd], g_tok[:, ti, gc])
                        k.cp(gam[:], pg[:, 0:4])
                        k.act(egam[:], gam[:], AF.Exp)
                        k.act(EG[:].rearrange("p a b -> p (a b)"), pG[:], AF.Exp)
                        k.op("dve", lambda e, tmp4=tmp4, pG=pG, gam=gam: e.tensor_tensor(
                                 out=tmp4[:], in0=pG[:].rearrange("p (a b) -> p a b", a=4),
                                 in1=gam[:].unsqueeze(2).to_broadcast([128, 4, 128]), op=ALU.subtract),
                             reads=[pG.name, gam.name, EG.name], writes=[tmp4.name])
                        k.tt(arg[:], tmp4[:], negi[dd].unsqueeze(1).to_broadcast([128, 4, 128]), ALU.add)
                        k.act(Einc[:], arg[:], AF.Exp)
                        k.tt(tmp4[:], pG2[:].rearrange("p (a b) -> p a b", a=4), gam[:].unsqueeze(2).to_broadcast([128, 4, 128]), ALU.subtract)
                        k.tt(arg[:], tmp4[:], negs[dd].unsqueeze(1).to_broadcast([128, 4, 128]), ALU.add)
                        k.act(Estr[:], arg[:], AF.Exp)
                        pkk, pqk = pb[5], pb[6]
                        for i in (0, 2, 1, 3):
                            h = hh * 4 + i
                            hp = slice((h % 2) * 64, (h % 2) * 64 + 64)
                            k.mm(pkk[:, i * 128:(i + 1) * 128], kT[hp, h // 2, ts_], kT[hp, h // 2, ts_], part=True)
                            k.mm(pqk[:, i * 128:(i + 1) * 128], kT[hp, h // 2, ts_], qT[hp, h // 2, ts_], part=True)
                        k.tt(A[:].rearrange("p a b -> p (a b)"), pkk[:], Estr[:].rearrange("p a b -> p (a b)"), ALU.mult)
                        k.tt(aqk[:].rearrange("p a b -> p (a b)"), pqk[:], Einc[:].rearrange("p a b -> p (a b)"), ALU.mult)
                        pL, pU, pW = pb[5], pb[6], pb[0]
                        for i in range(4):
                            k.mm(pL[:, i * 128:(i + 1) * 128], A[:, i, :], self.ident, part=True)
                        k.cp(Lm[:].rearrange("p a b -> p (a b)"), pL[:], eng="act")
                        k.tt(Wm[:], self.ident.unsqueeze(1).to_broadcast([128, 4, 128]), A[:], ALU.subtract)
                        Uc, Lc, Un, Ln_ = A, Lm, U2, L2m
                        for lev in range(1, 6):
                            if lev < 5:
                                for i in range(4):
                                    k.mm(pU[:, i * 128:(i + 1) * 128], Lc[:, i, :], Uc[:, i, :], part=True)
                            for i in range(4):
                                k.mm(pL[:, i * 128:(i + 1) * 128], Uc[:, i, :], Lc[:, i, :], part=True)
                            if lev < 5:
                                k.cp(Un[:].rearrange("p a b -> p (a b)"), pU[:], eng="act")
                            k.cp(Ln_[:].rearrange("p a b -> p (a b)"), pL[:])
                            for i in range(4):
                                k.mm(pW[:, i * 128:(i + 1) * 128], Ln_[:, i, :], Wm[:, i, :], part=True)
                            k.tt(Wm[:].rearrange("p a b -> p (a b)"), Wm[:].rearrange("p a b -> p (a b)"), pW[:], ALU.add)
                            Uc, Un = Un, Uc
                            Lc, Ln_ = Ln_, Lc
                        hcols = slice(hh * 256, (hh + 1) * 256)
                        k.tt(BR[:], v_tok[:, ti, hcols].rearrange("p (a b) -> p a b", a=4), bet[:, ti, gc].unsqueeze(2).to_broadcast([128, 4, 64]), ALU.mult)
                        k.tt(kf[:], bet[:, ti, gc], egam[:], ALU.mult)
                        for q in range(2):
                            k.tt(BRk[:, q::2, q * 64:(q + 1) * 64], k_tok[:, ti, hcols].rearrange("p (a b) -> p a b", a=4)[:, q::2, :],
                                 kf[:, q::2].unsqueeze(2).to_broadcast([128, 2, 64]), ALU.mult, pw=True)
                        pu_, pw_ = pb[1], pb[2]
                        for i in range(4):
                            k.mm(pu_[:, i * 64:(i + 1) * 64], Wm[:, i, :], BR[:, i, :], part=True)
                            k.mm(pw_[:, i * 128:(i + 1) * 128], BRk[:, i, :], Wm[:, i, :], part=True)
                        k.cp(u_tok[:, :, 0:64], pu_[:, 0:256].rearrange("p (a b) -> p a b", a=4), pw=True)
                        k.cp(wT[:].rearrange("p a b -> p (a b)"), pw_[:], eng="act")
                        k.tt(kf[:], Einc[:, :, cl[0]], Einc[:, :, cl[1]], ALU.add)
                        k.tt(kend[:], k_tok[:, ti, hcols].rearrange("p (a b) -> p a b", a=4), kf[:].unsqueeze(2).to_broadcast([128, 4, 64]), ALU.mult)
                        for i in range(4):
                            h = hh * 4 + i
                            hp = slice((h % 2) * 64, (h % 2) * 64 + 64)
                            k.tt(qdT[hp, i, :], qT[hp, h // 2, ts_], EG[hp, i, :], ALU.mult, pw=True)
                        for ch in ((0, 1) if dd == 0 else (1, 0)):
                            cr = slice(ch * 64, (ch + 1) * 64)
                            pws, po, pS = pb[3], pb[4], pb[3]
                            for i in (0, 2, 1, 3):
                                h = hh * 4 + i
                                hp = slice((h % 2) * 64, (h % 2) * 64 + 64)
                                k.mm(pws[cr, i * Wd:(i + 1) * Wd], wT[hp, i, cr], Sb[hp, i // 2, :], part=True)
                            k.tt(vnew[cr, :, :], u_tok[cr, :, :], pws[cr, 0:4 * Wd].rearrange("p (a b) -> p a b", a=4), ALU.subtract, pw=True)
                            for i in range(4):
                                h = hh * 4 + i
                                hp = slice((h % 2) * 64, (h % 2) * 64 + 64)
                                reg = po[0:Wd, i * 64:(i + 1) * 64]
                                k.mm(reg, Sb[hp, i // 2, :], qdT[hp, i, cr], start=True, stop=False, part=True)
                                k.mm(reg, vnew[cr, i, :], aqk[cr, i, cr], start=False, stop=True, part=True)
                            for i in range(4):
                                h = hh * 4 + i
                                hp = slice((h % 2) * 64, (h % 2) * 64 + 64)
                                k.mm(pS[hp, i * Wd:(i + 1) * Wd], kend[cr, i, :], vnew[cr, i, :], part=True)
                            for i in range(4):
                                h = hh * 4 + i
                                hp = slice((h % 2) * 64, (h % 2) * 64 + 64)
                                k.stt(S[hp, i // 2, :], S[hp, i // 2, :], EG[hp, i, cl[ch]:cl[ch] + 1], pS[hp, i * Wd:(i + 1) * Wd], ALU.mult, ALU.add, pw=True)
                            k.cp(Sb[:], S[:], eng="act")
                            col = slice(ti * 128 + ch * 64, ti * 128 + (ch + 1) * 64)
                            hsl = slice(hh * 4, hh * 4 + 4)
                            pov = po[0:64, 0:256].rearrange("p (a b) -> p a b", a=4)
                            if dd == 0:
                                k.cp(O1[0:64, hsl, col], pov, pw=True)
                            else:
                                k.tt(O1[0:64, hsl, col], O1[0:64, hsl, col], pov, ALU.add, pw=True)
                            if exch:
                                Ox = O1 if dd == 0 else O2
                                k.cp(Ox[64:128, hsl, col], po[64:128, 0:256].rearrange("p (a b) -> p a b", a=4), pw=True)
                    if not exch:
                        k.dma(d["nsgdn"][seg, l, dd, hh * 4:(hh + 1) * 4].rearrange("(j i) k v -> (i k) j v", i=2), S[:, :, 0:64])
                    else:
                        self.gdn_save_comp(u, l, ph, dd, hh, S)
        k.barrier()
        cm_stack.close()
        sb = sb_ph
        if exch:
            self.gdn_exchange(u, l, ph, O1, O2)
        k.barrier()
        sdz = sb("sdz", [64, 8, T], BF16)
        wz = self.wload(d["win"][l, :, :, C_DZ:C_DZ + 512], 8, 512)
        for h in range(8):
            for b_ in range(u.nblk):
                bs = slice(b_ * 512, (b_ + 1) * 512)
                p = pb[(h * u.nblk + b_) % 4]
                for kk in range(8):
                    k.mm(p[0:64, :], wz[:, kk, h * 64:(h + 1) * 64], xT[:, kk, bs], start=(kk == 0), stop=(kk == 7))
                k.act(sdz[:, h, bs], p[0:64, :], AF.Silu, pw=True)
        sqo = [sb("dsqo%d" % i, [64, 512], F32) for i in range(2)]
        rso = [sb("drso%d" % i, [64, 512], F32) for i in range(2)]
        n_ = 0
        for h in range(8):
            for b_ in range(u.nblk):
                bs = slice(b_ * 512, (b_ + 1) * 512)
                i2 = n_ % 2; n_ += 1
                k.tt(sqo[i2][:], O1[0:64, h, bs], O1[0:64, h, bs], ALU.mult)
                pss = pb[i2]
                k.mm(pss[0:64, :], self.onesf[0:64, 0:64], sqo[i2][:])
                k.act(rso[i2][:], pss[0:64, :], AF.Ln, scale=1.0 / 64, bias=self.epsc[0:64, 0:1])
                k.act(rso[i2][:], rso[i2][:], AF.Exp, scale=-0.5)
                k.tt(sqo[i2][:], O1[0:64, h, bs], rso[i2][:], ALU.mult)
                k.stt(self.ygdn[(h % 2) * 64:(h % 2) * 64 + 64, h // 2, bs], sqo[i2][:], self.gcol[0:64, l, 1:2], sdz[:, h, bs], ALU.mult, ALU.mult, pw=True)

    def gdn_halo(self, u, l, ph, raw):
        k = self.k
        nc = k.nc
        pb = self.pb
        T = u.T
        sb = lambda n, shp, dt: k.sb("%s_%s%d" % (n, u.name, l), shp, dt, ph)
        bnd = sb("dbnd", [128, 12, 2], F32)
        for c3 in range(3):
            w = self.wload(self.d["win"][l, :, :, C_DQKV + c3 * 512:C_DQKV + (c3 + 1) * 512], 8, 512)
            for c4 in range(4):
                c = c3 * 4 + c4
                p = pb[c % 4]
                for kk in range(8):
                    k.mm(p[:, 0:2], w[:, kk, c4 * 128:(c4 + 1) * 128], self.xT[:, kk, 0:T:T - 1], start=(kk == 0), stop=(kk == 7))
                k.cp(bnd[:, c, :], p[:, 0:2], pw=True)
        hin = nc.dram_tensor("dhin%d" % l, [128, 24], F32)
        hout = nc.dram_tensor("dhout%d" % l, [512, 24], F32)
        k.dma(hin[:, :], bnd[:].rearrange("p a b -> p (a b)"))
        self.allgather(hin, hout)
        hall = sb("dhall", [128, 4, 12, 2], F32)
        k.dma(hall[:].rearrange("p r a b -> p r (a b)"), hout[:, :].rearrange("(r p) f -> p r f", p=128))
        hp_ = sb("dhprev", [128, 12], F32); hn_ = sb("dhnext", [128, 12], F32)
        k.memset(hp_[:], 0.0); k.memset(hn_[:], 0.0)
        for j in range(4):
            k.stt(hp_[:], hall[:, j, :, 1], self.rank[:, 8 + j:9 + j], hp_[:], ALU.mult, ALU.add)
            k.stt(hn_[:], hall[:, j, :, 0], self.rank[:, 12 + j:13 + j], hn_[:], ALU.mult, ALU.add)
        return hp_, hn_

    def gdn_save_comp(self, u, l, ph, dd, hh, S):
        self.k.cp(self._gcomp[:, dd, hh, :, :], S[:], pw=True)

    def gdn_exchange(self, u, l, ph, O1, O2):
        k = self.k
        nc = k.nc
        pb = self.pb
        sb = lambda n, shp, dt: k.sb("%s_%s%d" % (n, u.name, l), shp, dt, ph)
        gin = nc.dram_tensor("dgin%d" % l, [128, 1024], F32)
        gout = nc.dram_tensor("dgout%d" % l, [512, 1024], F32)
        k.dma(gin[:, :], self._gcomp[:].rearrange("p a b c d -> p (a b c d)"))
        self.allgather(gin, gout)
        call = sb("dcall", [128, 4, 1024], F32)
        k.dma(call[:], gout[:, :].rearrange("(r p) f -> p r f", p=128))
        S0 = sb("dS0", [128, 2, 4, 64], F32)
        for dd in range(2):
            k.dma(S0[:, dd, :, :], self.d["sgdn"][l, dd].rearrange("(j i) k v -> (i k) j v", i=2), pw=True)
        PhiT = sb("dPhiT", [128, 64], F32); new = sb("dnew", [128, 64], F32); diff = sb("ddiff", [128, 64], F32)
        Sent = sb("dSent", [128, 2, 8, 64], BF16)
        for dd in range(2):
            for jp in range(4):
                cur = S0[:, dd, jp, :]
                off = ((dd * 2 + jp // 2) * 2 + jp % 2) * 128
                for j in ([0, 1, 2] if dd == 0 else [3, 2, 1]):
                    pT, pN = pb[0], pb[1]
                    for i in range(2):
                        hp = slice(i * 64, (i + 1) * 64)
                        k.mm(pT[hp, 0:64], call[hp, j, off + 64:off + 128], self.ident[hp, i * 64:(i + 1) * 64], part=True)
                    k.cp(PhiT[:], pT[:, 0:64])
                    for i in range(2):
                        hp = slice(i * 64, (i + 1) * 64)
                        k.mm(pN[hp, 0:64], PhiT[hp, :], cur[hp, :], part=True)
                    k.tt(new[:], pN[:, 0:64], call[:, j, off:off + 64], ALU.add)
                    k.tt(diff[:], new[:], cur, ALU.subtract)
                    mc = (j if dd == 0 else 4 + j)
                    k.stt(cur, diff[:], self.rank[:, mc:mc + 1], cur, ALU.mult, ALU.add, pw=True)
                for i in range(2):
                    k.cp(Sent[64:128, dd, jp * 2 + i, :], cur[i * 64:(i + 1) * 64, :], pw=True)
        n = 0
        for h in range(8):
            for b_ in range(u.nblk):
                bs = slice(b_ * 512, (b_ + 1) * 512)
                pc_ = pb[2 + n % 4]; n += 1
                k.mm(pc_[0:64, :], Sent[64:128, 0, h, :], O1[64:128, h, bs], start=True, stop=False)
                k.mm(pc_[0:64, :], Sent[64:128, 1, h, :], O2[64:128, h, bs], start=False, stop=True)
                k.tt(O1[0:64, h, bs], O1[0:64, h, bs], pc_[0:64, :], ALU.add, pw=True)

    def merge(self, u, l, ph):
        k = self.k
        d = self.d
        sb = lambda n, shp, dt: k.sb("%s_%s%d" % (n, u.name, l), shp, dt, ph)
        pb = self.pb
        T = u.T
        nbr = min(self.stage, 3)
        ys = [self.ymla, self.ygla, self.ygdn]
        ypre = sb("ypre", [128, 8, T], BF16)
        sg = [sb("sg%d" % i, [128, 512], F32) for i in range(3)]
        pr = [sb("pr%d" % i, [128, 512], F32) for i in range(3)]
        tmp = sb("mtmp", [128, 512], F32)
        for dch in range(8):
            wg = self.wload(d["win"][l, :, :, C_GATES + dch * 384:C_GATES + (dch + 1) * 384], 8, 384)
            wb = self.wload(d["wbr"][l, :, dch].rearrange("p n k c -> p (n k) c"), 12, 128)
            for b in range(u.nblk):
                bs = slice(b * 512, (b + 1) * 512)
                for n in range(nbr):
                    pg = pb[n]; pp = pb[3 + n]
                    for kk in range(8):
                        k.mm(pg[:], wg[:, kk, n * 128:(n + 1) * 128], self.xT[:, kk, bs], start=(kk == 0), stop=(kk == 7))
                    for kc in range(4):
                        k.mm(pp[:], wb[:, n * 4 + kc, :], ys[n][:, kc, bs], start=(kc == 0), stop=(kc == 3))
                    k.act(sg[n][:], pg[:], AF.Sigmoid)
                    k.tt(pr[n][:], pp[:], sg[n][:], ALU.mult)
                if nbr == 1:
                    k.cp(ypre[:, dch, bs], pr[0][:], pw=True)
                elif nbr == 2:
                    k.tt(ypre[:, dch, bs], pr[0][:], pr[1][:], ALU.add, pw=True)
                else:
                    k.tt(tmp[:], pr[0][:], pr[1][:], ALU.add)
                    k.tt(ypre[:, dch, bs], tmp[:], pr[2][:], ALU.add, pw=True)
        self.dump("dbg_ypre", ypre[:], [128, 8, T], BF16)
        self.dump("dbg_grow", self.grow[0][:], [128, D], F32)
        ht = [sb("htmp%d" % i, [128, 512], F32) for i in range(2)]
        for half in range(2):
            hs_ = slice(half * 512, (half + 1) * 512)
            wo = self.wload(d["wout"][l, :, :, hs_], 8, 512)
            for ti in range(u.ntile):
                t = ti
                p = pb[ti % 4]
                for kk in range(8):
                    k.mm(p[:], ypre[:, kk, ti * 128:(ti + 1) * 128], wo[:, kk, :], start=(kk == 0), stop=(kk == 7))
                k.tt(ht[ti % 2][:], p[:], self.grow[0][:, hs_], ALU.mult)
                k.tt(self.h[t][:, hs_], self.h[t][:, hs_], ht[ti % 2][:], ALU.add, pw=True)

    def ffn(self, u, l, ph):
        k = self.k
        d = self.d
        sb = lambda n, shp, dt: k.sb("%s_%s%d" % (n, u.name, l), shp, dt, ph)
        pb = self.pb
        T = u.T
        actT = sb("actT", [128, 22, T], BF16)
        sgt = [sb("sgt%d" % i, [128, 512], F32) for i in range(2)]
        n = 0
        for i2 in range(11):
            w = self.wload(d["fwin"][l, :, :, i2 * 512:(i2 + 1) * 512], 8, 512)
            for ii in range(2):
                i = i2 * 2 + ii
                for b in range(u.nblk):
                    bs = slice(b * 512, (b + 1) * 512)
                    pg = pb[(n % 3) * 2]; pu = pb[(n % 3) * 2 + 1]
                    for kk in range(8):
                        k.mm(pg[:], w[:, kk, ii * 256:ii * 256 + 128], self.xT[:, kk, bs], start=(kk == 0), stop=(kk == 7))
                    for kk in range(8):
                        k.mm(pu[:], w[:, kk, ii * 256 + 128:ii * 256 + 256], self.xT[:, kk, bs], start=(kk == 0), stop=(kk == 7))
                    k.act(sgt[n % 2][:], pg[:], AF.Silu)
                    k.tt(actT[:, i, bs], pu[:], sgt[n % 2][:], ALU.mult, pw=True)
                    n += 1
        self.dump("dbg_xfT", self.xT[:], [128, 8, T], BF16)
        self.dump("dbg_actT", actT[:], [128, 22, T], BF16)
        k.barrier()
        ht = [sb("ftmp%d" % i, [128, 128], F32) for i in range(2)]
        for c8 in range(8):
            cs = slice(c8 * 128, (c8 + 1) * 128)
            w = self.wload(d["fwout"][l, :, c8, :, :], 22, 128)
            for ti in range(u.ntile):
                t = ti
                acc = pb[ti % 4][:, 0:128]
                for kk in range(22):
                    k.mm(acc, actT[:, kk, ti * 128:(ti + 1) * 128], w[:, kk, :], start=(kk == 0), stop=(kk == 21))
                k.tt(ht[ti % 2][:], acc, self.grow[1][:, cs], ALU.mult)
                k.tt(self.h[t][:, cs], self.h[t][:, cs], ht[ti % 2][:], ALU.add, pw=True)


def _kmaj(w):
    K, C = w.shape
    return np.ascontiguousarray(w.reshape(K // 128, 128, C).transpose(1, 0, 2))


def _rope_tables(pos_or_none, n):
    if pos_or_none is None:
        return np.ones((n, 32), np.float32), np.zeros((n, 32), np.float32)
    pos = pos_or_none
    row = (pos // 64).astype(np.float32)
    col = (pos % 64).astype(np.float32)
    inv = (np.float32(10000.0) ** (-np.arange(0, 16, 2, dtype=np.float32) / np.float32(16))).astype(np.float32)
    ar = (row[:, None] * inv).astype(np.float32)
    ac = (col[:, None] * inv).astype(np.float32)
    cr, sr, cc, sc = np.cos(ar), np.sin(ar), np.cos(ac), np.sin(ac)
    cosf = np.concatenate([cr, cr, cc, cc], axis=1).astype(np.float32)
    sins = np.concatenate([-sr, sr, -sc, sc], axis=1).astype(np.float32)
    return cosf, sins


def _prep(inp):
    f = np.float32
    L = DEPTH
    g = lambda n: np.asarray(inp[n], dtype=f)
    idx = np.arange(128)
    same = (idx[:, None] // 64) == (idx[None, :] // 64)
    le = same & (idx[:, None] <= idx[None, :]); ge = same & (idx[:, None] >= idx[None, :])
    lt = same & (idx[:, None] < idx[None, :]); gt = same & (idx[:, None] > idx[None, :])
    NEG = -30000.0
    cst = np.zeros((128, 8, 128), f)
    cst[:, 0] = np.eye(128); cst[:, 1] = le; cst[:, 2] = ge
    cst[:, 3] = np.where(lt, 0, NEG); cst[:, 4] = np.where(le, 0, NEG)
    cst[:, 5] = np.where(gt, 0, NEG); cst[:, 6] = np.where(ge, 0, NEG)
    cst[:, 7] = same
    w_in = g("w_in")
    win = np.zeros((L, 128, 8, NWIN), f)
    for l in range(L):
        w = w_in[l]
        gates = w[:, O_GATES:].reshape(1024, 3, 8, 128).transpose(0, 2, 1, 3).reshape(1024, 3072)
        cols = np.concatenate([
            w[:, O_MQ:O_MQ + 256], w[:, O_GQ:O_GQ + 256], w[:, O_GK:O_GK + 256], w[:, O_GR:O_GR + 512],
            w[:, O_GLR:O_GLR + 16], np.zeros((1024, 16), f), w[:, O_GLR + 16:O_GLR + 32], np.zeros((1024, 16), f), w[:, O_DQKV:O_DQKV + 1536], w[:, O_DZ:O_DZ + 512],
            w[:, O_MKV:O_MKV + 160], w[:, O_DA:O_DA + 32], w[:, O_GK:O_GK + 256], w[:, O_GV:O_GV + 512], gates], axis=1)
        assert cols.shape[1] == NWIN
        win[l] = _kmaj(cols)
    common = {"cst": cst, "win": win}
    common["wmod"] = np.stack([_kmaj(g("w_mod")[l]) for l in range(L)])
    common["bmodc"] = np.ascontiguousarray(g("b_mod").reshape(L, 48, 128).transpose(2, 0, 1))
    ncol = np.zeros((128, L, 2, 8), f)
    ncol[:, :, 0, :] = g("norm_mix").reshape(L, 8, 128).transpose(2, 0, 1)
    ncol[:, :, 1, :] = g("norm_ffn").reshape(L, 8, 128).transpose(2, 0, 1)
    common["ncol"] = ncol
    common["fnorm"] = g("final_norm").reshape(1, D)
    perm = np.concatenate([np.arange(8, 16), np.arange(0, 8), np.arange(24, 32), np.arange(16, 24)])
    wuq = np.zeros((L, 128, 2, 1024), f)
    for l in range(L):
        w = g("mla_w_uq")[l].reshape(256, 8, 96)
        rope = w[:, :, 64:]
        cat = np.concatenate([w[:, :, :64], rope, rope[:, :, perm]], axis=2).reshape(256, 1024)
        wuq[l] = _kmaj(cat)
    common["wuq"] = wuq
    common["qnorm"] = np.ascontiguousarray(g("mla_q_norm").reshape(L, 2, 128).transpose(2, 0, 1))
    common["kvnorm"] = g("mla_kv_norm").reshape(L, 1, 128)
    wukv = g("mla_w_ukv").reshape(L, 128, 8, 128)
    common["wukT"] = np.ascontiguousarray(wukv[:, :, :, :64].transpose(0, 3, 2, 1))
    common["wuv"] = np.ascontiguousarray(wukv[:, :, :, 64:].reshape(L, 128, 512))
    wgate = np.zeros((L, 17, 512), f)
    wgate[:, 0:16, :] = g("gla_w_gate").transpose(0, 2, 1, 3).reshape(L, 16, 512)
    wgate[:, 16, :] = g("gla_b_gate").reshape(L, 512)
    common["wgate"] = wgate
    gcol = np.zeros((128, L, 2), f)
    gcol[:, :, 0] = g("gla_norm").T
    gcol[:, :, 1] = np.concatenate([g("gdn_norm"), g("gdn_norm")], axis=1).T
    common["gcol"] = gcol
    common["convc"] = np.ascontiguousarray(g("gdn_conv").reshape(L, 3, 12, 128).transpose(3, 0, 2, 1))
    common["adt"] = np.concatenate([g("gdn_a_log").reshape(L, 1, 16), g("gdn_dt_bias").reshape(L, 1, 16)], axis=2)
    common["wbr"] = np.ascontiguousarray(g("w_branch").reshape(L, 3, 4, 128, 8, 128).transpose(0, 3, 4, 1, 2, 5))
    common["wout"] = np.stack([_kmaj(g("w_out")[l]) for l in range(L)])
    fwin = np.zeros((L, 128, 8, 5632), f)
    for l in range(L):
        w = g("ffn_w_in")[l]
        cat = np.stack([w[:, :2816].reshape(1024, 22, 128), w[:, 2816:].reshape(1024, 22, 128)], axis=2).reshape(1024, 5632)
        fwin[l] = _kmaj(cat)
    common["fwin"] = fwin
    common["fwout"] = np.stack([np.ascontiguousarray(_kmaj(g("ffn_w_out")[l]).reshape(128, 22, 8, 128).transpose(0, 2, 1, 3)) for l in range(L)])
    xp, xs = g("x_prompt"), g("x_sample")
    cps, sps = _rope_tables(None, 512)
    maps = []
    for c in range(8):
        gb, r = c // 4, c % 4
        m = dict(common)
        m["xin"] = np.ascontiguousarray(np.concatenate([xp[2 * c], xp[2 * c + 1], xs[gb, 1024 * r:1024 * (r + 1)]], axis=0))
        m["cache"] = np.ascontiguousarray(g("cache_mla")[gb])
        m["sgla"] = np.ascontiguousarray(g("state_gla")[gb])
        m["sgdn"] = np.ascontiguousarray(g("state_gdn")[gb])
        condT = np.zeros((128, 8, 2), f)
        condT[:, :, 0] = g("c_ctx").reshape(8, 128).T
        condT[:, :, 1] = g("c")[gb].reshape(8, 128).T
        m["condT"] = condT.reshape(128, 16)
        cs, ss = _rope_tables(np.arange(1024 * r, 1024 * (r + 1)), 1024)
        cosf = np.concatenate([cps, cs], axis=0); sinf = np.concatenate([sps, ss], axis=0)
        m["ropeT"] = np.ascontiguousarray(np.stack([cosf.T, sinf.T], axis=1))
        m["ropetok"] = np.ascontiguousarray(np.stack([cosf.reshape(12, 128, 32), sinf.reshape(12, 128, 32)], axis=0).transpose(2, 0, 1, 3))
        rk = np.zeros((128, 16), f)
        for j in range(4):
            rk[:, j] = 1.0 if j < r else 0.0
            rk[:, 4 + j] = 1.0 if j > r else 0.0
            rk[:, 8 + j] = 1.0 if j == r - 1 else 0.0
            rk[:, 12 + j] = 1.0 if j == r + 1 else 0.0
        m["rankm"] = rk
        maps.append(m)
    return maps


_PROG = {}


def _get_prog(stage=9, units="PS"):
    key = (stage, units)
    if key not in _PROG:
        _PROG[key] = Prog(stage=stage, units=units)
    return _PROG[key]


def _run(inputs, stage=9, units="PS"):
    prog = _get_prog(stage, units)
    maps = _prep(inputs)
    maps = [{n: m[n] for n in prog.din} for m in maps]
    res = run_bass_kernel_spmd(prog.k.nc, maps, core_ids=list(range(8)))
    return res.results


DEFAULT_STAGE = int(os.environ.get("KSTAGE", "3"))


def kernel(**inputs):
    outs = _run(inputs, stage=DEFAULT_STAGE)
    f = np.float32
    y_prompt = np.zeros((16, 256, D), f); y_sample = np.zeros((2, 4096, D), f)
    ncache = np.zeros((16, DEPTH, 256, 160), f)
    nsgla = np.zeros((16, DEPTH, 2, 4, 64, 128), f); nsgdn = np.zeros((16, DEPTH, 2, 8, 64, 64), f)
    for c in range(8):
        o = outs[c]
        gb, r = c // 4, c % 4
        y = np.asarray(o["y"])
        y_prompt[2 * c] = y[0:256]; y_prompt[2 * c + 1] = y[256:512]
        y_sample[gb, 1024 * r:1024 * (r + 1)] = y[512:1536]
        ncache[2 * c:2 * c + 2] = np.asarray(o["ncache"])
        nsgla[2 * c:2 * c + 2] = np.asarray(o["nsgla"])
        nsgdn[2 * c:2 * c + 2] = np.asarray(o["nsgdn"])
    return (y_prompt, y_sample, ncache, nsgla, nsgdn)
```

```python
import os
import numpy as np
import concourse.bass as bass
import concourse.mybir as mybir
from concourse.bass_utils import run_bass_kernel_spmd
from contextlib import ExitStack

F32 = mybir.dt.float32
BF16 = mybir.dt.bfloat16
AF = mybir.ActivationFunctionType
ALU = mybir.AluOpType
AX = mybir.AxisListType

D = 1024
DEPTH = 2
EPS = 1e-6
NDMA_SEMS = 40
COMPUTE = ("pe", "act", "dve", "pool")

O_MQ, O_MKV, O_GQ, O_GK, O_GV, O_GR, O_GLR, O_DQKV, O_DZ, O_DA, O_DB, O_GATES = (
    0, 256, 416, 672, 928, 1440, 1952, 1984, 3520, 4032, 4048, 4064)
C_MQ = 0
C_GQ = 256
C_GKF = 512
C_GR = 768
C_GLR = 1280
C_DQKV = 1344
C_DZ = 2880
C_MKV = 3392
C_DAB = 3552
C_GKT = 3584
C_GV = 3840
C_GATES = 4352
NWIN = 7424


class _Op:
    __slots__ = ("eng", "fn", "waits", "dma", "idx", "milestone", "slot", "target", "guard", "cc")


class KB:
    def __init__(self):
        self.nc = bass.Bass("TRN2", target_bir_lowering=False)
        self.es = ExitStack()
        self.ops = []
        self.cnt = {e: 0 for e in ("pe", "act", "dve", "pool", "sp")}
        self.res = {}
        self.known = {e: {} for e in self.cnt}
        self.dma_n = 0
        self.slot_last = {}
        self.slot_cnt = {}

    def sb(self, name, shape, dt, stack=None):
        self.uid = getattr(self, "uid", 0) + 1
        return (stack or self.es).enter_context(self.nc.sbuf_tensor("%s_%d" % (name, self.uid), list(shape), dt))

    def ps(self, name, shape, dt, stack=None):
        return (stack or self.es).enter_context(self.nc.psum_tensor(name, list(shape), dt))

    def op(self, eng, fn, reads=(), writes=(), pwrites=(), dma=False, cc=False, force=()):
        o = _Op()
        o.cc = cc
        if cc:
            dma = True
        o.eng = eng; o.fn = fn; o.dma = dma; o.milestone = False
        o.idx = self.cnt[eng]; self.cnt[eng] += 1
        deps = []
        for k in reads:
            st = self.res.get(k)
            if st:
                deps += st["w"]
        for k in writes:
            st = self.res.get(k)
            if st:
                deps += st["r"]; deps += st["w"]
        for k in pwrites:
            st = self.res.get(k)
            if st:
                deps += st["r"]; deps += [d for d in st["w"] if not d[1]]
        kn = self.known[eng]
        need = {}
        for d in deps:
            dop = d[0]
            if dop.dma:
                key = ("dma", dop.slot)
                if key not in need or need[key].target < dop.target:
                    need[key] = dop
            else:
                if dop.eng == eng and eng in ("pe", "sp"):
                    continue
                key = dop.eng
                if key not in need or need[key].idx < dop.idx:
                    need[key] = dop
        waits = []
        for fo in force:
            if kn.get(fo.eng, -1) < fo.idx:
                kn[fo.eng] = fo.idx
                fo.milestone = True
                waits.append(fo)
        for key, dop in need.items():
            if dop.dma:
                if kn.get(key, -1) >= dop.target:
                    continue
                kn[key] = dop.target
            else:
                if kn.get(key, -1) >= dop.idx:
                    continue
                kn[key] = dop.idx
                dop.milestone = True
            waits.append(dop)
        o.guard = None
        if cc:
            self.cc_n = getattr(self, "cc_n", 0) + 1
            o.slot = 2000 + self.cc_n
            o.target = 1
            self.slot_last[o.slot] = o
        elif dma and eng == "pool" and os.environ.get("KSIM") == "1":
            self.pool_n = getattr(self, "pool_n", 0) + 1
            o.slot = 1000 + self.pool_n
            o.target = 16
            self.slot_last[o.slot] = o
        elif dma:
            o.slot = self.dma_n % NDMA_SEMS
            self.dma_n += 1
            self.slot_cnt[o.slot] = self.slot_cnt.get(o.slot, 0) + 16
            o.target = self.slot_cnt[o.slot]
            prev = self.slot_last.get(o.slot)
            if prev is not None:
                key = ("dma", o.slot)
                if kn.get(key, -1) < prev.target:
                    kn[key] = prev.target
                    o.guard = prev
            self.slot_last[o.slot] = o
        o.waits = waits
        self.ops.append(o)
        for k in reads:
            st = self.res.setdefault(k, {"w": [], "r": []})
            if dma:
                st["r"].append((o, False))
            else:
                st["r"] = [d for d in st["r"] if d[0].dma or d[0].eng != eng] + [(o, False)]
        for k in writes:
            self.res[k] = {"w": [(o, False)], "r": []}
        for k in pwrites:
            st = self.res.setdefault(k, {"w": [], "r": []})
            if dma:
                st["w"].append((o, True))
            else:
                st["w"] = [d for d in st["w"] if d[0].dma or d[0].eng != eng or not d[1]] + [(o, True)]
        return o

    def barrier(self):
        last = {}
        for o in self.ops:
            if o.fn is not None and not o.dma:
                last[o.eng] = o
        dmas = list(self.slot_last.values())
        for e in ("pe", "act", "dve", "pool", "sp"):
            b = _Op(); b.eng = e; b.fn = None; b.dma = False; b.milestone = False; b.idx = None; b.guard = None; b.cc = False
            waits = []
            kn = self.known[e]
            for x, lo in last.items():
                if x == e:
                    continue
                if kn.get(x, -1) >= lo.idx:
                    continue
                kn[x] = lo.idx; lo.milestone = True; waits.append(lo)
            for d in dmas:
                key = ("dma", d.slot)
                if kn.get(key, -1) >= d.target:
                    continue
                kn[key] = d.target; waits.append(d)
            b.waits = waits
            self.ops.append(b)
        self.res = {}

    def emit(self):
        nc = self.nc
        engs = {"pe": nc.tensor, "act": nc.scalar, "dve": nc.vector, "pool": nc.gpsimd, "sp": nc.sync}
        sems = {e: self.es.enter_context(nc.semaphore("sem_" + e)) for e in COMPUTE}
        dsem = {i: self.es.enter_context(nc.semaphore("dsem%d" % i)) for i in range(NDMA_SEMS)}
        for i in range(getattr(self, "pool_n", 0)):
            dsem[1001 + i] = self.es.enter_context(nc.semaphore("psem%d" % i))
        for i in range(getattr(self, "cc_n", 0)):
            dsem[2001 + i] = self.es.enter_context(nc.semaphore("ccsem%d" % i))
        mc = {e: 0 for e in COMPUTE}
        for o in self.ops:
            if o.fn is not None and not o.dma and o.milestone:
                mc[o.eng] += 1
                o.target = mc[o.eng]
        nw = 0
        for o in self.ops:
            e = engs[o.eng]
            for d in o.waits:
                if d.dma:
                    e.wait_ge(dsem[d.slot], d.target)
                else:
                    e.wait_ge(sems[d.eng], d.target)
                nw += 1
            if o.fn is None:
                continue
            if o.dma:
                if o.guard is not None:
                    e.wait_ge(dsem[o.slot], o.guard.target); nw += 1
                ins = o.fn(e)
                if o.cc:
                    ins.then_inc(dsem[o.slot])
                elif ins is not None:
                    ins.then_inc(dsem[o.slot], 16)
            else:
                ins = o.fn(e)
                if o.milestone:
                    ins.then_inc(sems[o.eng], 1)
        self.stats = dict(n_ops=len(self.ops), n_waits=nw, milestones=mc)
        return nc

    @staticmethod
    def _n(ap):
        return ap.name

    def mm(self, out, lhsT, rhs, start=True, stop=True, tr=False, part=False):
        rk = [lhsT.name, rhs.name]
        kw = dict(writes=[out.name]) if (start and not part) else dict(pwrites=[out.name])
        r0 = lhsT.base_partition(); r1 = r0 + lhsT.partition_size()
        c0 = out.base_partition(); c1 = c0 + out.partition_size()
        force = []
        prev = getattr(self, "_pmm", None)
        if prev is not None:
            (pr0, pr1, pc0, pc1, pop) = prev
            if r1 <= pr0 or pr1 <= r0:
                force = [pop]
        if tr:
            o = self.op("pe", lambda e: e.transpose(out=out, in_=lhsT, identity=rhs), reads=rk, force=force, **kw)
        else:
            o = self.op("pe", lambda e: e.matmul(out, lhsT=lhsT, rhs=rhs, start=start, stop=stop), reads=rk, force=force, **kw)
        self._pmm = (r0, r1, c0, c1, o)
        return o

    def act(self, out, in_, func, scale=None, bias=None, accum=None, pw=False, eng="act"):
        rk = [in_.name]
        kw = {}
        if scale is not None:
            kw["scale"] = scale
            if not isinstance(scale, (int, float)):
                rk.append(scale.name)
        if bias is not None:
            kw["bias"] = bias
            if not isinstance(bias, (int, float)):
                rk.append(bias.name)
        wk = [out.name]
        if accum is not None:
            kw["accum_out"] = accum
            wk.append(accum.name)
        wkw = dict(pwrites=wk) if pw else dict(writes=wk)
        return self.op(eng, lambda e: e.activation(out=out, in_=in_, func=func, **kw), reads=rk, **wkw)

    def tt(self, out, in0, in1, op, pw=False, eng="dve"):
        wkw = dict(pwrites=[out.name]) if pw else dict(writes=[out.name])
        return self.op(eng, lambda e: e.tensor_tensor(out=out, in0=in0, in1=in1, op=op), reads=[in0.name, in1.name], **wkw)

    def ts(self, out, in0, s1, s2, op0, op1=None, pw=False, eng="dve"):
        rk = [in0.name]
        for s_ in (s1, s2):
            if s_ is not None and not isinstance(s_, (int, float)):
                rk.append(s_.name)
        wkw = dict(pwrites=[out.name]) if pw else dict(writes=[out.name])
        if op1 is None:
            return self.op(eng, lambda e: e.tensor_scalar(out=out, in0=in0, scalar1=s1, scalar2=None, op0=op0), reads=rk, **wkw)
        return self.op(eng, lambda e: e.tensor_scalar(out=out, in0=in0, scalar1=s1, scalar2=s2, op0=op0, op1=op1), reads=rk, **wkw)

    def stt(self, out, in0, scalar, in1, op0, op1, pw=False):
        rk = [in0.name, in1.name]
        if not isinstance(scalar, (int, float)):
            rk.append(scalar.name)
        wkw = dict(pwrites=[out.name]) if pw else dict(writes=[out.name])
        return self.op("dve", lambda e: e.scalar_tensor_tensor(out=out, in0=in0, scalar=scalar, in1=in1, op0=op0, op1=op1), reads=rk, **wkw)

    def cp(self, out, in_, pw=False, eng="dve"):
        wkw = dict(pwrites=[out.name]) if pw else dict(writes=[out.name])
        if eng == "act":
            return self.op("act", lambda e: e.copy(out=out, in_=in_), reads=[in_.name], **wkw)
        return self.op(eng, lambda e: e.tensor_copy(out=out, in_=in_), reads=[in_.name], **wkw)

    def recip(self, out, in_, pw=False):
        wkw = dict(pwrites=[out.name]) if pw else dict(writes=[out.name])
        return self.op("dve", lambda e: e.reciprocal(out=out, in_=in_), reads=[in_.name], **wkw)

    def memset(self, ap, val, eng="dve", pw=False):
        wkw = dict(pwrites=[ap.name]) if pw else dict(writes=[ap.name])
        return self.op(eng, lambda e: e.memset(ap, val), **wkw)

    def dma(self, out, in_, q="sp", pw=False):
        wkw = dict(pwrites=[out.name]) if pw else dict(writes=[out.name])
        return self.op(q, lambda e: e.dma_start(out=out, in_=in_), reads=[in_.name], dma=True, **wkw)


class Unit:
    def __init__(self, name, tile0, ntile, segs, cond, sample):
        self.name = name; self.tile0 = tile0; self.ntile = ntile; self.T = ntile * 128
        self.segs = segs; self.cond = cond; self.sample = sample
        self.nblk = self.T // 512
        if sample:
            self.qblocks = [(b * 512, 512, list(range(34))) for b in range(self.nblk)]
            self.nkt = 34
        else:
            self.qblocks = [(s0, n, list(range(s0 // 128, (s0 + n) // 128))) for (s0, n) in segs]
            self.nkt = ntile


UNIT_P = Unit("P", 0, 4, [(0, 256), (256, 256)], 0, False)
UNIT_S = Unit("S", 4, 8, [(0, 1024)], 1, True)
ATT_SCALE = float(96 ** -0.5)


class Prog:
    def __init__(self, stage=9, units="PS", dbg=()):
        self.k = KB()
        self.stage = stage
        self.units = units
        self.dbg = dbg
        self.din = {}
        self.dout = {}
        self.wi = 0
        self.build()

    def inp(self, name, shape, dt=F32):
        t = self.k.nc.dram_tensor(name, list(shape), dt, kind="ExternalInput")
        self.din[name] = t
        return t

    def outp(self, name, shape, dt=F32):
        t = self.k.nc.dram_tensor(name, list(shape), dt, kind="ExternalOutput")
        self.dout[name] = t
        return t

    def wload(self, src, nk, ncol, parts=128):
        slot = self.wslots[self.wi % len(self.wslots)]
        self.wi += 1
        view = slot[0:parts, 0:nk * ncol].rearrange("p (k c) -> p k c", k=nk)
        self.k.dma(view, src, q="pool")
        return view

    def evac(self, out, in_, pw=True):
        self.ev = getattr(self, "ev", 0) + 1
        self.k.cp(out, in_, pw=pw, eng=("act" if self.ev % 2 else "dve"))

    def rstd_from_ss(self, st, n):
        k = self.k
        k.act(st[:, 1:2], st[:, 0:1], AF.Ln, scale=1.0 / n, bias=self.epsc[:, 0:1])
        k.act(st[:, 2:3], st[:, 1:2], AF.Exp, scale=-0.5)

    def build(self):
        k = self.k
        nc = k.nc
        inp, outp = self.inp, self.outp
        d_x = inp("xin", [1536, D])
        d_cache = inp("cache", [DEPTH, 256, 160])
        d_sgla = inp("sgla", [DEPTH, 2, 4, 64, 128])
        d_sgdn = inp("sgdn", [DEPTH, 2, 8, 64, 64])
        d_cond = inp("condT", [128, 16])
        d_cst = inp("cst", [128, 8, 128])
        d_rope = inp("ropeT", [32, 2, 1536])
        d_ropetok = inp("ropetok", [128, 2, 12, 32])
        d_rank = inp("rankm", [128, 16])
        d_wmod = inp("wmod", [DEPTH, 128, 8, 6144])
        d_bmod = inp("bmodc", [128, DEPTH, 48])
        d_win = inp("win", [DEPTH, 128, 8, NWIN])
        d_ncol = inp("ncol", [128, DEPTH, 2, 8])
        d_fnorm = inp("fnorm", [1, D])
        d_wuq = inp("wuq", [DEPTH, 128, 2, 1024])
        d_qnorm = inp("qnorm", [128, DEPTH, 2])
        d_kvnorm = inp("kvnorm", [DEPTH, 1, 128])
        d_wukT = inp("wukT", [DEPTH, 64, 8, 128])
        d_wuv = inp("wuv", [DEPTH, 128, 512])
        d_wgate = inp("wgate", [DEPTH, 17, 512])
        d_gcol = inp("gcol", [128, DEPTH, 2])
        d_conv = inp("convc", [128, DEPTH, 12, 3])
        d_adt = inp("adt", [DEPTH, 1, 32])
        d_wbr = inp("wbr", [DEPTH, 128, 8, 3, 4, 128])
        d_wout = inp("wout", [DEPTH, 128, 8, D])
        d_fwin = inp("fwin", [DEPTH, 128, 8, 5632])
        d_fwout = inp("fwout", [DEPTH, 128, 8, 22, 128])
        d_y = outp("y", [1536, D])
        d_ncache = outp("ncache", [2, DEPTH, 256, 160])
        d_nsgla = outp("nsgla", [2, DEPTH, 2, 4, 64, 128])
        d_nsgdn = outp("nsgdn", [2, DEPTH, 2, 8, 64, 64])
        self.d = dict(x=d_x, cache=d_cache, sgla=d_sgla, sgdn=d_sgdn, win=d_win, wuq=d_wuq, kvnorm=d_kvnorm,
                      wukT=d_wukT, wuv=d_wuv, wgate=d_wgate, adt=d_adt, wbr=d_wbr, wout=d_wout, fwin=d_fwin,
                      fwout=d_fwout, y=d_y, ncache=d_ncache, nsgla=d_nsgla, nsgdn=d_nsgdn, fnorm=d_fnorm)

        sb = k.sb
        self.wslots = [sb("ws%d" % i, [128, 4096], BF16) for i in range(3)]
        cst = sb("cst_sb", [128, 8, 128], F32)
        k.dma(cst[:], d_cst[:, :, :])
        self.ident = cst[:, 0, :]
        self.triF = cst[:, 1, :]; self.triB = cst[:, 2, :]
        self.negFs = cst[:, 3, :]; self.negFi = cst[:, 4, :]; self.negBs = cst[:, 5, :]; self.negBi = cst[:, 6, :]
        self.bones = cst[:, 7, :]
        self.cst = cst
        self.identb = sb("identb", [128, 128], BF16)
        k.cp(self.identb[:], self.ident)
        self.onesf = sb("onesf", [128, 128], F32); k.memset(self.onesf[:], 1.0)
        self.onesb = sb("onesb", [128, 128], BF16); k.memset(self.onesb[:], 1.0)
        self.epsc = sb("epsc", [128, 1], F32); k.memset(self.epsc[:], EPS)
        self.d_rope = d_rope; self.d_ropetok = d_ropetok
        self.rank = sb("rank_sb", [128, 16], F32); k.dma(self.rank[:], d_rank[:, :])
        self.ncol = sb("ncol_sb", [128, DEPTH, 2, 8], F32); k.dma(self.ncol[:], d_ncol[:, :, :, :])
        self.qnorm = sb("qnorm_sb", [128, DEPTH, 2], F32); k.dma(self.qnorm[:], d_qnorm[:, :, :])
        self.gcol = sb("gcol_sb", [128, DEPTH, 2], F32); k.dma(self.gcol[:], d_gcol[:, :, :])
        self.convc = sb("convc_sb", [128, DEPTH, 12, 3], F32); k.dma(self.convc[:], d_conv[:, :, :, :])
        self.frow = sb("frow", [128, D], F32); k.dma(self.frow[:], d_fnorm[0:1, :].partition_broadcast(128))
        self.modc = sb("modc", [128, DEPTH, 48, 2], F32)
        self.d_hsp = nc.dram_tensor("hspill", [1024, D], F32)
        self.Ga = sb("Ga", [128, 8], F32); self.Gf = sb("Gf", [128, 8], F32)
        self.grow = [sb("grow%d" % i, [128, D], BF16) for i in range(2)]
        self.st = [sb("st%d" % i, [128, 4], F32) for i in range(4)]
        self.sti = 0
        self.junk = sb("junk", [128, D], BF16)
        self.pb = [k.ps("pb%d" % i, [128, 512], F32) for i in range(7)]
        self.pbb = k.ps("pbb", [128, 1024], BF16)

        condT = sb("condT_sb", [128, 16], F32)
        k.dma(condT[:], d_cond[:, :])
        scT = sb("scT", [128, 8, 2], BF16)
        k.act(scT[:].rearrange("p k c -> p (k c)"), condT[:], AF.Silu)
        bmc = sb("bmc", [128, DEPTH, 48], F32)
        k.dma(bmc[:], d_bmod[:, :, :])
        for l in range(DEPTH):
            pbk = self.pb[l]
            for g in range(12):
                w = self.wload(d_wmod[l, :, :, g * 512:(g + 1) * 512], 8, 512)
                for c4 in range(4):
                    j = g * 4 + c4
                    for kk in range(8):
                        k.mm(pbk[:, j * 2:(j + 1) * 2], w[:, kk, c4 * 128:(c4 + 1) * 128], scT[:, kk, :],
                             start=(kk == 0), stop=(kk == 7), part=True)
            k.tt(self.modc[:, l, :, :], pbk[:, 0:96].rearrange("p (j c) -> p j c", c=2),
                 bmc[:, l, :].unsqueeze(2).to_broadcast([128, 48, 2]), ALU.add)
        k.barrier()

        try:
            self.ckpt(0)
            for u in (UNIT_P, UNIT_S):
                if u.name not in self.units:
                    continue
                self.run_unit(u)
        except StopIteration:
            pass
        k.barrier()
        k.emit()

    def ckpt(self, n):
        import os
        self.k.barrier()
        if float(os.environ.get("KSTOP", "999")) <= n:
            raise StopIteration

    def dump(self, name, ap, shape, dt):
        if os.environ.get("KDBG") != "1" or name in self.dout:
            return
        t = self.outp(name, shape, dt)
        self.k.dma(t.ap() if hasattr(t, "ap") else t, ap)

    def nst(self):
        self.sti += 1
        return self.st[self.sti % len(self.st)]

    def run_unit(self, u):
        k = self.k
        d = self.d
        g0 = u.tile0 * 128
        with ExitStack() as ust:
            self.xT = k.sb("xT_" + u.name, [128, 8, u.T], BF16, ust)
            self.ymla = k.sb("ymla_" + u.name, [128, 4, u.T], BF16, ust)
            self.ygla = k.sb("ygla_" + u.name, [128, 4, u.T], BF16, ust)
            self.ygdn = k.sb("ygdn_" + u.name, [128, 4, u.T], BF16, ust)
            self.rope = k.sb("rope_" + u.name, [32, 2, u.T], F32, ust)
            k.dma(self.rope[:], self.d_rope[:, :, g0:g0 + u.T])
            self.ropetok = k.sb("ropetok_" + u.name, [128, 2, u.ntile, 32], F32, ust)
            k.dma(self.ropetok[:], self.d_ropetok[:, :, u.tile0:u.tile0 + u.ntile, :])
            hst = ExitStack()
            self.h = [k.sb("h%d_%s" % (t, u.name), [128, D], F32, hst) for t in range(u.ntile)]
            for ti in range(u.ntile):
                k.dma(self.h[ti][:], d["x"][g0 + ti * 128:g0 + (ti + 1) * 128, :])
            for l in range(DEPTH):
                self.layer_mod(u, l)
                self.ckpt(1)
                self.norm_T(u, l, self.Ga, 0)
                for ti in range(u.ntile):
                    k.dma(self.d_hsp[ti * 128:(ti + 1) * 128, :], self.h[ti][:])
                self.ckpt(2)
                hst.close()
                with ExitStack() as ph:
                    self.mla(u, l, ph)
                    self.ckpt(3)
                if self.stage >= 2:
                    with ExitStack() as ph:
                        self.gla(u, l, ph)
                        self.ckpt(4)
                if self.stage >= 3:
                    with ExitStack() as ph:
                        self.gdn(u, l, ph)
                        self.ckpt(5)
                hst = ExitStack()
                self.h = [k.sb("h%d_%s" % (t, u.name), [128, D], F32, hst) for t in range(u.ntile)]
                for ti in range(u.ntile):
                    k.dma(self.h[ti][:], self.d_hsp[ti * 128:(ti + 1) * 128, :])
                with ExitStack() as ph:
                    self.merge(u, l, ph)
                    self.ckpt(6)
                self.norm_T(u, l, self.Gf, 24)
                with ExitStack() as ph:
                    self.ffn(u, l, ph)
                    self.ckpt(7)
            with ExitStack() as ph:
                yo = [k.sb("yo%d_%s" % (i, u.name), [128, D], F32, ph) for i in range(2)]
                for ti in range(u.ntile):
                    st = self.nst()
                    k.act(self.junk[:], self.h[ti][:], AF.Square, accum=st[:, 0:1])
                    self.rstd_from_ss(st, D)
                    k.stt(yo[ti % 2][:], self.h[ti][:], st[:, 2:3], self.frow[:], ALU.mult, ALU.mult)
                    k.dma(d["y"][g0 + ti * 128:g0 + (ti + 1) * 128, :], yo[ti % 2][:])
                k.barrier()
            hst.close()

    def layer_mod(self, u, l):
        k = self.k
        c = u.cond
        k.stt(self.Ga[:], self.modc[:, l, 8:16, c], 1.0, self.ncol[:, l, 0, :], ALU.add, ALU.mult)
        k.stt(self.Gf[:], self.modc[:, l, 32:40, c], 1.0, self.ncol[:, l, 1, :], ALU.add, ALU.mult)
        with ExitStack() as ph:
            bcf = [k.sb("bcf%d" % i, [128, 128], F32, ph) for i in range(2)]
            n = 0
            for gi, v in enumerate((16, 40)):
                for j in range(8):
                    b = bcf[n % 2]
                    k.cp(b[:], self.modc[:, l, v + j, c:c + 1].to_broadcast([128, 128]))
                    pb = self.pb[n % 4]
                    k.mm(pb[:, 0:128], b[:], self.ident, tr=True)
                    self.evac(self.grow[gi][:, j * 128:(j + 1) * 128], pb[:, 0:128])
                    n += 1
            k.barrier()

    def norm_T(self, u, l, G, shbase):
        k = self.k
        c = u.cond
        with ExitStack() as ph:
            hs = k.sb("hs_" + u.name, [128, 4, D], BF16, ph)
            for b in range(u.nblk):
                for ti in range(4):
                    t = b * 4 + ti
                    st = self.nst()
                    k.act(self.junk[:], self.h[t][:], AF.Square, accum=st[:, 0:1])
                    self.rstd_from_ss(st, D)
                    k.act(hs[:, ti, :], self.h[t][:], AF.Copy, scale=st[:, 2:3], pw=True)
                for j in range(8):
                    half = (j % 2) * 512
                    for ti in range(4):
                        k.mm(self.pbb[:, half + ti * 128:half + (ti + 1) * 128], hs[:, ti, j * 128:(j + 1) * 128],
                             self.identb[:], tr=True, part=True)
                    k.act(self.xT[:, j, b * 512:(b + 1) * 512], self.pbb[:, half:half + 512], AF.Identity,
                          scale=G[:, j:j + 1], bias=self.modc[:, l, shbase + j, c:c + 1], pw=True)
            k.barrier()

    def mla(self, u, l, ph):
        k = self.k
        d = self.d
        sb = lambda n, shp, dt: k.sb("%s_%s%d" % (n, u.name, l), shp, dt, ph)
        T = u.T
        pb = self.pb
        wuq = self.wload(d["wuq"][l, :, :, :], 2, 1024)
        wuqg = sb("wuqg", [128, 2, 1024], BF16)
        for kk in range(2):
            k.ts(wuqg[:, kk, :], wuq[:, kk, :], self.qnorm[:, l, kk:kk + 1], None, ALU.mult, pw=True)
        wukT = sb("wukT", [64, 8, 128], BF16)
        k.dma(wukT[:], d["wukT"][l, :, :, :], q="pool")
        wuv = sb("wuv", [128, 512], BF16)
        k.dma(wuv[:], d["wuv"][l, :, :], q="pool")
        gkv = sb("gkv", [128, 128], F32)
        k.dma(gkv[:], d["kvnorm"][l, 0:1, :].partition_broadcast(128))
        wkv = self.wload(d["win"][l, :, :, C_MKV:C_MKV + 160], 8, 160)
        own = sb("own", [128, u.ntile, 160], F32)
        nkc = u.nkt * 128
        Kl = sb("Kl", [128, nkc], BF16)
        Kr = sb("Kr", [32, nkc], BF16)
        r1 = sb("r1", [128, 32], F32); r2 = sb("r2", [128, 32], F32)
        if u.sample:
            KlO = sb("KlO", [128, T], BF16); KrO = sb("KrO", [32, T], BF16)
            kdst, rdst, kofs = KlO, KrO, 0
        else:
            kdst, rdst, kofs = Kl, Kr, 0
        self.ckpt(2.1)
        def kvproj(ti_):
            p_ = pb[ti_ % 2]
            for kk in range(8):
                k.mm(p_[:, 0:160], self.xT[:, kk, ti_ * 128:(ti_ + 1) * 128], wkv[:, kk, :], start=(kk == 0), stop=(kk == 7))

        kvproj(0)
        for ti in range(u.ntile):
            t = ti
            p = pb[ti % 2]
            if ti + 1 < u.ntile:
                kvproj(ti + 1)
            st = self.nst()
            k.act(self.junk[:, 0:128], p[:, 0:128], AF.Square, accum=st[:, 0:1])
            self.rstd_from_ss(st, 128)
            k.stt(own[:, ti, 0:128], p[:, 0:128], st[:, 2:3], gkv[:], ALU.mult, ALU.mult, pw=True)
            k.tt(r1[:], p[:, 128:160], self.ropetok[:, 0, t, :], ALU.mult)
            for (a, b) in ((0, 8), (8, 0), (16, 24), (24, 16)):
                k.tt(r2[:, a:a + 8], p[:, 128 + b:128 + b + 8], self.ropetok[:, 1, t, a:a + 8], ALU.mult, pw=True)
            k.tt(own[:, ti, 128:160], r1[:], r2[:], ALU.add, pw=True)
            if not u.sample:
                k.dma(d["ncache"][ti // 2, l, (ti % 2) * 128:(ti % 2 + 1) * 128, :], own[:, ti, :])
            import os
            if os.environ.get("KSKIP") == "tr":
                continue
            pt = pb[2 + ti % 2]
            k.mm(pt[:, 0:128], own[:, ti, 0:128], self.ident)
            k.cp(kdst[:, kofs + ti * 128:kofs + (ti + 1) * 128], pt[:, 0:128], pw=True, eng="act")
            if os.environ.get("KSKIP") == "tr2":
                continue
            k.mm(pt[:, 128:256], own[:, ti, 32:160], self.ident, part=True)
            k.cp(rdst[:, kofs + ti * 128:kofs + (ti + 1) * 128], pt[96:128, 128:256], pw=True)
        if u.sample:
            self.kv_exchange(u, l, ph, KlO, KrO, Kl, Kr)
        self.ckpt(2.2)
        V = sb("V", [128, u.nkt, 512], BF16)
        for kt in range(u.nkt):
            p = pb[kt % 4]
            k.mm(p[:], Kl[:, kt * 128:(kt + 1) * 128], wuv[:])
            self.evac(V[:, kt, :], p[:])
        self.ckpt(2.3)
        wq = self.wload(d["win"][l, :, :, C_MQ:C_MQ + 256], 8, 256)
        mqT = sb("mqT", [128, 2, T], BF16)
        qnT = sb("qnT", [128, 2, T], BF16)
        sq = sb("sq", [128, 2, 512], F32)
        rq = sb("rq", [128, 512], F32)
        for b in range(u.nblk):
            bs = slice(b * 512, (b + 1) * 512)
            for c in range(2):
                p = pb[c]
                for kk in range(8):
                    k.mm(p[:], wq[:, kk, c * 128:(c + 1) * 128], self.xT[:, kk, bs], start=(kk == 0), stop=(kk == 7))
                k.cp(mqT[:, c, bs], p[:], pw=True)
                if os.environ.get("KSKIP") == "c1":
                    continue
                k.act(sq[:, c, :], mqT[:, c, bs], AF.Square, pw=True)
            if os.environ.get("KSKIP") in ("c1", "c2"):
                continue
            pss = pb[2]
            for c in range(2):
                k.mm(pss[:], self.onesf[:], sq[:, c, :], start=(c == 0), stop=(c == 1))
            k.act(rq[:], pss[:], AF.Ln, scale=1.0 / 256, bias=self.epsc[:, 0:1])
            k.act(rq[:], rq[:], AF.Exp, scale=-0.5)
            if os.environ.get("KSKIP") == "c3":
                continue
            for c in range(2):
                k.tt(qnT[:, c, bs], mqT[:, c, bs], rq[:], ALU.mult, pw=True)
        self.ckpt(2.4)
        qn_ = [sb("qn%d" % i, [64, 512], BF16) for i in range(2)]
        qa = [sb("qa%d" % i, [128, 512], BF16) for i in range(2)]
        qr = [sb("qr%d" % i, [32, 512], BF16) for i in range(2)]
        t1 = sb("t1", [32, 512], F32); t2 = sb("t2", [32, 512], F32)
        PT = [sb("PT%d" % i, [128, 512], BF16) for i in range(3)]
        rl = sb("rl", [64, 512], F32)
        g0 = 0
        it = 0
        for h in range(8):
            for (q0, nq, kts) in u.qblocks:
                i2 = it % 2; it += 1
                qs = slice(q0, q0 + nq)
                p1 = pb[5]
                for c in range(2):
                    k.mm(p1[0:64, 0:nq], wuqg[:, c, h * 128:h * 128 + 64], qnT[:, c, qs], start=(c == 0), stop=(c == 1))
                k.cp(qn_[i2][:, 0:nq], p1[0:64, 0:nq], eng="act")
                p2 = pb[6]
                k.mm(p2[:, 0:nq], wukT[:, h, :], qn_[i2][:, 0:nq])
                k.cp(qa[i2][:, 0:nq], p2[:, 0:nq])
                p3 = pb[5]; p4 = pb[6]
                for c in range(2):
                    k.mm(p3[0:32, 0:nq], wuqg[:, c, h * 128 + 64:h * 128 + 96], qnT[:, c, qs], start=(c == 0), stop=(c == 1))
                for c in range(2):
                    k.mm(p4[0:32, 0:nq], wuqg[:, c, h * 128 + 96:h * 128 + 128], qnT[:, c, qs], start=(c == 0), stop=(c == 1))
                k.tt(t1[:, 0:nq], p3[0:32, 0:nq], self.rope[:, 0, g0 + q0:g0 + q0 + nq], ALU.mult)
                k.tt(t2[:, 0:nq], p4[0:32, 0:nq], self.rope[:, 1, g0 + q0:g0 + q0 + nq], ALU.mult)
                k.tt(qr[i2][:, 0:nq], t1[:, 0:nq], t2[:, 0:nq], ALU.add)
                po = pb[0]; pl = pb[1]
                nk = len(kts)
                def score(i):
                    kt_ = kts[i]
                    psc_ = pb[2 + i % 3]
                    k.mm(psc_[:, 0:nq], Kl[:, kt_ * 128:(kt_ + 1) * 128], qa[i2][:, 0:nq], start=True, stop=False)
                    k.mm(psc_[:, 0:nq], Kr[:, kt_ * 128:(kt_ + 1) * 128], qr[i2][:, 0:nq], start=False, stop=True)

                score(0)
                for i, kt in enumerate(kts):
                    if i + 1 < nk:
                        score(i + 1)
                    psc = pb[2 + i % 3]
                    P = PT[i % 3]
                    k.act(P[:, 0:nq], psc[:, 0:nq], AF.Exp, scale=ATT_SCALE)
                    k.mm(po[0:64, 0:nq], V[:, kt, h * 64:(h + 1) * 64], P[:, 0:nq], start=(i == 0), stop=(i == nk - 1))
                    k.mm(pl[0:64, 0:nq], self.onesb[:, 0:64], P[:, 0:nq], start=(i == 0), stop=(i == nk - 1))
                k.recip(rl[:, 0:nq], pl[0:64, 0:nq])
                k.tt(self.ymla[(h % 2) * 64:(h % 2) * 64 + 64, h // 2, qs], po[0:64, 0:nq], rl[:, 0:nq], ALU.mult, pw=True)

    def allgather(self, gin, gout):
        k = self.k
        groups = [[0, 1, 2, 3], [4, 5, 6, 7]]
        return k.op("pool", lambda e: e.collective_compute("AllGather", ALU.bypass, replica_groups=groups,
                                                            ins=[gin.ap().opt()], outs=[gout.ap().opt()]),
                    reads=[gin.name], writes=[gout.name], cc=True)

    def kv_exchange(self, u, l, ph, KlO, KrO, Kl, Kr):
        k = self.k
        nc = k.nc
        pb = self.pb
        gin = nc.dram_tensor("kvgin%d" % l, [160, 1024], BF16)
        gout = nc.dram_tensor("kvgout%d" % l, [640, 1024], BF16)
        k.dma(gin[0:128, :], KlO[:])
        k.dma(gin[128:160, :], KrO[:], pw=True)
        self.allgather(gin, gout)
        for r in range(4):
            k.dma(Kl[:, 256 + r * 1024:256 + (r + 1) * 1024], gout[r * 160:r * 160 + 128, :], pw=True)
            k.dma(Kr[:, 256 + r * 1024:256 + (r + 1) * 1024], gout[r * 160 + 128:r * 160 + 160, :], pw=True)
        cch = k.sb("cch%d" % l, [128, 2, 160], F32, ph)
        k.dma(cch[:], self.d["cache"][l].rearrange("(a p) f -> p a f", p=128))
        for a_ in range(2):
            pt = pb[2 + a_]
            k.mm(pt[:, 0:128], cch[:, a_, 0:128], self.ident)
            k.cp(Kl[:, a_ * 128:(a_ + 1) * 128], pt[:, 0:128], pw=True, eng="act")
            k.mm(pt[:, 128:256], cch[:, a_, 32:160], self.ident, part=True)
            k.cp(Kr[:, a_ * 128:(a_ + 1) * 128], pt[96:128, 128:256], pw=True)

    def gla(self, u, l, ph):
        k = self.k
        d = self.d
        pb = self.pb
        T = u.T
        sb = lambda n, shp, dt: k.sb("%s_%s%d" % (n, u.name, l), shp, dt, ph)
        xT = self.xT
        wgate = sb("wgate", [17, 512], BF16)
        k.dma(wgate[:], d["wgate"][l, :, :], q="pool")
        triFn = sb("triFn", [128, 128], F32); triBn = sb("triBn", [128, 128], F32)
        k.ts(triFn[:], self.triF, -1.0 / 16, None, ALU.mult)
        k.ts(triBn[:], self.triB, -1.0 / 16, None, ALU.mult)
        gqT = sb("gqT", [128, 2, T], BF16); gkT = sb("gkT", [128, 2, T], BF16); sgrT = sb("sgrT", [128, 4, T], BF16)
        glr = [sb("glrf", [17, T], BF16), sb("glrb", [17, T], BF16)]
        k.memset(glr[0][:], 1.0); k.memset(glr[1][:], 1.0)
        gk_tok = sb("gk_tok", [128, u.ntile, 256], BF16); gv_tok = sb("gv_tok", [128, u.ntile, 512], BF16)
        wA = self.wload(d["win"][l, :, :, C_GQ:C_GQ + 512], 8, 512)
        for b_ in range(u.nblk):
            bs = slice(b_ * 512, (b_ + 1) * 512)
            for c in range(4):
                p = pb[c % 4]
                for kk in range(8):
                    k.mm(p[:], wA[:, kk, c * 128:(c + 1) * 128], xT[:, kk, bs], start=(kk == 0), stop=(kk == 7))
                if c < 2:
                    k.act(gqT[:, c, bs], p[:], AF.Copy, scale=0.125, pw=True)
                else:
                    k.cp(gkT[:, c - 2, bs], p[:], pw=True)
        wB = self.wload(d["win"][l, :, :, C_GR:C_GR + 512], 8, 512)
        for b_ in range(u.nblk):
            bs = slice(b_ * 512, (b_ + 1) * 512)
            for c in range(4):
                p = pb[c % 4]
                for kk in range(8):
                    k.mm(p[:], wB[:, kk, c * 128:(c + 1) * 128], xT[:, kk, bs], start=(kk == 0), stop=(kk == 7))
                k.act(sgrT[:, c, bs], p[:], AF.Silu, pw=True)
        wC = self.wload(d["win"][l, :, :, C_GLR:C_GLR + 64], 8, 64)
        for b_ in range(u.nblk):
            bs = slice(b_ * 512, (b_ + 1) * 512)
            p = pb[4 + b_ % 2]
            for kk in range(8):
                k.mm(p[0:64, :], wC[:, kk, :], xT[:, kk, bs], start=(kk == 0), stop=(kk == 7))
            k.cp(glr[0][0:16, bs], p[0:16, :], pw=True)
            k.cp(glr[1][0:16, bs], p[32:48, :], pw=True)
        wD = self.wload(d["win"][l, :, :, C_GKT:C_GKT + 512], 8, 512)
        wE = self.wload(d["win"][l, :, :, C_GKT + 512:C_GKT + 768], 8, 256)
        for ti in range(u.ntile):
            ts_ = slice(ti * 128, (ti + 1) * 128)
            p = pb[(ti % 2) * 2]; p2 = pb[(ti % 2) * 2 + 1]
            for kk in range(8):
                k.mm(p[:], xT[:, kk, ts_], wD[:, kk, :], start=(kk == 0), stop=(kk == 7))
            for kk in range(8):
                k.mm(p2[:, 0:256], xT[:, kk, ts_], wE[:, kk, :], start=(kk == 0), stop=(kk == 7))
            k.cp(gk_tok[:, ti, :], p[:, 0:256], pw=True)
            k.cp(gv_tok[:, ti, 0:256], p[:, 256:512], pw=True)
            k.cp(gv_tok[:, ti, 256:512], p2[:, 0:256], pw=True, eng="act")
        self.ckpt(3.1)
        W = []
        for i in range(2):
            W.append(dict(
                e1=sb("ge1_%d" % i, [128, 512], F32), sp=sb("gsp_%d" % i, [128, 512], F32),
                en=sb("gen_%d" % i, [128, 512], F32), kinv=sb("gkinv_%d" % i, [128, 512], BF16),
                EpT=sb("gEpT_%d" % i, [128, 512], F32), EnT=sb("gEnT_%d" % i, [128, 512], F32),
                qdec=sb("gqdec_%d" % i, [128, 4, 128], BF16), kinvT=sb("gkinvT_%d" % i, [128, 4, 128], BF16),
                aTm=[sb("gaTm0_%d" % i, [128, 512], BF16), sb("gaTm1_%d" % i, [128, 512], BF16)]))
        pz, pc, pcT, pu, pa, po = pb[0], pb[1], pb[2], pb[3], [pb[4], pb[5]], pb[6]
        tri = [triFn, triBn]
        msk = [self.triF, self.triB]

        def prep(ti, w):
            ts_ = slice(ti * 128, (ti + 1) * 128)
            for dd in range(2):
                k.mm(pz[:, dd * 256:(dd + 1) * 256], glr[dd][:, ts_], wgate[:, dd * 256:(dd + 1) * 256], part=True)
            k.act(w["e1"][:], pz[:], AF.Exp, scale=-1.0)
            k.act(w["sp"][:], w["e1"][:], AF.Ln, bias=1.0)
            for dd in range(2):
                k.mm(pc[:, dd * 256:(dd + 1) * 256], tri[dd][:], w["sp"][:, dd * 256:(dd + 1) * 256], part=True)
            k.act(w["en"][:], pc[:], AF.Exp, scale=-1.0)
            k.tt(w["kinv"][:].rearrange("p (a b) -> p a b", a=2), w["en"][:].rearrange("p (a b) -> p a b", a=2),
                 gk_tok[:, ti, :].unsqueeze(1).to_broadcast([128, 2, 256]), ALU.mult)
            for dd in range(2):
                for pr_ in range(2):
                    i4 = dd * 2 + pr_
                    k.mm(pcT[:, i4 * 128:(i4 + 1) * 128], w["sp"][:, dd * 256 + pr_ * 128:dd * 256 + (pr_ + 1) * 128], tri[dd][:], part=True)
            k.act(w["EpT"][:], pcT[:], AF.Exp)
            k.act(w["EnT"][:], pcT[:], AF.Exp, scale=-1.0)
            k.tt(w["qdec"][:].rearrange("p (a b) c -> p a b c", a=2), w["EpT"][:].rearrange("p (a b c) -> p a b c", a=2, b=2),
                 gqT[:, :, ts_].unsqueeze(1).to_broadcast([128, 2, 2, 128]), ALU.mult)
            k.tt(w["kinvT"][:].rearrange("p (a b) c -> p a b c", a=2), w["EnT"][:].rearrange("p (a b c) -> p a b c", a=2, b=2),
                 gkT[:, :, ts_].unsqueeze(1).to_broadcast([128, 2, 2, 128]), ALU.mult)

        def dec_col(w, dd, pr_, ch):
            c = (dd * 2 + pr_) * 128 + ch * 64 + (63 if dd == 0 else 0)
            return w["EpT"][:, c:c + 1]

        nch = 2 * u.ntile
        S = [sb("gS%d" % i, [128, 2, 128], F32) for i in range(2)]
        Stmp = sb("gStmp", [128, 2, 128], F32)
        SpB = sb("gSpB", [128, nch, 2, 128], BF16)
        Sfb = [sb("gSfb%d" % i, [128, 2, 2, 128], BF16) for i in range(2)]
        oT = sb("goT", [128, 4, T], F32)
        exch = u.sample
        if exch:
            qhT = sb("gqhT", [128, 2, 2, T], BF16)
            cum = [sb("gcum%d" % i, [128, 2], F32) for i in range(2)]

        def state_update(w, ti, dd, ch):
            cr = slice(ch * 64, (ch + 1) * 64)
            for h in range(4):
                k.mm(pu[(h % 2) * 64:(h % 2) * 64 + 64, (h // 2) * 128:(h // 2 + 1) * 128],
                     w["kinv"][cr, dd * 256 + h * 64:dd * 256 + (h + 1) * 64], gv_tok[cr, ti, h * 128:(h + 1) * 128], part=True)
            k.tt(Stmp[:].rearrange("p a b -> p (a b)"), S[dd][:].rearrange("p a b -> p (a b)"), pu[:, 0:256], ALU.add)
            for pr_ in range(2):
                k.ts(S[dd][:, pr_, :], Stmp[:, pr_, :], dec_col(w, dd, pr_, ch), None, ALU.mult, pw=True)

        def qhat(w, ti, dd, ch):
            if not exch:
                return
            col = slice(ti * 128 + ch * 64, ti * 128 + (ch + 1) * 64)
            for pr_ in range(2):
                k.ts(qhT[:, dd, pr_, col], w["qdec"][:, dd * 2 + pr_, ch * 64:(ch + 1) * 64], cum[dd][:, pr_:pr_ + 1], None, ALU.mult, pw=True)
                k.tt(cum[dd][:, pr_:pr_ + 1], cum[dd][:, pr_:pr_ + 1], dec_col(w, dd, pr_, ch), ALU.mult, pw=True)

        for (s0, n) in u.segs:
            tiles = list(range(s0 // 128, (s0 + n) // 128))
            seg = s0 // 256
            k.memset(S[1][:], 0.0)
            if exch:
                k.memset(cum[1][:], 1.0)
            for it, ti in enumerate(reversed(tiles)):
                w = W[it % 2]
                prep(ti, w)
                for ch in (1, 0):
                    k.cp(SpB[:, ti * 2 + ch, :, :], S[1][:], pw=True, eng="act")
                    qhat(w, ti, 1, ch)
                    state_update(w, ti, 1, ch)
            if not exch:
                k.dma(d["nsgla"][seg, l, 1].rearrange("(j i) k v -> (i k) j v", i=2), S[1][:])
            self.ckpt(3.2)
            k.memset(S[0][:], 0.0)
            if exch:
                k.memset(cum[0][:], 1.0)
            for it, ti in enumerate(tiles):
                w = W[it % 2]
                sf = Sfb[it % 2]
                ts_ = slice(ti * 128, (ti + 1) * 128)
                prep(ti, w)
                for ch in (0, 1):
                    k.cp(sf[:, ch, :, :], S[0][:], pw=True, eng="act")
                    qhat(w, ti, 0, ch)
                    state_update(w, ti, 0, ch)
                sk = os.environ.get("KSKIP", "")
                if sk == "s2a":
                    continue
                for dd in range(2):
                    for h in (0, 2, 1, 3):
                        hp = slice((h % 2) * 64, (h % 2) * 64 + 64)
                        for sh in range(2):
                            k.mm(pa[dd][sh * 64:(sh + 1) * 64, h * 128:(h + 1) * 128], w["kinvT"][hp, dd * 2 + h // 2, sh * 64:(sh + 1) * 64],
                                 w["qdec"][hp, dd * 2 + h // 2, :], part=True)
                    k.tt(w["aTm"][dd][:].rearrange("p (a b) -> p a b", a=4), pa[dd][:].rearrange("p (a b) -> p a b", a=4),
                         msk[dd].unsqueeze(1).to_broadcast([128, 4, 128]), ALU.mult)
                if sk == "s2b":
                    continue
                pi_ = pb[3]
                for h in range(4):
                    hp = slice((h % 2) * 64, (h % 2) * 64 + 64)
                    reg = po[:, h * 128:(h + 1) * 128]
                    k.mm(reg, gv_tok[:, ti, h * 128:(h + 1) * 128], w["aTm"][0][:, h * 128:(h + 1) * 128], start=True, stop=False, part=True)
                    k.mm(reg, gv_tok[:, ti, h * 128:(h + 1) * 128], w["aTm"][1][:, h * 128:(h + 1) * 128], start=False, stop=True, part=True)
                if sk == "s2c":
                    k.cp(oT[:, :, ts_], po[:].rearrange("p (a b) -> p a b", a=4), pw=True, eng="act")
                    continue
                for h in (0, 2, 1, 3):
                    hp = slice((h % 2) * 64, (h % 2) * 64 + 64)
                    for ch in range(2):
                        for vh in range(2):
                            sub = pi_[vh * 64:(vh + 1) * 64, h * 128 + ch * 64:h * 128 + (ch + 1) * 64]
                            k.mm(sub, sf[hp, ch, h // 2, vh * 64:(vh + 1) * 64], w["qdec"][hp, 0 * 2 + h // 2, ch * 64:(ch + 1) * 64], start=True, stop=False, part=True)
                            k.mm(sub, SpB[hp, ti * 2 + ch, h // 2, vh * 64:(vh + 1) * 64], w["qdec"][hp, 1 * 2 + h // 2, ch * 64:(ch + 1) * 64], start=False, stop=True, part=True)
                k.cp(oT[:, :, ts_], po[:].rearrange("p (a b) -> p a b", a=4), pw=True, eng="act")
                k.tt(oT[:, :, ts_], oT[:, :, ts_], pi_[:].rearrange("p (a b) -> p a b", a=4), ALU.add, pw=True)
            if not exch:
                k.dma(d["nsgla"][seg, l, 0].rearrange("(j i) k v -> (i k) j v", i=2), S[0][:])
            self.ckpt(3.3)
        if exch:
            self.gla_exchange(u, l, ph, S, cum, qhT, oT)
        k.barrier()
        sq = [W[0]["e1"], W[0]["sp"]]
        rs = [W[0]["en"], W[0]["EpT"]]
        n = 0
        for h in range(4):
            for b_ in range(u.nblk):
                bs = slice(b_ * 512, (b_ + 1) * 512)
                i2 = n % 2; n += 1
                k.tt(sq[i2][:], oT[:, h, bs], oT[:, h, bs], ALU.mult)
                pss = pb[i2]
                k.mm(pss[:], self.onesf[:], sq[i2][:])
                k.act(rs[i2][:], pss[:], AF.Ln, scale=1.0 / 128, bias=self.epsc[:, 0:1])
                k.act(rs[i2][:], rs[i2][:], AF.Exp, scale=-0.5)
                k.tt(sq[i2][:], oT[:, h, bs], rs[i2][:], ALU.mult)
                k.stt(self.ygla[:, h, bs], sq[i2][:], self.gcol[:, l, 0:1], sgrT[:, h, bs], ALU.mult, ALU.mult, pw=True)

    def gla_exchange(self, u, l, ph, S, cum, qhT, oT):
        k = self.k
        nc = k.nc
        pb = self.pb
        sb = lambda n, shp, dt: k.sb("%s_%s%d" % (n, u.name, l), shp, dt, ph)
        cg = sb("gcg", [128, 2, 2, 129], F32)
        for dd in range(2):
            k.cp(cg[:, dd, :, 0:128], S[dd][:], pw=True)
            k.cp(cg[:, dd, :, 128:129], cum[dd][:].unsqueeze(2), pw=True)
        gin = nc.dram_tensor("glagin%d" % l, [128, 516], F32)
        gout = nc.dram_tensor("glagout%d" % l, [512, 516], F32)
        k.dma(gin[:, :], cg[:].rearrange("p a b c -> p (a b c)"))
        self.allgather(gin, gout)
        cgs = sb("gcgs", [128, 4, 516], F32)
        k.dma(cgs[:], gout[:, :].rearrange("(r p) f -> p r f", p=128))
        S0 = sb("gS0", [128, 2, 2, 128], F32)
        for dd in range(2):
            k.dma(S0[:, dd, :, :], self.d["sgla"][l, dd].rearrange("(j i) k v -> (i k) j v", i=2), pw=True)
        new = sb("gnew", [128, 2, 128], F32); diff = sb("gdiff", [128, 2, 128], F32)
        Sent = sb("gSent", [128, 2, 2, 128], BF16)
        for dd in range(2):
            cur = S0[:, dd, :, :]
            for j in ([0, 1, 2] if dd == 0 else [3, 2, 1]):
                for pr_ in range(2):
                    off = (dd * 2 + pr_) * 129
                    k.stt(new[:, pr_, :], cur[:, pr_, :], cgs[:, j, off + 128:off + 129], cgs[:, j, off:off + 128], ALU.mult, ALU.add, pw=True)
                k.tt(diff[:], new[:], cur, ALU.subtract)
                mc = (j if dd == 0 else 4 + j)
                k.stt(cur, diff[:], self.rank[:, mc:mc + 1], cur, ALU.mult, ALU.add, pw=True)
            k.cp(Sent[:, dd, :, :], cur, pw=True)
        n = 0
        for h in range(4):
            hp = slice((h % 2) * 64, (h % 2) * 64 + 64)
            for b_ in range(u.nblk):
                bs = slice(b_ * 512, (b_ + 1) * 512)
                pc_ = pb[n % 4]; n += 1
                for vh in range(2):
                    for dd in range(2):
                        k.mm(pc_[vh * 64:(vh + 1) * 64, :], Sent[hp, dd, h // 2, vh * 64:(vh + 1) * 64], qhT[hp, dd, h // 2, bs],
                             start=(dd == 0), stop=(dd == 1), part=True)
                k.tt(oT[:, h, bs], oT[:, h, bs], pc_[:], ALU.add, pw=True)

    def gdn(self, u, l, ph):
        k = self.k
        d = self.d
        pb = self.pb
        T = u.T
        sb = lambda n, shp, dt: k.sb("%s_%s%d" % (n, u.name, l), shp, dt, ph)
        xT = self.xT
        exch = u.sample
        Wd = 128 if exch else 64
        nseg = len(u.segs)
        Tp = T + 2 * nseg
        adtb = sb("adtb", [128, 32], F32)
        k.dma(adtb[:], d["adt"][l, 0:1, :].partition_broadcast(128))
        negA = sb("negA", [128, 16], F32)
        k.act(negA[:], adtb[:, 0:16], AF.Exp)
        k.ts(negA[:], negA[:], -1.0, None, ALU.mult)
        qT = sb("dqT", [128, 4, T], BF16); kT = sb("dkT", [128, 4, T], BF16)
        k_tok = sb("dk_tok", [128, u.ntile, 512], BF16); v_tok = sb("dv_tok", [128, u.ntile, 512], BF16)
        O1 = sb("dO1", [128, 8, T], BF16)
        O2 = sb("dO2", [128, 8, T], BF16) if exch else None
        nt = u.ntile
        dab = sb("dab", [128, u.ntile, 32], F32)
        g_tok = sb("dg", [128, nt, 16], F32); l2 = sb("dl2", [128, nt, 16], F32); bet = sb("dbet", [128, nt, 16], F32)
        tmpg = sb("dtmpg", [128, nt, 16], F32)
        halo = None
        if exch:
            halo = self.gdn_halo(u, l, ph, None)
            self._gcomp = sb("dgcomp", [128, 2, 2, 2, 128], F32)
        cs_stack = ExitStack()
        sbc = lambda n, shp, dt: k.sb("%s_%s%d" % (n, u.name, l), shp, dt, cs_stack)
        vT = sbc("dvT", [128, 4, T], BF16)
        raw = [sbc("draw%d" % i, [128, Tp], F32) for i in range(2)]
        cacc = [sbc("dcacc%d" % i, [128, T], F32) for i in range(2)]
        csl = [sbc("dcsl%d" % i, [128, T], F32) for i in range(2)]
        sq = [sbc("dsq%d" % i, [128, 512], F32) for i in range(2)]
        rn = [sbc("drn%d" % i, [128, 512], F32) for i in range(2)]
        for r_ in raw:
            k.memset(r_[:], 0.0)
        for c3 in range(3):
            wq_ = self.wload(d["win"][l, :, :, C_DQKV + c3 * 512:C_DQKV + (c3 + 1) * 512], 8, 512)
            for c4 in range(4):
                c = c3 * 4 + c4
                rw = raw[c % 2]; ca = cacc[c % 2]; cs_ = csl[c % 2]
                for b_ in range(u.nblk):
                    p = pb[(c * u.nblk + b_) % 4]
                    for kk in range(8):
                        k.mm(p[:], wq_[:, kk, c4 * 128:(c4 + 1) * 128], xT[:, kk, b_ * 512:(b_ + 1) * 512], start=(kk == 0), stop=(kk == 7))
                    for si, (s0, n) in enumerate(u.segs):
                        lo = max(s0, b_ * 512); hi = min(s0 + n, (b_ + 1) * 512)
                        if lo < hi:
                            o = 2 * si + 1
                            k.cp(rw[:, o + lo:o + hi], p[:, lo - b_ * 512:hi - b_ * 512], pw=True)
                if halo is not None:
                    k.cp(rw[:, 0:1], halo[0][:, c:c + 1], pw=True)
                    k.cp(rw[:, Tp - 1:Tp], halo[1][:, c:c + 1], pw=True)
                for si, (s0, n) in enumerate(u.segs):
                    o = s0 + 2 * si
                    k.ts(ca[:, s0:s0 + n], rw[:, o:o + n], self.convc[:, l, c, 0:1], None, ALU.mult, pw=True)
                    k.stt(ca[:, s0:s0 + n], rw[:, o + 1:o + 1 + n], self.convc[:, l, c, 1:2], ca[:, s0:s0 + n], ALU.mult, ALU.add, pw=True)
                    k.stt(ca[:, s0:s0 + n], rw[:, o + 2:o + 2 + n], self.convc[:, l, c, 2:3], ca[:, s0:s0 + n], ALU.mult, ALU.add, pw=True)
                if c3 == 2:
                    k.act(vT[:, c4, :], ca[:], AF.Silu, pw=True)
                    continue
                k.act(cs_[:], ca[:], AF.Silu)
                for b_ in range(u.nblk):
                    bs = slice(b_ * 512, (b_ + 1) * 512)
                    i2 = (c * u.nblk + b_) % 2
                    k.tt(sq[i2][:], cs_[:, bs], cs_[:, bs], ALU.mult)
                    pss = pb[4 + i2]
                    k.mm(pss[:], self.bones, sq[i2][:])
                    k.act(rn[i2][:], pss[:], AF.Ln, bias=self.epsc[:, 0:1])
                    k.act(rn[i2][:], rn[i2][:], AF.Exp, scale=-0.5)
                    if c3 == 0:
                        k.stt(qT[:, c4, bs], cs_[:, bs], 0.125, rn[i2][:], ALU.mult, ALU.mult, pw=True)
                    else:
                        k.tt(kT[:, c4, bs], cs_[:, bs], rn[i2][:], ALU.mult, pw=True)
        wab = self.wload(d["win"][l, :, :, C_DAB:C_DAB + 32], 8, 32)
        for ti in range(u.ntile):
            p = pb[ti % 4]
            for kk in range(8):
                k.mm(p[:, 0:32], xT[:, kk, ti * 128:(ti + 1) * 128], wab[:, kk, :], start=(kk == 0), stop=(kk == 7))
            self.evac(dab[:, ti, :], p[:, 0:32])
        k.tt(tmpg[:], dab[:, :, 0:16], adtb[:, 16:32].unsqueeze(1).to_broadcast([128, nt, 16]), ALU.add)
        k.act(tmpg[:], tmpg[:], AF.Exp)
        k.act(tmpg[:], tmpg[:], AF.Ln, bias=1.0)
        k.tt(g_tok[:], tmpg[:], negA[:].unsqueeze(1).to_broadcast([128, nt, 16]), ALU.mult)
        k.act(l2[:], dab[:, :, 16:32], AF.Exp, scale=-1.0)
        k.act(l2[:], l2[:], AF.Ln, bias=1.0)
        k.act(bet[:], l2[:], AF.Exp, scale=-1.0)
        for ti in range(nt):
            ts_ = slice(ti * 128, (ti + 1) * 128)
            for c in range(4):
                k.mm(self.pbb[:, c * 128:(c + 1) * 128], kT[:, c, ts_], self.identb[:], tr=True, part=True)
                k.mm(self.pbb[:, 512 + c * 128:512 + (c + 1) * 128], vT[:, c, ts_], self.identb[:], tr=True, part=True)
            k.cp(k_tok[:, ti, :], self.pbb[:, 0:512], pw=True)
            k.cp(v_tok[:, ti, :], self.pbb[:, 512:1024], pw=True)
        k.barrier()
        cs_stack.close()
        cm_stack = ExitStack()
        sb_ph = sb
        sb = lambda n, shp, dt: k.sb("%s_%s%d" % (n, u.name, l), shp, dt, cm_stack)
        tri01 = [self.triF, self.triB]
        negi = [self.negFi, self.negBi]; negs = [self.negFs, self.negBs]
        def mkset(i):
            f4 = lambda n: sb("%s_%d" % (n, i), [128, 4, 128], F32)
            B = dict(rhsG=f4("drhsG"), rhsG2=f4("drhsG2"), tmp4=f4("dtmp4"), arg=f4("darg"), Einc=f4("dEinc"), Estr=f4("dEstr"),
                     EG=f4("dEG"), A=f4("dA"), Lm=f4("dL"), U2=f4("dU2"), L2m=f4("dL2"), Wm=f4("dWm"))
            B["aqk"] = sb("daqk_%d" % i, [128, 4, 128], BF16)
            B["BR"] = sb("dBR_%d" % i, [128, 4, 64], F32); B["BRk"] = sb("dBRk_%d" % i, [128, 4, 128], F32)
            B["gam"] = sb("dgam_%d" % i, [128, 4], F32); B["egam"] = sb("degam_%d" % i, [128, 4], F32); B["kf"] = sb("dkf_%d" % i, [128, 4], F32)
            B["u_tok"] = sb("du_tok_%d" % i, [128, 4, Wd], F32)
            B["wT"] = sb("dwT_%d" % i, [128, 4, 128], BF16)
            B["kend"] = sb("dkend_%d" % i, [128, 4, 64], BF16)
            B["qdT"] = sb("dqdT_%d" % i, [128, 4, 128], BF16)
            B["vnew"] = sb("dvnew_%d" % i, [128, 4, Wd], BF16)
            k.memset(B["BRk"][:], 0.0)
            k.memset(B["u_tok"][:], 0.0)
            return B
        bsets = [mkset(0)]
        if os.environ.get("KDB2", "1") == "1":
            try:
                bsets.append(mkset(1))
            except AssertionError:
                bsets.append(bsets[0])
        else:
            bsets.append(bsets[0])
        git = 0
        S = sb("dS", [128, 2, Wd], F32)
        Sb = sb("dSb", [128, 2, Wd], BF16)
        self.gdn_comp = []
        first_dir = True
        for dd in range(2):
            cl = [63, 127] if dd == 0 else [0, 64]
            for hh in range(2):
                gc = slice(dd * 8 + hh * 4, dd * 8 + hh * 4 + 4)
                for si, (s0, n) in enumerate(u.segs):
                    tiles = list(range(s0 // 128, (s0 + n) // 128))
                    if dd == 1:
                        tiles = tiles[::-1]
                    seg = s0 // 256
                    k.memset(S[:], 0.0)
                    if exch:
                        for pr_ in range(2):
                            for q in range(2):
                                k.cp(S[q * 64:(q + 1) * 64, pr_, 64:128], self.ident[q * 64:(q + 1) * 64, q * 64:(q + 1) * 64], pw=True)
                    k.cp(Sb[:], S[:])
                    for ti in tiles:
                        ts_ = slice(ti * 128, (ti + 1) * 128)
                        B = bsets[git % 2]; git += 1
                        rhsG, rhsG2, tmp4, arg, Einc, Estr, EG = B["rhsG"], B["rhsG2"], B["tmp4"], B["arg"], B["Einc"], B["Estr"], B["EG"]
                        A, Lm, U2, L2m, Wm = B["A"], B["Lm"], B["U2"], B["L2m"], B["Wm"]
                        aqk, BR, BRk, gam, egam, kf = B["aqk"], B["BR"], B["BRk"], B["gam"], B["egam"], B["kf"]
                        u_tok, wT, kend, qdT, vnew = B["u_tok"], B["wT"], B["kend"], B["qdT"], B["vnew"]
                        k.tt(rhsG[:], tri01[dd].unsqueeze(1).to_broadcast([128, 4, 128]), g_tok[:, ti, gc].unsqueeze(2).to_broadcast([128, 4, 128]), ALU.mult)
                        k.tt(tmp4[:], self.ident.unsqueeze(1).to_broadcast([128, 4, 128]), l2[:, ti, gc].unsqueeze(2).to_broadcast([128, 4, 128]), ALU.mult)
                        k.tt(rhsG2[:], rhsG[:], tmp4[:], ALU.subtract)
                        pG, pG2, pg = pb[0], pb[1], pb[2]
                        k.mm(pG[:], self.onesf[:], rhsG[:].rearrange("p a b -> p (a b)"))
                        k.mm(pG2[:], self.onesf[:], rhsG2[:].rearrange("p a b -> p (a b)"))
                        k.mm(pg[:, 0:4], tri01[dd], g_tok[:, ti, gc])
                        k.cp(gam[:], pg[:, 0:4])
                        k.act(egam[:], gam[:], AF.Exp)
                        k.act(EG[:].rearrange("p a b -> p (a b)"), pG[:], AF.Exp)
                        k.op("dve", lambda e, tmp4=tmp4, pG=pG, gam=gam: e.tensor_tensor(
                                 out=tmp4[:], in0=pG[:].rearrange("p (a b) -> p a b", a=4),
                                 in1=gam[:].unsqueeze(2).to_broadcast([128, 4, 128]), op=ALU.subtract),
                             reads=[pG.name, gam.name, EG.name], writes=[tmp4.name])
                        k.tt(arg[:], tmp4[:], negi[dd].unsqueeze(1).to_broadcast([128, 4, 128]), ALU.add)
                        k.act(Einc[:], arg[:], AF.Exp)
                        k.tt(tmp4[:], pG2[:].rearrange("p (a b) -> p a b", a=4), gam[:].unsqueeze(2).to_broadcast([128, 4, 128]), ALU.subtract)
                        k.tt(arg[:], tmp4[:], negs[dd].unsqueeze(1).to_broadcast([128, 4, 128]), ALU.add)
                        k.act(Estr[:], arg[:], AF.Exp)
                        pkk, pqk = pb[5], pb[6]
                        for i in (0, 2, 1, 3):
                            h = hh * 4 + i
                            hp = slice((h % 2) * 64, (h % 2) * 64 + 64)
                            k.mm(pkk[:, i * 128:(i + 1) * 128], kT[hp, h // 2, ts_], kT[hp, h // 2, ts_], part=True)
                            k.mm(pqk[:, i * 128:(i + 1) * 128], kT[hp, h // 2, ts_], qT[hp, h // 2, ts_], part=True)
                        k.tt(A[:].rearrange("p a b -> p (a b)"), pkk[:], Estr[:].rearrange("p a b -> p (a b)"), ALU.mult)
                        k.tt(aqk[:].rearrange("p a b -> p (a b)"), pqk[:], Einc[:].rearrange("p a b -> p (a b)"), ALU.mult)
                        pL, pU, pW = pb[5], pb[6], pb[0]
                        for i in range(4):
                            k.mm(pL[:, i * 128:(i + 1) * 128], A[:, i, :], self.ident, part=True)
                        k.cp(Lm[:].rearrange("p a b -> p (a b)"), pL[:], eng="act")
                        k.tt(Wm[:], self.ident.unsqueeze(1).to_broadcast([128, 4, 128]), A[:], ALU.subtract)
                        Uc, Lc, Un, Ln_ = A, Lm, U2, L2m
                        for lev in range(1, 6):
                            if lev < 5:
                                for i in range(4):
                                    k.mm(pU[:, i * 128:(i + 1) * 128], Lc[:, i, :], Uc[:, i, :], part=True)
                            for i in range(4):
                                k.mm(pL[:, i * 128:(i + 1) * 128], Uc[:, i, :], Lc[:, i, :], part=True)
                            if lev < 5:
                                k.cp(Un[:].rearrange("p a b -> p (a b)"), pU[:], eng="act")
                            k.cp(Ln_[:].rearrange("p a b -> p (a b)"), pL[:])
                            for i in range(4):
                                k.mm(pW[:, i * 128:(i + 1) * 128], Ln_[:, i, :], Wm[:, i, :], part=True)
                            k.tt(Wm[:].rearrange("p a b -> p (a b)"), Wm[:].rearrange("p a b -> p (a b)"), pW[:], ALU.add)
                            Uc, Un = Un, Uc
                            Lc, Ln_ = Ln_, Lc
                        hcols = slice(hh * 256, (hh + 1) * 256)
                        k.tt(BR[:], v_tok[:, ti, hcols].rearrange("p (a b) -> p a b", a=4), bet[:, ti, gc].unsqueeze(2).to_broadcast([128, 4, 64]), ALU.mult)
                        k.tt(kf[:], bet[:, ti, gc], egam[:], ALU.mult)
                        for q in range(2):
                            k.tt(BRk[:, q::2, q * 64:(q + 1) * 64], k_tok[:, ti, hcols].rearrange("p (a b) -> p a b", a=4)[:, q::2, :],
                                 kf[:, q::2].unsqueeze(2).to_broadcast([128, 2, 64]), ALU.mult, pw=True)
                        pu_, pw_ = pb[1], pb[2]
                        for i in range(4):
                            k.mm(pu_[:, i * 64:(i + 1) * 64], Wm[:, i, :], BR[:, i, :], part=True)
                            k.mm(pw_[:, i * 128:(i + 1) * 128], BRk[:, i, :], Wm[:, i, :], part=True)
                        k.cp(u_tok[:, :, 0:64], pu_[:, 0:256].rearrange("p (a b) -> p a b", a=4), pw=True)
                        k.cp(wT[:].rearrange("p a b -> p (a b)"), pw_[:], eng="act")
                        k.tt(kf[:], Einc[:, :, cl[0]], Einc[:, :, cl[1]], ALU.add)
                        k.tt(kend[:], k_tok[:, ti, hcols].rearrange("p (a b) -> p a b", a=4), kf[:].unsqueeze(2).to_broadcast([128, 4, 64]), ALU.mult)
                        for i in range(4):
                            h = hh * 4 + i
                            hp = slice((h % 2) * 64, (h % 2) * 64 + 64)
                            k.tt(qdT[hp, i, :], qT[hp, h // 2, ts_], EG[hp, i, :], ALU.mult, pw=True)
                        for ch in ((0, 1) if dd == 0 else (1, 0)):
                            cr = slice(ch * 64, (ch + 1) * 64)
                            pws, po, pS = pb[3], pb[4], pb[3]
                            for i in (0, 2, 1, 3):
                                h = hh * 4 + i
                                hp = slice((h % 2) * 64, (h % 2) * 64 + 64)
                                k.mm(pws[cr, i * Wd:(i + 1) * Wd], wT[hp, i, cr], Sb[hp, i // 2, :], part=True)
                            k.tt(vnew[cr, :, :], u_tok[cr, :, :], pws[cr, 0:4 * Wd].rearrange("p (a b) -> p a b", a=4), ALU.subtract, pw=True)
                            for i in range(4):
                                h = hh * 4 + i
                                hp = slice((h % 2) * 64, (h % 2) * 64 + 64)
                                reg = po[0:Wd, i * 64:(i + 1) * 64]
                                k.mm(reg, Sb[hp, i // 2, :], qdT[hp, i, cr], start=True, stop=False, part=True)
                                k.mm(reg, vnew[cr, i, :], aqk[cr, i, cr], start=False, stop=True, part=True)
                            for i in range(4):
                                h = hh * 4 + i
                                hp = slice((h % 2) * 64, (h % 2) * 64 + 64)
                                k.mm(pS[hp, i * Wd:(i + 1) * Wd], kend[cr, i, :], vnew[cr, i, :], part=True)
                            for i in range(4):
                                h = hh * 4 + i
                                hp = slice((h % 2) * 64, (h % 2) * 64 + 64)
                                k.stt(S[hp, i // 2, :], S[hp, i // 2, :], EG[hp, i, cl[ch]:cl[ch] + 1], pS[hp, i * Wd:(i + 1) * Wd], ALU.mult, ALU.add, pw=True)
                            k.cp(Sb[:], S[:], eng="act")
                            col = slice(ti * 128 + ch * 64, ti * 128 + (ch + 1) * 64)
                            hsl = slice(hh * 4, hh * 4 + 4)
                            pov = po[0:64, 0:256].rearrange("p (a b) -> p a b", a=4)
                            if dd == 0:
                                k.cp(O1[0:64, hsl, col], pov, pw=True)
                            else:
                                k.tt(O1[0:64, hsl, col], O1[0:64, hsl, col], pov, ALU.add, pw=True)
                            if exch:
                                Ox = O1 if dd == 0 else O2
                                k.cp(Ox[64:128, hsl, col], po[64:128, 0:256].rearrange("p (a b) -> p a b", a=4), pw=True)
                    if not exch:
                        k.dma(d["nsgdn"][seg, l, dd, hh * 4:(hh + 1) * 4].rearrange("(j i) k v -> (i k) j v", i=2), S[:, :, 0:64])
                    else:
                        self.gdn_save_comp(u, l, ph, dd, hh, S)
        k.barrier()
        cm_stack.close()
        sb = sb_ph
        if exch:
            self.gdn_exchange(u, l, ph, O1, O2)
        k.barrier()
        sdz = sb("sdz", [64, 8, T], BF16)
        wz = self.wload(d["win"][l, :, :, C_DZ:C_DZ + 512], 8, 512)
        for h in range(8):
            for b_ in range(u.nblk):
                bs = slice(b_ * 512, (b_ + 1) * 512)
                p = pb[(h * u.nblk + b_) % 4]
                for kk in range(8):
                    k.mm(p[0:64, :], wz[:, kk, h * 64:(h + 1) * 64], xT[:, kk, bs], start=(kk == 0), stop=(kk == 7))
                k.act(sdz[:, h, bs], p[0:64, :], AF.Silu, pw=True)
        sqo = [sb("dsqo%d" % i, [64, 512], F32) for i in range(2)]
        rso = [sb("drso%d" % i, [64, 512], F32) for i in range(2)]
        n_ = 0
        for h in range(8):
            for b_ in range(u.nblk):
                bs = slice(b_ * 512, (b_ + 1) * 512)
                i2 = n_ % 2; n_ += 1
                k.tt(sqo[i2][:], O1[0:64, h, bs], O1[0:64, h, bs], ALU.mult)
                pss = pb[i2]
                k.mm(pss[0:64, :], self.onesf[0:64, 0:64], sqo[i2][:])
                k.act(rso[i2][:], pss[0:64, :], AF.Ln, scale=1.0 / 64, bias=self.epsc[0:64, 0:1])
                k.act(rso[i2][:], rso[i2][:], AF.Exp, scale=-0.5)
                k.tt(sqo[i2][:], O1[0:64, h, bs], rso[i2][:], ALU.mult)
                k.stt(self.ygdn[(h % 2) * 64:(h % 2) * 64 + 64, h // 2, bs], sqo[i2][:], self.gcol[0:64, l, 1:2], sdz[:, h, bs], ALU.mult, ALU.mult, pw=True)

    def gdn_halo(self, u, l, ph, raw):
        k = self.k
        nc = k.nc
        pb = self.pb
        T = u.T
        sb = lambda n, shp, dt: k.sb("%s_%s%d" % (n, u.name, l), shp, dt, ph)
        bnd = sb("dbnd", [128, 12, 2], F32)
        for c3 in range(3):
            w = self.wload(self.d["win"][l, :, :, C_DQKV + c3 * 512:C_DQKV + (c3 + 1) * 512], 8, 512)
            for c4 in range(4):
                c = c3 * 4 + c4
                p = pb[c % 4]
                for kk in range(8):
                    k.mm(p[:, 0:2], w[:, kk, c4 * 128:(c4 + 1) * 128], self.xT[:, kk, 0:T:T - 1], start=(kk == 0), stop=(kk == 7))
                k.cp(bnd[:, c, :], p[:, 0:2], pw=True)
        hin = nc.dram_tensor("dhin%d" % l, [128, 24], F32)
        hout = nc.dram_tensor("dhout%d" % l, [512, 24], F32)
        k.dma(hin[:, :], bnd[:].rearrange("p a b -> p (a b)"))
        self.allgather(hin, hout)
        hall = sb("dhall", [128, 4, 12, 2], F32)
        k.dma(hall[:].rearrange("p r a b -> p r (a b)"), hout[:, :].rearrange("(r p) f -> p r f", p=128))
        hp_ = sb("dhprev", [128, 12], F32); hn_ = sb("dhnext", [128, 12], F32)
        k.memset(hp_[:], 0.0); k.memset(hn_[:], 0.0)
        for j in range(4):
            k.stt(hp_[:], hall[:, j, :, 1], self.rank[:, 8 + j:9 + j], hp_[:], ALU.mult, ALU.add)
            k.stt(hn_[:], hall[:, j, :, 0], self.rank[:, 12 + j:13 + j], hn_[:], ALU.mult, ALU.add)
        return hp_, hn_

    def gdn_save_comp(self, u, l, ph, dd, hh, S):
        self.k.cp(self._gcomp[:, dd, hh, :, :], S[:], pw=True)

    def gdn_exchange(self, u, l, ph, O1, O2):
        k = self.k
        nc = k.nc
        pb = self.pb
        sb = lambda n, shp, dt: k.sb("%s_%s%d" % (n, u.name, l), shp, dt, ph)
        gin = nc.dram_tensor("dgin%d" % l, [128, 1024], F32)
        gout = nc.dram_tensor("dgout%d" % l, [512, 1024], F32)
        k.dma(gin[:, :], self._gcomp[:].rearrange("p a b c d -> p (a b c d)"))
        self.allgather(gin, gout)
        call = sb("dcall", [128, 4, 1024], F32)
        k.dma(call[:], gout[:, :].rearrange("(r p) f -> p r f", p=128))
        S0 = sb("dS0", [128, 2, 4, 64], F32)
        for dd in range(2):
            k.dma(S0[:, dd, :, :], self.d["sgdn"][l, dd].rearrange("(j i) k v -> (i k) j v", i=2), pw=True)
        PhiT = sb("dPhiT", [128, 64], F32); new = sb("dnew", [128, 64], F32); diff = sb("ddiff", [128, 64], F32)
        Sent = sb("dSent", [128, 2, 8, 64], BF16)
        for dd in range(2):
            for jp in range(4):
                cur = S0[:, dd, jp, :]
                off = ((dd * 2 + jp // 2) * 2 + jp % 2) * 128
                for j in ([0, 1, 2] if dd == 0 else [3, 2, 1]):
                    pT, pN = pb[0], pb[1]
                    for i in range(2):
                        hp = slice(i * 64, (i + 1) * 64)
                        k.mm(pT[hp, 0:64], call[hp, j, off + 64:off + 128], self.ident[hp, i * 64:(i + 1) * 64], part=True)
                    k.cp(PhiT[:], pT[:, 0:64])
                    for i in range(2):
                        hp = slice(i * 64, (i + 1) * 64)
                        k.mm(pN[hp, 0:64], PhiT[hp, :], cur[hp, :], part=True)
                    k.tt(new[:], pN[:, 0:64], call[:, j, off:off + 64], ALU.add)
                    k.tt(diff[:], new[:], cur, ALU.subtract)
                    mc = (j if dd == 0 else 4 + j)
                    k.stt(cur, diff[:], self.rank[:, mc:mc + 1], cur, ALU.mult, ALU.add, pw=True)
                for i in range(2):
                    k.cp(Sent[64:128, dd, jp * 2 + i, :], cur[i * 64:(i + 1) * 64, :], pw=True)
        n = 0
        for h in range(8):
            for b_ in range(u.nblk):
                bs = slice(b_ * 512, (b_ + 1) * 512)
                pc_ = pb[2 + n % 4]; n += 1
                k.mm(pc_[0:64, :], Sent[64:128, 0, h, :], O1[64:128, h, bs], start=True, stop=False)
                k.mm(pc_[0:64, :], Sent[64:128, 1, h, :], O2[64:128, h, bs], start=False, stop=True)
                k.tt(O1[0:64, h, bs], O1[0:64, h, bs], pc_[0:64, :], ALU.add, pw=True)

    def merge(self, u, l, ph):
        k = self.k
        d = self.d
        sb = lambda n, shp, dt: k.sb("%s_%s%d" % (n, u.name, l), shp, dt, ph)
        pb = self.pb
        T = u.T
        nbr = min(self.stage, 3)
        ys = [self.ymla, self.ygla, self.ygdn]
        ypre = sb("ypre", [128, 8, T], BF16)
        sg = [sb("sg%d" % i, [128, 512], F32) for i in range(3)]
        pr = [sb("pr%d" % i, [128, 512], F32) for i in range(3)]
        tmp = sb("mtmp", [128, 512], F32)
        for dch in range(8):
            wg = self.wload(d["win"][l, :, :, C_GATES + dch * 384:C_GATES + (dch + 1) * 384], 8, 384)
            wb = self.wload(d["wbr"][l, :, dch].rearrange("p n k c -> p (n k) c"), 12, 128)
            for b in range(u.nblk):
                bs = slice(b * 512, (b + 1) * 512)
                for n in range(nbr):
                    pg = pb[n]; pp = pb[3 + n]
                    for kk in range(8):
                        k.mm(pg[:], wg[:, kk, n * 128:(n + 1) * 128], self.xT[:, kk, bs], start=(kk == 0), stop=(kk == 7))
                    for kc in range(4):
                        k.mm(pp[:], wb[:, n * 4 + kc, :], ys[n][:, kc, bs], start=(kc == 0), stop=(kc == 3))
                    k.act(sg[n][:], pg[:], AF.Sigmoid)
                    k.tt(pr[n][:], pp[:], sg[n][:], ALU.mult)
                if nbr == 1:
                    k.cp(ypre[:, dch, bs], pr[0][:], pw=True)
                elif nbr == 2:
                    k.tt(ypre[:, dch, bs], pr[0][:], pr[1][:], ALU.add, pw=True)
                else:
                    k.tt(tmp[:], pr[0][:], pr[1][:], ALU.add)
                    k.tt(ypre[:, dch, bs], tmp[:], pr[2][:], ALU.add, pw=True)
        self.dump("dbg_ypre", ypre[:], [128, 8, T], BF16)
        self.dump("dbg_grow", self.grow[0][:], [128, D], F32)
        ht = [sb("htmp%d" % i, [128, 512], F32) for i in range(2)]
        for half in range(2):
            hs_ = slice(half * 512, (half + 1) * 512)
            wo = self.wload(d["wout"][l, :, :, hs_], 8, 512)
            for ti in range(u.ntile):
                t = ti
                p = pb[ti % 4]
                for kk in range(8):
                    k.mm(p[:], ypre[:, kk, ti * 128:(ti + 1) * 128], wo[:, kk, :], start=(kk == 0), stop=(kk == 7))
                k.tt(ht[ti % 2][:], p[:], self.grow[0][:, hs_], ALU.mult)
                k.tt(self.h[t][:, hs_], self.h[t][:, hs_], ht[ti % 2][:], ALU.add, pw=True)

    def ffn(self, u, l, ph):
        k = self.k
        d = self.d
        sb = lambda n, shp, dt: k.sb("%s_%s%d" % (n, u.name, l), shp, dt, ph)
        pb = self.pb
        T = u.T
        actT = sb("actT", [128, 22, T], BF16)
        sgt = [sb("sgt%d" % i, [128, 512], F32) for i in range(2)]
        n = 0
        for i2 in range(11):
            w = self.wload(d["fwin"][l, :, :, i2 * 512:(i2 + 1) * 512], 8, 512)
            for ii in range(2):
                i = i2 * 2 + ii
                for b in range(u.nblk):
                    bs = slice(b * 512, (b + 1) * 512)
                    pg = pb[(n % 3) * 2]; pu = pb[(n % 3) * 2 + 1]
                    for kk in range(8):
                        k.mm(pg[:], w[:, kk, ii * 256:ii * 256 + 128], self.xT[:, kk, bs], start=(kk == 0), stop=(kk == 7))
                    for kk in range(8):
                        k.mm(pu[:], w[:, kk, ii * 256 + 128:ii * 256 + 256], self.xT[:, kk, bs], start=(kk == 0), stop=(kk == 7))
                    k.act(sgt[n % 2][:], pg[:], AF.Silu)
                    k.tt(actT[:, i, bs], pu[:], sgt[n % 2][:], ALU.mult, pw=True)
                    n += 1
        self.dump("dbg_xfT", self.xT[:], [128, 8, T], BF16)
        self.dump("dbg_actT", actT[:], [128, 22, T], BF16)
        k.barrier()
        ht = [sb("ftmp%d" % i, [128, 128], F32) for i in range(2)]
        for c8 in range(8):
            cs = slice(c8 * 128, (c8 + 1) * 128)
            w = self.wload(d["fwout"][l, :, c8, :, :], 22, 128)
            for ti in range(u.ntile):
                t = ti
                acc = pb[ti % 4][:, 0:128]
                for kk in range(22):
                    k.mm(acc, actT[:, kk, ti * 128:(ti + 1) * 128], w[:, kk, :], start=(kk == 0), stop=(kk == 21))
                k.tt(ht[ti % 2][:], acc, self.grow[1][:, cs], ALU.mult)
                k.tt(self.h[t][:, cs], self.h[t][:, cs], ht[ti % 2][:], ALU.add, pw=True)


def _kmaj(w):
    K, C = w.shape
    return np.ascontiguousarray(w.reshape(K // 128, 128, C).transpose(1, 0, 2))


def _rope_tables(pos_or_none, n):
    if pos_or_none is None:
        return np.ones((n, 32), np.float32), np.zeros((n, 32), np.float32)
    pos = pos_or_none
    row = (pos // 64).astype(np.float32)
    col = (pos % 64).astype(np.float32)
    inv = (np.float32(10000.0) ** (-np.arange(0, 16, 2, dtype=np.float32) / np.float32(16))).astype(np.float32)
    ar = (row[:, None] * inv).astype(np.float32)
    ac = (col[:, None] * inv).astype(np.float32)
    cr, sr, cc, sc = np.cos(ar), np.sin(ar), np.cos(ac), np.sin(ac)
    cosf = np.concatenate([cr, cr, cc, cc], axis=1).astype(np.float32)
    sins = np.concatenate([-sr, sr, -sc, sc], axis=1).astype(np.float32)
    return cosf, sins


def _prep(inp):
    f = np.float32
    L = DEPTH
    g = lambda n: np.asarray(inp[n], dtype=f)
    idx = np.arange(128)
    same = (idx[:, None] // 64) == (idx[None, :] // 64)
    le = same & (idx[:, None] <= idx[None, :]); ge = same & (idx[:, None] >= idx[None, :])
    lt = same & (idx[:, None] < idx[None, :]); gt = same & (idx[:, None] > idx[None, :])
    NEG = -30000.0
    cst = np.zeros((128, 8, 128), f)
    cst[:, 0] = np.eye(128); cst[:, 1] = le; cst[:, 2] = ge
    cst[:, 3] = np.where(lt, 0, NEG); cst[:, 4] = np.where(le, 0, NEG)
    cst[:, 5] = np.where(gt, 0, NEG); cst[:, 6] = np.where(ge, 0, NEG)
    cst[:, 7] = same
    w_in = g("w_in")
    win = np.zeros((L, 128, 8, NWIN), f)
    for l in range(L):
        w = w_in[l]
        gates = w[:, O_GATES:].reshape(1024, 3, 8, 128).transpose(0, 2, 1, 3).reshape(1024, 3072)
        cols = np.concatenate([
            w[:, O_MQ:O_MQ + 256], w[:, O_GQ:O_GQ + 256], w[:, O_GK:O_GK + 256], w[:, O_GR:O_GR + 512],
            w[:, O_GLR:O_GLR + 16], np.zeros((1024, 16), f), w[:, O_GLR + 16:O_GLR + 32], np.zeros((1024, 16), f), w[:, O_DQKV:O_DQKV + 1536], w[:, O_DZ:O_DZ + 512],
            w[:, O_MKV:O_MKV + 160], w[:, O_DA:O_DA + 32], w[:, O_GK:O_GK + 256], w[:, O_GV:O_GV + 512], gates], axis=1)
        assert cols.shape[1] == NWIN
        win[l] = _kmaj(cols)
    common = {"cst": cst, "win": win}
    common["wmod"] = np.stack([_kmaj(g("w_mod")[l]) for l in range(L)])
    common["bmodc"] = np.ascontiguousarray(g("b_mod").reshape(L, 48, 128).transpose(2, 0, 1))
    ncol = np.zeros((128, L, 2, 8), f)
    ncol[:, :, 0, :] = g("norm_mix").reshape(L, 8, 128).transpose(2, 0, 1)
    ncol[:, :, 1, :] = g("norm_ffn").reshape(L, 8, 128).transpose(2, 0, 1)
    common["ncol"] = ncol
    common["fnorm"] = g("final_norm").reshape(1, D)
    perm = np.concatenate([np.arange(8, 16), np.arange(0, 8), np.arange(24, 32), np.arange(16, 24)])
    wuq = np.zeros((L, 128, 2, 1024), f)
    for l in range(L):
        w = g("mla_w_uq")[l].reshape(256, 8, 96)
        rope = w[:, :, 64:]
        cat = np.concatenate([w[:, :, :64], rope, rope[:, :, perm]], axis=2).reshape(256, 1024)
        wuq[l] = _kmaj(cat)
    common["wuq"] = wuq
    common["qnorm"] = np.ascontiguousarray(g("mla_q_norm").reshape(L, 2, 128).transpose(2, 0, 1))
    common["kvnorm"] = g("mla_kv_norm").reshape(L, 1, 128)
    wukv = g("mla_w_ukv").reshape(L, 128, 8, 128)
    common["wukT"] = np.ascontiguousarray(wukv[:, :, :, :64].transpose(0, 3, 2, 1))
    common["wuv"] = np.ascontiguousarray(wukv[:, :, :, 64:].reshape(L, 128, 512))
    wgate = np.zeros((L, 17, 512), f)
    wgate[:, 0:16, :] = g("gla_w_gate").transpose(0, 2, 1, 3).reshape(L, 16, 512)
    wgate[:, 16, :] = g("gla_b_gate").reshape(L, 512)
    common["wgate"] = wgate
    gcol = np.zeros((128, L, 2), f)
    gcol[:, :, 0] = g("gla_norm").T
    gcol[:, :, 1] = np.concatenate([g("gdn_norm"), g("gdn_norm")], axis=1).T
    common["gcol"] = gcol
    common["convc"] = np.ascontiguousarray(g("gdn_conv").reshape(L, 3, 12, 128).transpose(3, 0, 2, 1))
    common["adt"] = np.concatenate([g("gdn_a_log").reshape(L, 1, 16), g("gdn_dt_bias").reshape(L, 1, 16)], axis=2)
    common["wbr"] = np.ascontiguousarray(g("w_branch").reshape(L, 3, 4, 128, 8, 128).transpose(0, 3, 4, 1, 2, 5))
    common["wout"] = np.stack([_kmaj(g("w_out")[l]) for l in range(L)])
    fwin = np.zeros((L, 128, 8, 5632), f)
    for l in range(L):
        w = g("ffn_w_in")[l]
        cat = np.stack([w[:, :2816].reshape(1024, 22, 128), w[:, 2816:].reshape(1024, 22, 128)], axis=2).reshape(1024, 5632)
        fwin[l] = _kmaj(cat)
    common["fwin"] = fwin
    common["fwout"] = np.stack([np.ascontiguousarray(_kmaj(g("ffn_w_out")[l]).reshape(128, 22, 8, 128).transpose(0, 2, 1, 3)) for l in range(L)])
    xp, xs = g("x_prompt"), g("x_sample")
    cps, sps = _rope_tables(None, 512)
    maps = []
    for c in range(8):
        gb, r = c // 4, c % 4
        m = dict(common)
        m["xin"] = np.ascontiguousarray(np.concatenate([xp[2 * c], xp[2 * c + 1], xs[gb, 1024 * r:1024 * (r + 1)]], axis=0))
        m["cache"] = np.ascontiguousarray(g("cache_mla")[gb])
        m["sgla"] = np.ascontiguousarray(g("state_gla")[gb])
        m["sgdn"] = np.ascontiguousarray(g("state_gdn")[gb])
        condT = np.zeros((128, 8, 2), f)
        condT[:, :, 0] = g("c_ctx").reshape(8, 128).T
        condT[:, :, 1] = g("c")[gb].reshape(8, 128).T
        m["condT"] = condT.reshape(128, 16)
        cs, ss = _rope_tables(np.arange(1024 * r, 1024 * (r + 1)), 1024)
        cosf = np.concatenate([cps, cs], axis=0); sinf = np.concatenate([sps, ss], axis=0)
        m["ropeT"] = np.ascontiguousarray(np.stack([cosf.T, sinf.T], axis=1))
        m["ropetok"] = np.ascontiguousarray(np.stack([cosf.reshape(12, 128, 32), sinf.reshape(12, 128, 32)], axis=0).transpose(2, 0, 1, 3))
        rk = np.zeros((128, 16), f)
        for j in range(4):
            rk[:, j] = 1.0 if j < r else 0.0
            rk[:, 4 + j] = 1.0 if j > r else 0.0
            rk[:, 8 + j] = 1.0 if j == r - 1 else 0.0
            rk[:, 12 + j] = 1.0 if j == r + 1 else 0.0
        m["rankm"] = rk
        maps.append(m)
    return maps


_PROG = {}


def _get_prog(stage=9, units="PS"):
    key = (stage, units)
    if key not in _PROG:
        _PROG[key] = Prog(stage=stage, units=units)
    return _PROG[key]


def _run(inputs, stage=9, units="PS"):
    prog = _get_prog(stage, units)
    maps = _prep(inputs)
    maps = [{n: m[n] for n in prog.din} for m in maps]
    res = run_bass_kernel_spmd(prog.k.nc, maps, core_ids=list(range(8)))
    return res.results


DEFAULT_STAGE = int(os.environ.get("KSTAGE", "3"))


def kernel(**inputs):
    outs = _run(inputs, stage=DEFAULT_STAGE)
    f = np.float32
    y_prompt = np.zeros((16, 256, D), f); y_sample = np.zeros((2, 4096, D), f)
    ncache = np.zeros((16, DEPTH, 256, 160), f)
    nsgla = np.zeros((16, DEPTH, 2, 4, 64, 128), f); nsgdn = np.zeros((16, DEPTH, 2, 8, 64, 64), f)
    for c in range(8):
        o = outs[c]
        gb, r = c // 4, c % 4
        y = np.asarray(o["y"])
        y_prompt[2 * c] = y[0:256]; y_prompt[2 * c + 1] = y[256:512]
        y_sample[gb, 1024 * r:1024 * (r + 1)] = y[512:1536]
        ncache[2 * c:2 * c + 2] = np.asarray(o["ncache"])
        nsgla[2 * c:2 * c + 2] = np.asarray(o["nsgla"])
        nsgdn[2 * c:2 * c + 2] = np.asarray(o["nsgdn"])
    return (y_prompt, y_sample, ncache, nsgla, nsgdn)
```
